# Optimizing a Trainium2 kernel written in Bass

```python
import math
import jax, jax.numpy as jnp
from jax import lax
import numpy as np

D_MODEL = 1024
BATCH = 2
SEQ = 8192
DEPTH = 4

N_MIXERS = 3
RET_HEADS = 4
RET_DK = D_MODEL // RET_HEADS
RET_DV = 2 * RET_DK
RET_CHUNK = 128
GN_EPS = 1e-5
LRU_WIDTH = (5 * D_MODEL) // 4
LRU_BLOCKS = 10
LRU_BLOCK_DIM = LRU_WIDTH // LRU_BLOCKS
CONV_WIDTH = 4
LRU_C = 8.0
MLA_HEADS = 8
MLA_NOPE = 128
MLA_ROPE = 64
MLA_V = 128
MLA_Q_RANK = 384
MLA_KV_RANK = 256
ATTN_BLOCK = 128
D_FF = 4 * D_MODEL
ROPE_BASE = 10000.0
LN_EPS = 1e-5
RMS_EPS = 1e-6
ALPHA = (2.0 * DEPTH) ** 0.25
BETA = (8.0 * DEPTH) ** -0.25
N_RET = (DEPTH + 2) // 3
N_LRU = (DEPTH + 1) // 3
N_MLA = DEPTH // 3

kernel_name = 'hybrid_retention_rglru_mla_deepnorm'


def layer_norm(x, g, b):
    xf = x.astype(jnp.float32)
    mu = xf.mean(-1, keepdims=True)
    var = jnp.square(xf - mu).mean(-1, keepdims=True)
    return ((xf - mu) * lax.rsqrt(var + LN_EPS) * g + b).astype(x.dtype)


def rms_norm(x, g):
    xf = x.astype(jnp.float32)
    y = xf * lax.rsqrt(jnp.mean(jnp.square(xf), -1, keepdims=True) + RMS_EPS)
    return (y * g).astype(x.dtype)


def rope(x, positions):
    half = x.shape[-1] // 2
    inv = ROPE_BASE ** (-jnp.arange(half, dtype=jnp.float32) / half)
    ang = positions.astype(jnp.float32)[..., None] * inv
    cos = jnp.cos(ang)[:, :, None, :]
    sin = jnp.sin(ang)[:, :, None, :]
    xf = x.astype(jnp.float32)
    x1, x2 = xf[..., :half], xf[..., half:]
    return jnp.concatenate([x1 * cos - x2 * sin, x1 * sin + x2 * cos], -1).astype(x.dtype)


def retention_mixer(x, positions, w_in, gn_g, w_o):
    B, S, _ = x.shape
    H, dk, dv, C = RET_HEADS, RET_DK, RET_DV, RET_CHUNK
    n = S // C
    q, k, v, g = jnp.split(x @ w_in, [H * dk, 2 * H * dk, 2 * H * dk + H * dv], axis=-1)
    q = rope(q.reshape(B, S, H, dk), positions).astype(jnp.float32)
    k = rope(k.reshape(B, S, H, dk), positions).astype(jnp.float32) * (dk ** -0.5)
    v = v.reshape(B, S, H, dv).astype(jnp.float32)
    log_gamma = jnp.log1p(-jnp.exp2(-5.0 - jnp.arange(H, dtype=jnp.float32)))
    idx = jnp.arange(C, dtype=jnp.float32)
    diff = idx[:, None] - idx[None, :]
    intra_decay = jnp.where(diff >= 0, jnp.exp(log_gamma[:, None, None] * jnp.maximum(diff, 0.0)), 0.0)
    q_decay = jnp.exp(log_gamma[:, None] * (idx + 1.0))[None, :, :, None]
    k_decay = jnp.exp(log_gamma[:, None] * (C - 1.0 - idx))[None, :, :, None]
    chunk_decay = jnp.exp(log_gamma * C)[None, :, None, None]

    def to_chunks(t):
        return t.reshape(B, n, C, H, t.shape[-1]).transpose(1, 0, 3, 2, 4)

    def step(state, inp):
        qi, ki, vi = inp
        s = jnp.einsum('bhid,bhjd->bhij', qi, ki) * intra_decay
        inner = jnp.einsum('bhij,bhjv->bhiv', s, vi)
        cross = jnp.einsum('bhid,bhdv->bhiv', qi * q_decay, state)
        state = state * chunk_decay + jnp.einsum('bhjd,bhjv->bhdv', ki * k_decay, vi)
        return state, inner + cross

    state0 = jnp.zeros((B, H, dk, dv), jnp.float32)
    _, out = lax.scan(step, state0, (to_chunks(q), to_chunks(k), to_chunks(v)))
    out = out.transpose(1, 0, 3, 2, 4).reshape(B, S, H, dv)
    mu = out.mean(-1, keepdims=True)
    var = jnp.square(out - mu).mean(-1, keepdims=True)
    y = ((out - mu) * lax.rsqrt(var + GN_EPS)).reshape(B, S, H * dv) * gn_g
    y = jax.nn.silu(g.astype(jnp.float32)) * y
    return y.astype(x.dtype) @ w_o


def rglru_mixer(x, w_in, conv_w, conv_b, w_a, b_a, w_x, b_x, lam, w_o):
    B, S, _ = x.shape
    W = LRU_WIDTH
    gate_branch, rec = jnp.split(x @ w_in, 2, axis=-1)
    gate_branch = jax.nn.gelu(gate_branch.astype(jnp.float32))
    u = lax.conv_general_dilated(
        rec.astype(jnp.float32), conv_w.astype(jnp.float32)[:, None, :],
        window_strides=(1,), padding=[(CONV_WIDTH - 1, 0)],
        dimension_numbers=('NWC', 'WIO', 'NWC'), feature_group_count=W) + conv_b
    ub = u.reshape(B, S, LRU_BLOCKS, LRU_BLOCK_DIM)
    r = jax.nn.sigmoid(jnp.einsum('bsnd,nde->bsne', ub, w_a.astype(jnp.float32)).reshape(B, S, W) + b_a)
    i = jax.nn.sigmoid(jnp.einsum('bsnd,nde->bsne', ub, w_x.astype(jnp.float32)).reshape(B, S, W) + b_x)
    log_a = -LRU_C * r * jax.nn.softplus(-lam.astype(jnp.float32))
    a = jnp.exp(log_a)
    b_in = jnp.sqrt(-jnp.expm1(2.0 * log_a)) * (i * u)

    def combine(left, right):
        a1, b1 = left
        a2, b2 = right
        return a1 * a2, a2 * b1 + b2

    _, h = lax.associative_scan(combine, (a, b_in), axis=1)
    return (h * gate_branch).astype(x.dtype) @ w_o


def mla_mixer(x, positions, w_in, q_norm_g, kv_norm_g, w_uq, w_ukv, w_o):
    B, S, _ = x.shape
    H, dqk = MLA_HEADS, MLA_NOPE + MLA_ROPE
    c_q, c_kv, k_pe = jnp.split(x @ w_in, [MLA_Q_RANK, MLA_Q_RANK + MLA_KV_RANK], axis=-1)
    q = (rms_norm(c_q, q_norm_g) @ w_uq).reshape(B, S, H, dqk)
    q = jnp.concatenate([q[..., :MLA_NOPE], rope(q[..., MLA_NOPE:], positions)], -1) * (dqk ** -0.5)
    kv = (rms_norm(c_kv, kv_norm_g) @ w_ukv).reshape(B, S, H, MLA_NOPE + MLA_V)
    k_nope, v = kv[..., :MLA_NOPE], kv[..., MLA_NOPE:]
    k_pe = rope(k_pe[:, :, None, :], positions)
    k = jnp.concatenate([k_nope, jnp.broadcast_to(k_pe, (B, S, H, MLA_ROPE))], -1)
    nb = S // ATTN_BLOCK
    qb = q.reshape(B, nb, ATTN_BLOCK, H, dqk).transpose(1, 0, 2, 3, 4)
    key_pos = jnp.arange(S)

    def attend(args):
        qi, blk = args
        s = jnp.einsum('bqhd,bkhd->bhqk', qi, k, preferred_element_type=jnp.float32)
        qpos = blk * ATTN_BLOCK + jnp.arange(ATTN_BLOCK)
        s = jnp.where(key_pos[None, :] <= qpos[:, None], s, -jnp.inf)
        p = jax.nn.softmax(s, axis=-1).astype(v.dtype)
        return jnp.einsum('bhqk,bkhd->bqhd', p, v)

    o = lax.map(attend, (qb, jnp.arange(nb)))
    o = o.transpose(1, 0, 2, 3, 4).reshape(B, S, H * MLA_V)
    return o @ w_o


def sq_relu_mlp(x, w1, w2):
    return jnp.square(jax.nn.relu(x @ w1)) @ w2


def _normal(key, shape, std):
    return jax.random.normal(key, shape, jnp.float32) * std


def setup_inputs(seed: int = 0) -> dict:
    key = jax.random.key(seed)
    ks = list(jax.random.split(key, 32))
    D = D_MODEL
    x = jax.random.normal(ks[0], (BATCH, SEQ, D), jnp.float32)
    positions = jnp.broadcast_to(jnp.arange(SEQ, dtype=jnp.int32), (BATCH, SEQ))
    ret_qk = _normal(ks[1], (N_RET, D, 2 * RET_HEADS * RET_DK), D ** -0.5)
    ret_v = _normal(ks[2], (N_RET, D, RET_HEADS * RET_DV), BETA * D ** -0.5)
    ret_g = _normal(ks[3], (N_RET, D, RET_HEADS * RET_DV), D ** -0.5)
    ret_w_in = jnp.concatenate([ret_qk, ret_v, ret_g], -1)
    ret_gn_g = 1.0 + _normal(ks[4], (N_RET, RET_HEADS * RET_DV), 0.02)
    ret_w_o = _normal(ks[5], (N_RET, RET_HEADS * RET_DV, D), BETA * (RET_HEADS * RET_DV) ** -0.5)
    lru_w_in = _normal(ks[6], (N_LRU, D, 2 * LRU_WIDTH), D ** -0.5)
    lru_conv_w = _normal(ks[7], (N_LRU, CONV_WIDTH, LRU_WIDTH), CONV_WIDTH ** -0.5)
    lru_conv_b = _normal(ks[8], (N_LRU, LRU_WIDTH), 0.01)
    lru_w_a = _normal(ks[9], (N_LRU, LRU_BLOCKS, LRU_BLOCK_DIM, LRU_BLOCK_DIM), LRU_BLOCK_DIM ** -0.5)
    lru_b_a = _normal(ks[10], (N_LRU, LRU_WIDTH), 0.01)
    lru_w_x = _normal(ks[11], (N_LRU, LRU_BLOCKS, LRU_BLOCK_DIM, LRU_BLOCK_DIM), LRU_BLOCK_DIM ** -0.5)
    lru_b_x = _normal(ks[12], (N_LRU, LRU_WIDTH), 0.01)
    a_c = jax.random.uniform(ks[13], (N_LRU, LRU_WIDTH), jnp.float32, minval=0.9, maxval=0.999)
    s = a_c ** (1.0 / LRU_C)
    lru_lam = jnp.log(s) - jnp.log1p(-s)
    lru_w_o = _normal(ks[14], (N_LRU, LRU_WIDTH, D), BETA * LRU_WIDTH ** -0.5)
    mla_w_in = _normal(ks[15], (N_MLA, D, MLA_Q_RANK + MLA_KV_RANK + MLA_ROPE), D ** -0.5)
    mla_q_norm = 1.0 + _normal(ks[16], (N_MLA, MLA_Q_RANK), 0.02)
    mla_kv_norm = 1.0 + _normal(ks[17], (N_MLA, MLA_KV_RANK), 0.02)
    mla_w_uq = _normal(ks[18], (N_MLA, MLA_Q_RANK, MLA_HEADS * (MLA_NOPE + MLA_ROPE)), MLA_Q_RANK ** -0.5)
    uk = _normal(ks[19], (N_MLA, MLA_KV_RANK, MLA_HEADS, MLA_NOPE), MLA_KV_RANK ** -0.5)
    uv = _normal(ks[20], (N_MLA, MLA_KV_RANK, MLA_HEADS, MLA_V), BETA * MLA_KV_RANK ** -0.5)
    mla_w_ukv = jnp.concatenate([uk, uv], -1).reshape(N_MLA, MLA_KV_RANK, MLA_HEADS * (MLA_NOPE + MLA_V))
    mla_w_o = _normal(ks[21], (N_MLA, MLA_HEADS * MLA_V, D), BETA * (MLA_HEADS * MLA_V) ** -0.5)
    ln_g = 1.0 + _normal(ks[22], (DEPTH, 2, D), 0.02)
    ln_b = _normal(ks[23], (DEPTH, 2, D), 0.02)
    mlp_w1 = _normal(ks[24], (DEPTH, D, D_FF), BETA * D ** -0.5)
    mlp_w2 = _normal(ks[25], (DEPTH, D_FF, D), BETA * D_FF ** -0.5)
    return {'x': x, 'positions': positions,
            'ret_w_in': ret_w_in, 'ret_gn_g': ret_gn_g, 'ret_w_o': ret_w_o,
            'lru_w_in': lru_w_in, 'lru_conv_w': lru_conv_w, 'lru_conv_b': lru_conv_b,
            'lru_w_a': lru_w_a, 'lru_b_a': lru_b_a, 'lru_w_x': lru_w_x, 'lru_b_x': lru_b_x,
            'lru_lam': lru_lam, 'lru_w_o': lru_w_o,
            'mla_w_in': mla_w_in, 'mla_q_norm': mla_q_norm, 'mla_kv_norm': mla_kv_norm,
            'mla_w_uq': mla_w_uq, 'mla_w_ukv': mla_w_ukv, 'mla_w_o': mla_w_o,
            'ln_g': ln_g, 'ln_b': ln_b, 'mlp_w1': mlp_w1, 'mlp_w2': mlp_w2}


def reference(x, positions, ret_w_in, ret_gn_g, ret_w_o,
              lru_w_in, lru_conv_w, lru_conv_b, lru_w_a, lru_b_a, lru_w_x, lru_b_x, lru_lam, lru_w_o,
              mla_w_in, mla_q_norm, mla_kv_norm, mla_w_uq, mla_w_ukv, mla_w_o,
              ln_g, ln_b, mlp_w1, mlp_w2):
    h = x
    for i in range(DEPTH):
        kind, j = i % N_MIXERS, i // N_MIXERS
        if kind == 0:
            mix = retention_mixer(h, positions, ret_w_in[j], ret_gn_g[j], ret_w_o[j])
        elif kind == 1:
            mix = rglru_mixer(h, lru_w_in[j], lru_conv_w[j], lru_conv_b[j], lru_w_a[j], lru_b_a[j],
                              lru_w_x[j], lru_b_x[j], lru_lam[j], lru_w_o[j])
        else:
            mix = mla_mixer(h, positions, mla_w_in[j], mla_q_norm[j], mla_kv_norm[j],
                            mla_w_uq[j], mla_w_ukv[j], mla_w_o[j])
        h = layer_norm(ALPHA * h + mix, ln_g[i, 0], ln_b[i, 0])
        h = layer_norm(ALPHA * h + sq_relu_mlp(h, mlp_w1[i], mlp_w2[i]), ln_g[i, 1], ln_b[i, 1])
    return h
```

```python
import numpy as np
from contextlib import ExitStack
import concourse.bass as bass
import concourse.mybir as mybir
from concourse.bass_utils import run_bass_kernel_spmd

F32 = mybir.dt.float32
BF16 = mybir.dt.bfloat16
I32 = mybir.dt.int32
AF = mybir.ActivationFunctionType
ALU = mybir.AluOpType
AX = mybir.AxisListType

D = 1024
S = 8192
B = 2
DEPTH = 4
ALPHA = (2.0 * DEPTH) ** 0.25
LN_EPS = 1e-5
NCORE = 8
TT = 512
SEG = 2048

ENGS = ("pe", "act", "dve", "pool", "sp")
SAME_ENG_SYNC = True


class Ins:
    __slots__ = ("eng", "fn", "deps", "sig", "sem", "val", "dma", "gidx")

    def __init__(self, eng, fn, dma, gidx):
        self.eng = eng
        self.fn = fn
        self.deps = set()
        self.sig = False
        self.sem = None
        self.val = 0
        self.dma = dma
        self.gidx = gidx


class Prog:
    def __init__(self, nc, n_dma_sems=24):
        self.nc = nc
        self.all = []
        self.last_w = {}
        self.readers = {}
        self.stack = ExitStack()
        self.n_dma_sems = n_dma_sems
        self.out_dmas = []

    def sb(self, name, shape, dt):
        return self.stack.enter_context(self.nc.sbuf_tensor(name, list(shape), dt))

    def ps(self, name, shape, dt=F32):
        return self.stack.enter_context(self.nc.psum_tensor(name, list(shape), dt))

    def emit(self, eng, fn, r=(), w=(), dma=False, out=False):
        ins = Ins(eng, fn, dma, len(self.all))
        deps = set()
        for k in r:
            lw = self.last_w.get(k)
            if lw is not None:
                deps.add(lw)
        for k in w:
            lw = self.last_w.get(k)
            if lw is not None:
                deps.add(lw)
            for rd in self.readers.get(k, ()):
                deps.add(rd)
        for k in r:
            self.readers.setdefault(k, []).append(ins)
        for k in w:
            self.last_w[k] = ins
            self.readers[k] = []
        deps.discard(ins)
        for d in deps:
            if d.eng == eng and not d.dma:
                if eng == "pe" or eng == "sp" or not SAME_ENG_SYNC:
                    continue
            ins.deps.add(d)
        self.all.append(ins)
        if out:
            self.out_dmas.append(ins)
        return ins

    def dma(self, out, in_, r=(), w=(), eng="sp", is_out=False, slow=False):
        if slow:
            return self.emit(eng, lambda e: e.dma_start(out=out, in_=in_, allow_slow_non_contiguous=True),
                             r, w, dma=True, out=is_out)
        return self.emit(eng, lambda e: e.dma_start(out=out, in_=in_), r, w, dma=True, out=is_out)

    def mm(self, out, lhsT, rhs, start, stop, r=(), w=()):
        return self.emit("pe", lambda e: e.matmul(out, lhsT, rhs, start=start, stop=stop), r, w)

    def act(self, out, in_, func, r=(), w=(), bias=None, scale=None, eng="act"):
        kw = {}
        if bias is not None:
            kw["bias"] = bias
        if scale is not None:
            kw["scale"] = scale
        return self.emit("act", lambda e: e.activation(out, in_, func, **kw), r, w)

    def tt(self, eng, out, in0, in1, op, r=(), w=()):
        return self.emit(eng, lambda e: e.tensor_tensor(out, in0, in1, op), r, w)

    def ts(self, eng, out, in0, s1, s2, op0, op1=None, r=(), w=()):
        if op1 is None:
            return self.emit(eng, lambda e: e.tensor_scalar(out, in0, s1, None, op0), r, w)
        return self.emit(eng, lambda e: e.tensor_scalar(out, in0, s1, s2, op0, op1), r, w)

    def stt(self, out, in0, scalar, in1, op0, op1, r=(), w=()):
        return self.emit("dve", lambda e: e.scalar_tensor_tensor(out, in0, scalar, in1, op0, op1), r, w)

    def copy(self, eng, out, in_, r=(), w=()):
        if eng == "act":
            return self.emit("act", lambda e: e.activation(out, in_, AF.Copy), r, w)
        return self.emit(eng, lambda e: e.tensor_copy(out, in_), r, w)

    def memset(self, eng, ap, val, w=()):
        return self.emit(eng, lambda e: e.memset(ap, val), (), w)

    def finalize(self):
        nc = self.nc
        dma_prev = [None] * self.n_dma_sems
        di = 0
        for ins in self.all:
            if ins.dma:
                s = di % self.n_dma_sems
                di += 1
                ins.sem = ("dma", s)
                ins.sig = True
                if dma_prev[s] is not None:
                    ins.deps.add(dma_prev[s])
                dma_prev[s] = ins
        for ins in self.all:
            for d in ins.deps:
                d.sig = True
        cnt = {}
        for ins in self.all:
            if not ins.sig:
                continue
            if not ins.dma:
                ins.sem = ("eng", ins.eng)
            c = cnt.get(ins.sem, 0) + (16 if ins.dma else 1)
            cnt[ins.sem] = c
            ins.val = c
        sems = {}
        for key in sorted(cnt, key=str):
            sems[key] = self.stack.enter_context(nc.semaphore("s_%s_%s" % key))
        per = {e: [i for i in self.all if i.eng == e] for e in ENGS}
        final = list(self.out_dmas)

        def run(engobj, lst, extra_final=()):
            have = {}
            for ins in lst:
                need = {}
                for d in ins.deps:
                    if need.get(d.sem, 0) < d.val:
                        need[d.sem] = d.val
                for sk, v in need.items():
                    if have.get(sk, 0) < v:
                        engobj.wait_ge(sems[sk], v)
                        have[sk] = v
                bi = ins.fn(engobj)
                if ins.sig:
                    bi.then_inc(sems[ins.sem], 16 if ins.dma else 1)
            for d in extra_final:
                if have.get(d.sem, 0) < d.val:
                    engobj.wait_ge(sems[d.sem], d.val)
                    have[d.sem] = d.val

        with nc.Block() as block:
            @block.gpsimd
            def _(e):
                for s in sems.values():
                    e.sem_clear(s)

        with nc.Block() as block:
            @block.sync
            def _(e):
                run(e, per["sp"], final)

            if per["pe"]:
                @block.tensor
                def _(e):
                    run(e, per["pe"])
            if per["act"]:
                @block.scalar
                def _(e):
                    run(e, per["act"])
            if per["dve"]:
                @block.vector
                def _(e):
                    run(e, per["dve"])
            if per["pool"]:
                @block.gpsimd
                def _(e):
                    run(e, per["pool"])
        self.stack.close()


class Ctx:
    def __init__(self, P, nbanks=8, ln_banks=(6, 7), bf_bank=False, n_wst=2, n_wbf=2):
        self.P = P
        self.psb = [P.ps("psb%d" % i, [128, 512]) for i in range(nbanks)]
        self.pst = P.ps("pst", [128, 1024], BF16) if bf_bank else None
        self.ln_banks = ln_banks
        self.wst = [P.sb("wst%d" % i, [128, 8, 512], F32) for i in range(n_wst)]
        self.wbf = [P.sb("wbf%d" % i, [128, 8, 512], BF16) for i in range(n_wbf)]
        self.wi = 0
        self.ones = P.sb("ones_f", [128, 128], F32)
        P.memset("pool", self.ones[:, :], 1.0, w=["ones_f"])
        self.sq = [P.sb("sq%d" % i, [128, 512], F32) for i in range(2)]
        self.sqi = 0
        self.st = {n: P.sb("st_" + n, [128, 512], F32) for n in ("mean", "msq", "var", "rstd", "nmr")}
        self.tmp = [P.sb("tmp%d" % i, [128, 512], F32) for i in range(2)]
        self.tmpi = 0

    def load_wblock(self, wdram_pkn, kc0, nkc, c0, ncol=512):
        P = self.P
        i = self.wi % 2
        self.wi += 1
        st, bf = self.wst[i], self.wbf[i]
        ks, kb = "wst%d" % i, "wbf%d" % i
        P.dma(st[:, 0:nkc, 0:ncol], wdram_pkn[:, kc0:kc0 + nkc, c0:c0 + ncol], w=[ks])
        P.copy("pool", bf[:, 0:nkc, 0:ncol], st[:, 0:nkc, 0:ncol], r=[ks], w=[kb])
        return bf, kb

    def load_resident(self, wdram_pkn, dst, dkey, nkc, ncol):
        P = self.P
        i = self.wi % len(self.wst)
        self.wi += 1
        st = self.wst[i]
        ks = "wst%d" % i
        P.dma(st[:, 0:nkc, 0:ncol], wdram_pkn[:, 0:nkc, 0:ncol], w=[ks])
        P.copy("pool", dst[:, 0:nkc, 0:ncol], st[:, 0:nkc, 0:ncol], r=[ks], w=[dkey])


def layer_norm_tile(C, z, zkey, g_ap, b_ap, out_f, okey_f, out_b, okey_b, nch=8, T=TT, eps=LN_EPS, gkey="lnp"):
    P = C.P
    nfeat = float(nch * 128)
    ps1, k1 = C.psb[C.ln_banks[0]], "psb%d" % C.ln_banks[0]
    ps2, k2 = C.psb[C.ln_banks[1]], "psb%d" % C.ln_banks[1]
    for c in range(nch):
        sq = C.sq[C.sqi % 2]
        ksq = "sq%d" % (C.sqi % 2)
        C.sqi += 1
        P.act(sq[:, 0:T], z[:, c, 0:T], AF.Square, r=[zkey + str(c)], w=[ksq])
        P.mm(ps1[:, 0:T], C.ones[:, :], z[:, c, 0:T], c == 0, c == nch - 1, r=["ones_f", zkey + str(c)], w=[k1])
        P.mm(ps2[:, 0:T], C.ones[:, :], sq[:, 0:T], c == 0, c == nch - 1, r=["ones_f", ksq], w=[k2])
    mean, msq, var, rstd, nmr = (C.st[n] for n in ("mean", "msq", "var", "rstd", "nmr"))
    P.act(mean[:, 0:T], ps1[:, 0:T], AF.Copy, r=[k1], w=["st_mean"], scale=1.0 / nfeat)
    P.tt("dve", msq[:, 0:T], mean[:, 0:T], mean[:, 0:T], ALU.mult, r=["st_mean"], w=["st_msq"])
    P.stt(var[:, 0:T], ps2[:, 0:T], 1.0 / nfeat, msq[:, 0:T], ALU.mult, ALU.subtract, r=[k2, "st_msq"], w=["st_var"])
    P.ts("dve", var[:, 0:T], var[:, 0:T], eps, None, ALU.add, r=["st_var"], w=["st_var"])
    P.act(rstd[:, 0:T], var[:, 0:T], AF.Ln, r=["st_var"], w=["st_rstd"])
    P.act(rstd[:, 0:T], rstd[:, 0:T], AF.Exp, r=["st_rstd"], w=["st_rstd"], scale=-0.5)
    P.stt(nmr[:, 0:T], mean[:, 0:T], -1.0, rstd[:, 0:T], ALU.mult, ALU.mult, r=["st_mean", "st_rstd"], w=["st_nmr"])
    for c in range(nch):
        t = C.tmp[C.tmpi % 2]
        kt = "tmp%d" % (C.tmpi % 2)
        C.tmpi += 1
        P.tt("dve", t[:, 0:T], z[:, c, 0:T], rstd[:, 0:T], ALU.mult, r=[zkey + str(c), "st_rstd"], w=[kt])
        P.tt("pool", t[:, 0:T], t[:, 0:T], nmr[:, 0:T], ALU.add, r=[kt, "st_nmr"], w=[kt])
        if b_ap is not None:
            P.act(out_f[:, c, 0:T], t[:, 0:T], AF.Identity, r=[kt, gkey], w=[okey_f + str(c)],
                  bias=b_ap[:, c:c + 1], scale=g_ap[:, c:c + 1])
        else:
            P.act(out_f[:, c, 0:T], t[:, 0:T], AF.Copy, r=[kt, gkey], w=[okey_f + str(c)],
                  scale=g_ap[:, c:c + 1])
        if out_b is not None:
            P.copy("pool", out_b[:, c, 0:T], out_f[:, c, 0:T], r=[okey_f + str(c)], w=[okey_b + str(c)])


def build_post(F, ntok=SEG):
    nc = bass.Bass("TRN2", target_bir_lowering=False)
    FC = F // 128
    yT = nc.dram_tensor("yT", [F, ntok], F32, kind="ExternalInput").ap()
    rT = nc.dram_tensor("rT", [D, ntok], F32, kind="ExternalInput").ap()
    wo = nc.dram_tensor("wo", [F, D], F32, kind="ExternalInput").ap()
    w1 = nc.dram_tensor("w1", [D, 4 * D], F32, kind="ExternalInput").ap()
    w2 = nc.dram_tensor("w2", [4 * D, D], F32, kind="ExternalInput").ap()
    lng = nc.dram_tensor("lng", [2, D], F32, kind="ExternalInput").ap()
    lnb = nc.dram_tensor("lnb", [2, D], F32, kind="ExternalInput").ap()
    oT = nc.dram_tensor("oT", [D, ntok], F32, kind="ExternalOutput").ap()
    P = Prog(nc)
    C = Ctx(P)
    yv = yT.rearrange("(c p) t -> p c t", p=128)
    rv = rT.rearrange("(c p) t -> p c t", p=128)
    ov = oT.rearrange("(c p) t -> p c t", p=128)
    wov = wo.rearrange("(c p) n -> p c n", p=128)
    w1v = w1.rearrange("(c p) n -> p c n", p=128)
    w2v = w2.rearrange("(c p) n -> p c n", p=128)
    lnp_g = P.sb("lnp_g", [128, 2, 8], F32)
    lnp_b = P.sb("lnp_b", [128, 2, 8], F32)
    for j in range(2):
        P.dma(lnp_g[:, j, :], lng[j, :].rearrange("(c p) -> p c", p=128), w=["lnp"], slow=True)
        P.dma(lnp_b[:, j, :], lnb[j, :].rearrange("(c p) -> p c", p=128), w=["lnp"], slow=True)
    yst = P.sb("yst", [128, 8, TT], F32)
    yb = P.sb("yb", [128, FC, TT], BF16)
    z = P.sb("z", [128, 8, TT], F32)
    hb = P.sb("hb", [128, 8, TT], BF16)
    hid = P.sb("hid", [128, 32, TT], BF16)
    rl = [P.sb("rl%d" % i, [128, TT], F32) for i in range(2)]
    for t in range(ntok // TT):
        t0 = t * TT
        for c0 in range(0, FC, 8):
            n = min(8, FC - c0)
            P.dma(yst[:, 0:n, :], yv[:, c0:c0 + n, t0:t0 + TT], w=["yst"])
            P.copy("dve", yb[:, c0:c0 + n, :], yst[:, 0:n, :], r=["yst"], w=["yb"])
        P.dma(z[:, :, :], rv[:, :, t0:t0 + TT], w=["z%d" % i for i in range(8)], eng="pool")
        for half in range(2):
            blocks = [(c0, min(8, FC - c0)) for c0 in range(0, FC, 8)]
            for bi, (c0, n) in enumerate(blocks):
                wb, kb = C.load_wblock(wov, c0, n, half * 512)
                for m in range(4):
                    for kc in range(n):
                        P.mm(C.psb[m][:, :], wb[:, kc, m * 128:(m + 1) * 128], yb[:, c0 + kc, :],
                             bi == 0 and kc == 0, bi == len(blocks) - 1 and kc == n - 1,
                             r=[kb, "yb"], w=["psb%d" % m])
            for m in range(4):
                c = half * 4 + m
                P.stt(z[:, c, :], z[:, c, :], ALPHA, C.psb[m][:, :], ALU.mult, ALU.add,
                      r=["z%d" % c, "psb%d" % m], w=["z%d" % c])
        layer_norm_tile(C, z, "z", lnp_g[:, 0, :], lnp_b[:, 0, :], z, "z", hb, "hb")
        for nb in range(8):
            wb, kb = C.load_wblock(w1v, 0, 8, nb * 512)
            for m in range(4):
                pb = C.psb[m]
                kp = "psb%d" % m
                for kc in range(8):
                    P.mm(pb[:, :], wb[:, kc, m * 128:(m + 1) * 128], hb[:, kc, :], kc == 0, kc == 7,
                         r=[kb, "hb%d" % kc], w=[kp])
                rr = rl[m % 2]
                kr = "rl%d" % (m % 2)
                P.act(rr[:, :], pb[:, :], AF.Relu, r=[kp], w=[kr])
                P.tt("dve", hid[:, nb * 4 + m, :], rr[:, :], rr[:, :], ALU.mult, r=[kr], w=["hid"])
        for half in range(2):
            for kb4 in range(4):
                wb, kb = C.load_wblock(w2v, kb4 * 8, 8, half * 512)
                for m in range(4):
                    for kc in range(8):
                        P.mm(C.psb[m][:, :], wb[:, kc, m * 128:(m + 1) * 128], hid[:, kb4 * 8 + kc, :],
                             kb4 == 0 and kc == 0, kb4 == 3 and kc == 7, r=[kb, "hid"], w=["psb%d" % m])
            for m in range(4):
                c = half * 4 + m
                P.stt(z[:, c, :], z[:, c, :], ALPHA, C.psb[m][:, :], ALU.mult, ALU.add,
                      r=["z%d" % c, "psb%d" % m], w=["z%d" % c])
        layer_norm_tile(C, z, "z", lnp_g[:, 1, :], lnp_b[:, 1, :], z, "z", None, None)
        P.dma(ov[:, :, t0:t0 + TT], z[:, :, :], r=["z%d" % i for i in range(8)], is_out=True)
    P.finalize()
    return nc


def rope_tables(P, pos_dram, t0, T, inv_ap, bufs, sign_ap=None):
    import math
    npart = bufs["np"]
    posi, a, b, ki, cs, sn = (bufs[k] for k in ("posi", "a", "b", "ki", "cs", "sn"))
    sl = slice(0, npart)
    P.dma(posi[sl, 0:T], bass.AP(pos_dram, t0, [[0, npart], [1, T]]), w=["rt_posi"])
    P.copy("dve", a[sl, 0:T], posi[sl, 0:T], r=["rt_posi"], w=["rt_a"])
    P.ts("dve", a[sl, 0:T], a[sl, 0:T], inv_ap, None, ALU.mult, r=["rt_a", "rt_inv"], w=["rt_a"])
    P.ts("dve", b[sl, 0:T], a[sl, 0:T], 1.0 / (2 * math.pi), None, ALU.mult, r=["rt_a"], w=["rt_b"])
    P.copy("dve", ki[sl, 0:T], b[sl, 0:T], r=["rt_b"], w=["rt_ki"])
    P.copy("dve", b[sl, 0:T], ki[sl, 0:T], r=["rt_ki"], w=["rt_b"])
    c1 = float(np.float32(6.28125))
    c2 = float(2 * math.pi - 6.28125)
    P.stt(a[sl, 0:T], b[sl, 0:T], -c1, a[sl, 0:T], ALU.mult, ALU.add, r=["rt_a", "rt_b"], w=["rt_a"])
    P.stt(a[sl, 0:T], b[sl, 0:T], -c2, a[sl, 0:T], ALU.mult, ALU.add, r=["rt_a", "rt_b"], w=["rt_a"])
    P.ts("dve", b[sl, 0:T], a[sl, 0:T], math.pi / 2, None, ALU.add, r=["rt_a"], w=["rt_b"])
    P.ts("dve", cs[sl, 0:T], b[sl, 0:T], math.pi, None, ALU.is_gt, r=["rt_b"], w=["rt_cs"])
    P.stt(b[sl, 0:T], cs[sl, 0:T], -2 * math.pi, b[sl, 0:T], ALU.mult, ALU.add, r=["rt_b", "rt_cs"], w=["rt_b"])
    lim = 3.14159
    P.ts("dve", b[sl, 0:T], b[sl, 0:T], lim, -lim, ALU.min, ALU.max, r=["rt_b"], w=["rt_b"])
    P.ts("dve", a[sl, 0:T], a[sl, 0:T], lim, -lim, ALU.min, ALU.max, r=["rt_a"], w=["rt_a"])
    P.act(cs[sl, 0:T], b[sl, 0:T], AF.Sin, r=["rt_b"], w=["rt_cs"])
    if sign_ap is None:
        P.act(sn[sl, 0:T], a[sl, 0:T], AF.Sin, r=["rt_a"], w=["rt_sn"])
    else:
        P.act(sn[sl, 0:T], a[sl, 0:T], AF.Sin, r=["rt_a", "rt_inv"], w=["rt_sn"], scale=sign_ap)


def make_identity_bf(P, idb, key):
    P.memset("pool", idb[:, :], 0.0, w=[key])
    P.emit("pool", lambda e: e.affine_select(idb[:, :], idb[:, :], [[-1, 128]], ALU.not_equal, 1.0,
                                             base=0, channel_multiplier=1), r=[key], w=[key])


def build_ret(nseq=S):
    nc = bass.Bass("TRN2", target_bir_lowering=False)
    hT = nc.dram_tensor("hT", [D, nseq], F32, kind="ExternalInput").ap()
    wq = nc.dram_tensor("wq", [D, 256], F32, kind="ExternalInput").ap()
    wk = nc.dram_tensor("wk", [D, 256], F32, kind="ExternalInput").ap()
    wv = nc.dram_tensor("wv", [D, 512], F32, kind="ExternalInput").ap()
    wg = nc.dram_tensor("wg", [D, 512], F32, kind="ExternalInput").ap()
    gng_d = nc.dram_tensor("gng", [128, 4], F32, kind="ExternalInput").ap()
    pos = nc.dram_tensor("pos", [1, nseq], I32, kind="ExternalInput")
    inv_d = nc.dram_tensor("inv", [128, 1], F32, kind="ExternalInput").ap()
    maskT_d = nc.dram_tensor("maskT", [128, 128], F32, kind="ExternalInput").ap()
    qdec_d = nc.dram_tensor("qdec", [128, 512], F32, kind="ExternalInput").ap()
    kdec_d = nc.dram_tensor("kdec", [128, 1], F32, kind="ExternalInput").ap()
    cdec_d = nc.dram_tensor("cdec", [128, 1], F32, kind="ExternalInput").ap()
    yT = nc.dram_tensor("yT", [512, nseq], F32, kind="ExternalOutput").ap()
    P = Prog(nc)
    C = Ctx(P, nbanks=7, ln_banks=(4, 5), bf_bank=True, n_wst=1, n_wbf=0)
    hv = hT.rearrange("(c p) t -> p c t", p=128)
    yv = yT.rearrange("(c p) t -> p c t", p=128)
    pk = lambda w: w.rearrange("(c p) n -> p c n", p=128)
    wqb = P.sb("wqb", [128, 8, 256], BF16)
    wkb = P.sb("wkb", [128, 8, 256], BF16)
    wvb = P.sb("wvb", [128, 8, 512], BF16)
    wgb = P.sb("wgb", [128, 8, 512], BF16)
    C.load_resident(pk(wq), wqb, "wqb", 8, 256)
    C.load_resident(pk(wk), wkb, "wkb", 8, 256)
    C.load_resident(pk(wv), wvb, "wvb", 8, 512)
    C.load_resident(pk(wg), wgb, "wgb", 8, 512)
    gng = P.sb("gng_s", [128, 4], F32)
    inv = P.sb("inv_s", [128, 1], F32)
    maskT = P.sb("maskT_s", [128, 128], F32)
    qdec = P.sb("qdec_s", [128, 512], F32)
    kdec = P.sb("kdec_s", [128, 1], F32)
    cdec = P.sb("cdec_s", [128, 1], F32)
    P.dma(gng[:, :], gng_d, w=["gng"])
    P.dma(inv[:, :], inv_d, w=["rt_inv"])
    P.dma(maskT[:, :], maskT_d, w=["maskT"])
    P.dma(qdec[:, :], qdec_d, w=["qdec"])
    P.dma(kdec[:, :], kdec_d, w=["kdec"])
    P.dma(cdec[:, :], cdec_d, w=["cdec"])
    idb = P.sb("idb", [128, 128], BF16)
    make_identity_bf(P, idb, "idb")
    rb = {"np": 128, "posi": P.sb("rt_posi", [128, TT], I32), "a": P.sb("rt_a", [128, TT], F32),
          "b": P.sb("rt_b", [128, TT], F32), "ki": P.sb("rt_ki", [128, TT], I32),
          "cs": P.sb("rt_cs", [128, TT], F32), "sn": P.sb("rt_sn", [128, TT], F32)}
    csk = P.sb("csk", [128, TT], F32)
    snk = P.sb("snk", [128, TT], F32)
    xst = P.sb("xst", [128, 8, TT], F32)
    xb = P.sb("xb", [128, 8, TT], BF16)
    qf = P.sb("qf", [128, 2, TT], F32)
    qb = P.sb("qb", [128, 2, TT], BF16)
    qdb = P.sb("qdb", [128, 2, TT], BF16)
    kb = P.sb("kb", [128, 2, TT], BF16)
    kdT = P.sb("kdT", [128, 4, 256], BF16)
    vb = P.sb("vb", [128, 4, 512], BF16)
    sg = P.sb("sg", [128, 4, TT], F32)
    of = P.sb("of", [128, 4, TT], F32)
    ra = P.sb("ra", [128, TT], F32)
    rbb = P.sb("rbb", [128, TT], F32)
    pT = P.sb("pT", [128, 128], BF16)
    st = P.sb("st", [128, 2, 512], F32)
    stb = P.sb("stb", [128, 2, 512], BF16)
    P.memset("pool", st[:, :, :], 0.0, w=["st0", "st1"])
    P.memset("pool", stb[:, :, :], 0.0, w=["stb0", "stb1"])
    X8 = ["xb%d" % i for i in range(8)]
    for t in range(nseq // TT):
        t0 = t * TT
        P.dma(xst[:, :, :], hv[:, :, t0:t0 + TT], w=["xst"])
        for kc in range(8):
            P.copy("dve" if kc % 2 == 0 else "pool", xb[:, kc, :], xst[:, kc, :], r=["xst"], w=["xb%d" % kc])
        rope_tables(P, pos, t0, TT, inv[:, 0:1], rb)
        cs, sn = rb["cs"], rb["sn"]
        P.ts("pool", csk[:, :], cs[:, :], 1.0 / 16.0, None, ALU.mult, r=["rt_cs"], w=["csk"])
        P.ts("pool", snk[:, :], sn[:, :], 1.0 / 16.0, None, ALU.mult, r=["rt_sn"], w=["snk"])
        for m in range(2):
            for kc in range(8):
                P.mm(C.psb[m][:, :], wqb[:, kc, m * 128:(m + 1) * 128], xb[:, kc, :], kc == 0, kc == 7,
                     r=["wqb", "xb%d" % kc], w=["psb%d" % m])
        P.tt("dve", ra[:, :], C.psb[0][:, :], cs[:, :], ALU.mult, r=["psb0", "rt_cs"], w=["ra"])
        P.tt("dve", rbb[:, :], C.psb[1][:, :], sn[:, :], ALU.mult, r=["psb1", "rt_sn"], w=["rbb"])
        P.tt("pool", qf[:, 0, :], ra[:, :], rbb[:, :], ALU.subtract, r=["ra", "rbb"], w=["qf0"])
        P.tt("dve", ra[:, :], C.psb[0][:, :], sn[:, :], ALU.mult, r=["psb0", "rt_sn"], w=["ra"])
        P.tt("dve", rbb[:, :], C.psb[1][:, :], cs[:, :], ALU.mult, r=["psb1", "rt_cs"], w=["rbb"])
        P.tt("pool", qf[:, 1, :], ra[:, :], rbb[:, :], ALU.add, r=["ra", "rbb"], w=["qf1"])
        for m in range(2):
            P.copy("act", qb[:, m, :], qf[:, m, :], r=["qf%d" % m], w=["qb%d" % m])
            P.tt("pool", qdb[:, m, :], qf[:, m, :], qdec[:, :], ALU.mult, r=["qf%d" % m, "qdec"], w=["qdb%d" % m])
        for m in range(2):
            for kc in range(8):
                P.mm(C.psb[2 + m][:, :], wkb[:, kc, m * 128:(m + 1) * 128], xb[:, kc, :], kc == 0, kc == 7,
                     r=["wkb", "xb%d" % kc], w=["psb%d" % (2 + m)])
        P.tt("dve", ra[:, :], C.psb[2][:, :], csk[:, :], ALU.mult, r=["psb2", "csk"], w=["ra"])
        P.tt("dve", rbb[:, :], C.psb[3][:, :], snk[:, :], ALU.mult, r=["psb3", "snk"], w=["rbb"])
        P.tt("pool", kb[:, 0, :], ra[:, :], rbb[:, :], ALU.subtract, r=["ra", "rbb"], w=["kb0"])
        P.tt("dve", ra[:, :], C.psb[2][:, :], snk[:, :], ALU.mult, r=["psb2", "snk"], w=["ra"])
        P.tt("dve", rbb[:, :], C.psb[3][:, :], csk[:, :], ALU.mult, r=["psb3", "csk"], w=["rbb"])
        P.tt("pool", kb[:, 1, :], ra[:, :], rbb[:, :], ALU.add, r=["ra", "rbb"], w=["kb1"])
        for m in range(4):
            bk = m % 4
            for kc in range(8):
                P.mm(C.psb[bk][:, :], wgb[:, kc, m * 128:(m + 1) * 128], xb[:, kc, :], kc == 0, kc == 7,
                     r=["wgb", "xb%d" % kc], w=["psb%d" % bk])
            P.act(sg[:, m, :], C.psb[bk][:, :], AF.Silu, r=["psb%d" % bk], w=["sg%d" % m])
        for i in range(4):
            bk = i % 4
            for kc in range(8):
                P.mm(C.psb[bk][:, :], xb[:, kc, i * 128:(i + 1) * 128], wvb[:, kc, :], kc == 0, kc == 7,
                     r=["wvb", "xb%d" % kc], w=["psb%d" % bk])
            P.copy("act", vb[:, i, :], C.psb[bk][:, :], r=["psb%d" % bk], w=["vb%d" % i])
        for i in range(4):
            for m in range(2):
                col = (i * 2 + m) * 128
                P.emit("pe", lambda e, col=col, m=m, i=i: e.transpose(C.pst[:, col:col + 128], kb[:, m, i * 128:(i + 1) * 128], idb[:, :]),
                       r=["kb%d" % m, "idb"], w=["pst"])
        P.ts("dve", kdT[:, :, :], C.pst[:, :].rearrange("p (i d) -> p i d", i=4), kdec[:, 0:1], None, ALU.mult,
             r=["pst", "kdec"], w=["kdT%d" % i for i in range(4)])
        for i in range(4):
            cl = slice(i * 128, (i + 1) * 128)
            for m in range(2):
                P.mm(C.psb[4][:, 0:128], kb[:, m, cl], qb[:, m, cl], m == 0, m == 1,
                     r=["kb%d" % m, "qb%d" % m], w=["psb4"])
            P.tt("dve", pT[:, :], C.psb[4][:, 0:128], maskT[:, :], ALU.mult, r=["psb4", "maskT"], w=["pT"])
            for vc in range(4):
                vs = slice(vc * 128, (vc + 1) * 128)
                P.mm(C.psb[5][:, vs], vb[:, i, vs], pT[:, :], True, False, r=["vb%d" % i, "pT"], w=["psb5"])
                for m in range(2):
                    P.mm(C.psb[5][:, vs], stb[:, m, vs], qdb[:, m, cl], False, m == 1,
                         r=["stb%d" % m, "qdb%d" % m], w=["psb5"])
            P.emit("act", lambda e, cl=cl: e.activation(of[:, :, cl], C.psb[5][:, :].rearrange("p (c t) -> p c t", c=4), AF.Copy),
                   r=["psb5"], w=["of%d" % c for c in range(4)])
            for m in range(2):
                ds_ = slice(m * 128, (m + 1) * 128)
                P.mm(C.psb[6][:, :], kdT[:, i, ds_], vb[:, i, :], True, True, r=["kdT%d" % i, "vb%d" % i], w=["psb6"])
                P.stt(st[:, m, :], st[:, m, :], cdec[:, 0:1], C.psb[6][:, :], ALU.mult, ALU.add,
                      r=["st%d" % m, "cdec", "psb6"], w=["st%d" % m])
                P.copy("pool", stb[:, m, :], st[:, m, :], r=["st%d" % m], w=["stb%d" % m])
        layer_norm_tile(C, of, "of", gng, None, of, "of", None, None, nch=4, gkey="gng")
        for c in range(4):
            P.tt("dve" if c % 2 == 0 else "pool", of[:, c, :], of[:, c, :], sg[:, c, :], ALU.mult,
                 r=["of%d" % c, "sg%d" % c], w=["of%d" % c])
        P.dma(yv[:, :, t0:t0 + TT], of[:, :, :], r=["of%d" % c for c in range(4)], is_out=True)
    P.finalize()
    return nc


def ret_consts(h):
    idx = np.arange(128, dtype=np.float64)
    lg = np.log1p(-np.exp2(-5.0 - h))
    diff = idx[None, :] - idx[:, None]
    maskT = np.where(diff >= 0, np.exp(lg * np.maximum(diff, 0.0)), 0.0)
    qd = np.exp(lg * (idx + 1.0))
    qdec = np.broadcast_to(np.tile(qd, 4)[None, :], (128, 512))
    kdec = np.exp(lg * (127.0 - idx))[:, None]
    cdec = np.full((128, 1), np.exp(lg * 128.0))
    inv = (10000.0 ** (-np.arange(128, dtype=np.float32) / np.float32(128))).astype(np.float32)[:, None]
    f = lambda a: np.ascontiguousarray(a, dtype=np.float32)
    return dict(maskT=f(maskT), qdec=f(qdec), kdec=f(kdec), cdec=f(cdec), inv=f(inv))


NBLK = 3


def build_lru(nseq=S):
    nc = bass.Bass("TRN2", target_bir_lowering=False)
    W3 = NBLK * 128
    hT = nc.dram_tensor("hT", [D, nseq], F32, kind="ExternalInput").ap()
    wgt = nc.dram_tensor("wgt", [D, W3], F32, kind="ExternalInput").ap()
    wrc = nc.dram_tensor("wrc", [D, W3], F32, kind="ExternalInput").ap()
    wa_d = nc.dram_tensor("wa", [128, NBLK, 128], F32, kind="ExternalInput").ap()
    wx_d = nc.dram_tensor("wx", [128, NBLK, 128], F32, kind="ExternalInput").ap()
    vec_d = nc.dram_tensor("vec", [128, NBLK, 8], F32, kind="ExternalInput").ap()
    yT = nc.dram_tensor("yT", [W3, nseq], F32, kind="ExternalOutput").ap()
    P = Prog(nc)
    C = Ctx(P, nbanks=8, n_wst=1, n_wbf=0)
    hv = hT.rearrange("(c p) t -> p c t", p=128)
    yv = yT.rearrange("(c p) t -> p c t", p=128)
    pk = lambda w: w.rearrange("(c p) n -> p c n", p=128)
    wgb = P.sb("wgb", [128, 8, W3], BF16)
    wrb = P.sb("wrb", [128, 8, W3], BF16)
    C.load_resident(pk(wgt), wgb, "wgb", 8, W3)
    C.load_resident(pk(wrc), wrb, "wrb", 8, W3)
    waf = P.sb("waf", [128, NBLK, 128], F32)
    wxf = P.sb("wxf", [128, NBLK, 128], F32)
    wab = P.sb("wab", [128, NBLK, 128], BF16)
    wxb = P.sb("wxb", [128, NBLK, 128], BF16)
    P.dma(waf[:, :, :], wa_d, w=["waf"])
    P.dma(wxf[:, :, :], wx_d, w=["wxf"])
    P.copy("dve", wab[:, :, :], waf[:, :, :], r=["waf"], w=["wab"])
    P.copy("dve", wxb[:, :, :], wxf[:, :, :], r=["wxf"], w=["wxb"])
    vec = P.sb("vec_s", [128, NBLK, 8], F32)
    P.dma(vec[:, :, :], vec_d, w=["vec"])
    der = P.sb("der", [128, NBLK, 4], F32)
    P.act(der[:, :, 0], vec[:, :, 7], AF.Exp, r=["vec"], w=["der"], scale=-1.0)
    P.ts("dve", der[:, :, 0], der[:, :, 0], 1.0, None, ALU.add, r=["der"], w=["der"])
    P.act(der[:, :, 0], der[:, :, 0], AF.Ln, r=["der"], w=["der"])
    P.ts("dve", der[:, :, 1], der[:, :, 0], -8.0, None, ALU.mult, r=["der"], w=["der"])
    P.ts("dve", der[:, :, 2], der[:, :, 0], -16.0, None, ALU.mult, r=["der"], w=["der"])
    P.ts("dve", der[:, :, 3], der[:, :, 0], 8.0, None, ALU.mult, r=["der"], w=["der"])
    xst = P.sb("xst", [128, 8, TT], F32)
    xb = P.sb("xb", [128, 8, TT], BF16)
    rec = P.sb("rec", [128, NBLK, TT + 4], F32)
    P.memset("pool", rec[:, :, :], 0.0, w=["rec%d" % b for b in range(NBLK)])
    hprev = P.sb("hprev", [128, NBLK], F32)
    P.memset("pool", hprev[:, :], 0.0, w=["hprev%d" % b for b in range(NBLK)])
    gt = P.sb("gt", [128, TT], F32)
    g2 = P.sb("g2", [128, TT], F32)
    u = P.sb("u", [128, TT], F32)
    ub = P.sb("ub", [128, TT], BF16)
    rr = P.sb("rr", [128, TT], F32)
    ii = P.sb("ii", [128, TT], F32)
    aa = P.sb("aa", [128, TT], F32)
    ee = P.sb("ee", [128, TT], F32)
    th = P.sb("th", [128, TT], F32)
    hs = P.sb("hs", [128, TT], F32)
    yo = P.sb("yo", [128, NBLK, TT], F32)
    for t in range(nseq // TT):
        t0 = t * TT
        P.dma(xst[:, :, :], hv[:, :, t0:t0 + TT], w=["xst"])
        for kc in range(8):
            P.copy("dve" if kc % 2 == 0 else "pool", xb[:, kc, :], xst[:, kc, :], r=["xst"], w=["xb%d" % kc])
        for b in range(NBLK):
            bs = slice(b * 128, (b + 1) * 128)
            kr = "rec%d" % b
            for kc in range(8):
                P.mm(C.psb[0][:, :], wgb[:, kc, bs], xb[:, kc, :], kc == 0, kc == 7, r=["wgb", "xb%d" % kc], w=["psb0"])
            P.act(g2[:, :], C.psb[0][:, :], AF.Square, r=["psb0"], w=["g2"])
            P.ts("dve", g2[:, :], g2[:, :], 0.044715, 1.0, ALU.mult, ALU.add, r=["g2"], w=["g2"])
            P.tt("dve", g2[:, :], g2[:, :], C.psb[0][:, :], ALU.mult, r=["g2", "psb0"], w=["g2"])
            P.act(g2[:, :], g2[:, :], AF.Sigmoid, r=["g2"], w=["g2"], scale=1.5957691216057308)
            P.tt("dve", gt[:, :], g2[:, :], C.psb[0][:, :], ALU.mult, r=["g2", "psb0"], w=["gt"])
            for kc in range(8):
                P.mm(C.psb[1][:, :], wrb[:, kc, bs], xb[:, kc, :], kc == 0, kc == 7, r=["wrb", "xb%d" % kc], w=["psb1"])
            P.copy("act", rec[:, b, 3:3 + TT], C.psb[1][:, :], r=["psb1"], w=[kr])
            P.ts("dve", u[:, :], rec[:, b, 3:3 + TT], vec[:, b, 3:4], vec[:, b, 4:5], ALU.mult, ALU.add, r=[kr, "vec"], w=["u"])
            for k in range(3):
                P.stt(u[:, :], rec[:, b, k:k + TT], vec[:, b, k:k + 1], u[:, :], ALU.mult, ALU.add, r=[kr, "vec", "u"], w=["u"])
            P.copy("pool", rec[:, b, 0:3], rec[:, b, TT:TT + 3], r=[kr], w=[kr])
            P.copy("pool", ub[:, :], u[:, :], r=["u"], w=["ub"])
            P.mm(C.psb[2][:, :], wab[:, b, :], ub[:, :], True, True, r=["wab", "ub"], w=["psb2"])
            P.mm(C.psb[3][:, :], wxb[:, b, :], ub[:, :], True, True, r=["wxb", "ub"], w=["psb3"])
            P.act(rr[:, :], C.psb[2][:, :], AF.Sigmoid, r=["psb2", "vec"], w=["rr"], bias=vec[:, b, 5:6])
            P.act(ii[:, :], C.psb[3][:, :], AF.Sigmoid, r=["psb3", "vec"], w=["ii"], bias=vec[:, b, 6:7])
            P.act(aa[:, :], rr[:, :], AF.Exp, r=["rr", "der"], w=["aa"], scale=der[:, b, 1:2])
            P.act(ee[:, :], rr[:, :], AF.Exp, r=["rr", "der"], w=["ee"], scale=der[:, b, 2:3])
            P.act(th[:, :], rr[:, :], AF.Tanh, r=["rr", "der"], w=["th"], scale=der[:, b, 3:4])
            P.stt(ee[:, :], ee[:, :], 1.0, th[:, :], ALU.add, ALU.mult, r=["ee", "th"], w=["ee"])
            P.act(ee[:, :], ee[:, :], AF.Sqrt, r=["ee"], w=["ee"])
            P.tt("pool", ii[:, :], ii[:, :], u[:, :], ALU.mult, r=["ii", "u"], w=["ii"])
            P.tt("pool", ii[:, :], ii[:, :], ee[:, :], ALU.mult, r=["ii", "ee"], w=["ii"])
            P.emit("dve", lambda e, b=b: e.tensor_tensor_scan(hs[:, :], aa[:, :], ii[:, :], hprev[:, b:b + 1], ALU.mult, ALU.add),
                   r=["aa", "ii", "hprev%d" % b], w=["hs"])
            P.copy("dve", hprev[:, b:b + 1], hs[:, TT - 1:TT], r=["hs"], w=["hprev%d" % b])
            P.tt("pool", yo[:, b, :], hs[:, :], gt[:, :], ALU.mult, r=["hs", "gt"], w=["yo%d" % b])
        P.dma(yv[:, :, t0:t0 + TT], yo[:, :, :], r=["yo%d" % b for b in range(NBLK)], is_out=True)
    P.finalize()
    return nc


def rms_norm_tile(C, z, zkey, g_ap, gkey, out_b, okey, nch, eps, bank, T=TT):
    P = C.P
    ps, kp = C.psb[bank], "psb%d" % bank
    for c in range(nch):
        sq = C.sq[C.sqi % 2]
        ksq = "sq%d" % (C.sqi % 2)
        C.sqi += 1
        P.act(sq[:, 0:T], z[:, c, 0:T], AF.Square, r=[zkey + str(c)], w=[ksq])
        P.mm(ps[:, 0:T], C.ones[:, :], sq[:, 0:T], c == 0, c == nch - 1, r=["ones_f", ksq], w=[kp])
    rstd = C.st["rstd"]
    P.ts("dve", rstd[:, 0:T], ps[:, 0:T], 1.0 / (nch * 128.0), eps, ALU.mult, ALU.add, r=[kp], w=["st_rstd"])
    P.act(rstd[:, 0:T], rstd[:, 0:T], AF.Ln, r=["st_rstd"], w=["st_rstd"])
    P.act(rstd[:, 0:T], rstd[:, 0:T], AF.Exp, r=["st_rstd"], w=["st_rstd"], scale=-0.5)
    for c in range(nch):
        t = C.tmp[C.tmpi % 2]
        kt = "tmp%d" % (C.tmpi % 2)
        C.tmpi += 1
        P.tt("dve", t[:, 0:T], z[:, c, 0:T], rstd[:, 0:T], ALU.mult, r=[zkey + str(c), "st_rstd"], w=[kt])
        P.ts("pool", out_b[:, c, 0:T], t[:, 0:T], g_ap[:, c:c + 1], None, ALU.mult, r=[kt, gkey], w=[okey + str(c)])


QSCALE = 192.0 ** -0.5
RMS_EPS = 1e-6


def build_mla(nseq=S):
    nc = bass.Bass("TRN2", target_bir_lowering=False)
    NKB = nseq // 128
    hT = nc.dram_tensor("hT", [D, nseq], F32, kind="ExternalInput").ap()
    wcq = nc.dram_tensor("wcq", [D, 384], F32, kind="ExternalInput").ap()
    wckv = nc.dram_tensor("wckv", [D, 256], F32, kind="ExternalInput").ap()
    wkpe = nc.dram_tensor("wkpe", [D, 128], F32, kind="ExternalInput").ap()
    wuqn = nc.dram_tensor("wuqn", [384, 256], F32, kind="ExternalInput").ap()
    wuqr = nc.dram_tensor("wuqr", [384, 256], F32, kind="ExternalInput").ap()
    wuk = nc.dram_tensor("wuk", [256, 256], F32, kind="ExternalInput").ap()
    wuv = nc.dram_tensor("wuv", [256, 256], F32, kind="ExternalInput").ap()
    qng_d = nc.dram_tensor("qng", [128, 3], F32, kind="ExternalInput").ap()
    kvng_d = nc.dram_tensor("kvng", [128, 2], F32, kind="ExternalInput").ap()
    pos = nc.dram_tensor("pos", [1, nseq], I32, kind="ExternalInput")
    inv_d = nc.dram_tensor("inv", [64, 2], F32, kind="ExternalInput").ap()
    mask_d = nc.dram_tensor("mask", [128, 4 * 512], F32, kind="ExternalInput").ap()
    oT = nc.dram_tensor("oT", [256, nseq], F32, kind="ExternalOutput").ap()
    P = Prog(nc)
    C = Ctx(P, nbanks=8, n_wst=1, n_wbf=0)
    hv = hT.rearrange("(c p) t -> p c t", p=128)
    ov = oT.rearrange("(c p) t -> p c t", p=128)
    pk = lambda w: w.rearrange("(c p) n -> p c n", p=128)
    wcqb = P.sb("wcqb", [128, 8, 384], BF16)
    wckvb = P.sb("wckvb", [128, 8, 256], BF16)
    wkpeb = P.sb("wkpeb", [128, 8, 128], BF16)
    wuqnb = P.sb("wuqnb", [128, 3, 256], BF16)
    wuqrb = P.sb("wuqrb", [128, 3, 256], BF16)
    wukb = P.sb("wukb", [128, 2, 256], BF16)
    wuvb = P.sb("wuvb", [128, 2, 256], BF16)
    C.load_resident(pk(wcq), wcqb, "wcqb", 8, 384)
    C.load_resident(pk(wckv), wckvb, "wckvb", 8, 256)
    C.load_resident(pk(wkpe), wkpeb, "wkpeb", 8, 128)
    C.load_resident(pk(wuqn), wuqnb, "wuqnb", 3, 256)
    C.load_resident(pk(wuqr), wuqrb, "wuqrb", 3, 256)
    C.load_resident(pk(wuk), wukb, "wukb", 2, 256)
    C.load_resident(pk(wuv), wuvb, "wuvb", 2, 256)
    qng = P.sb("qng_s", [128, 3], F32)
    kvng = P.sb("kvng_s", [128, 2], F32)
    inv = P.sb("inv_s", [64, 2], F32)
    P.dma(qng[:, :], qng_d, w=["qng"])
    P.dma(kvng[:, :], kvng_d, w=["kvng"])
    P.dma(inv[:, :], inv_d, w=["rt_inv"])
    maskb = P.sb("maskb", [128, 4, 512], BF16)
    C.load_resident(mask_d.rearrange("p (c n) -> p c n", c=4), maskb, "maskb", 4, 512)
    onesb = P.sb("onesb", [128, 128], BF16)
    P.memset("pool", onesb[:, :], 1.0, w=["onesb"])
    rb = {"np": 64, "posi": P.sb("rt_posi", [64, TT], I32), "a": P.sb("rt_a", [64, TT], F32),
          "b": P.sb("rt_b", [64, TT], F32), "ki": P.sb("rt_ki", [64, TT], I32),
          "cs": P.sb("rt_cs", [64, TT], F32), "sn": P.sb("rt_sn", [64, TT], F32)}
    knT = P.sb("knT", [128, 2, nseq], BF16)
    kpe = P.sb("kpe", [64, nseq], BF16)
    vb = P.sb("vb", [128, NKB, 256], BF16)
    xst = P.sb("xst", [128, 8, TT], F32)
    xb = P.sb("xb", [128, 8, TT], BF16)
    cqf = P.sb("cqf", [128, 3, TT], F32)
    ckvf = P.sb("ckvf", [128, 2, TT], F32)
    cqn = P.sb("cqn", [128, 3, TT], BF16)
    ckvn = P.sb("ckvn", [128, 2, TT], BF16)
    ra = P.sb("ra", [64, TT], F32)
    rbb = P.sb("rbb", [64, TT], F32)
    qnb = P.sb("qnb", [128, 2, TT], BF16)
    qpb = P.sb("qpb", [64, 2, TT], BF16)
    pTs = [P.sb("pT%d" % i, [128, TT], BF16) for i in range(2)]
    rl = P.sb("rl", [128, TT], F32)
    ob = P.sb("ob", [128, 2, TT], F32)
    pti = 0
    for t in range(nseq // TT):
        t0 = t * TT
        ts_ = slice(t0, t0 + TT)
        P.dma(xst[:, :, :], hv[:, :, ts_], w=["xst"])
        for kc in range(8):
            P.copy("dve" if kc % 2 == 0 else "pool", xb[:, kc, :], xst[:, kc, :], r=["xst"], w=["xb%d" % kc])
        rope_tables(P, pos, t0, TT, inv[:, 0:1], rb, sign_ap=inv[:, 1:2])
        cs, sn = rb["cs"], rb["sn"]
        for m in range(3):
            bk = m % 2
            for kc in range(8):
                P.mm(C.psb[bk][:, :], wcqb[:, kc, m * 128:(m + 1) * 128], xb[:, kc, :], kc == 0, kc == 7,
                     r=["wcqb", "xb%d" % kc], w=["psb%d" % bk])
            P.copy("act", cqf[:, m, :], C.psb[bk][:, :], r=["psb%d" % bk], w=["cqf%d" % m])
        for m in range(2):
            bk = (m + 1) % 2
            for kc in range(8):
                P.mm(C.psb[bk][:, :], wckvb[:, kc, m * 128:(m + 1) * 128], xb[:, kc, :], kc == 0, kc == 7,
                     r=["wckvb", "xb%d" % kc], w=["psb%d" % bk])
            P.copy("act", ckvf[:, m, :], C.psb[bk][:, :], r=["psb%d" % bk], w=["ckvf%d" % m])
        for j in range(2):
            for kc in range(8):
                P.mm(C.psb[j][0:64, :], wkpeb[:, kc, j * 64:(j + 1) * 64], xb[:, kc, :], kc == 0, kc == 7,
                     r=["wkpeb", "xb%d" % kc], w=["psb%d" % j])
        P.tt("dve", ra[:, :], C.psb[0][0:64, :], cs[:, :], ALU.mult, r=["psb0", "rt_cs"], w=["ra"])
        P.tt("dve", rbb[:, :], C.psb[1][0:64, :], sn[:, :], ALU.mult, r=["psb1", "rt_sn"], w=["rbb"])
        P.tt("pool", kpe[:, ts_], ra[:, :], rbb[:, :], ALU.add, r=["ra", "rbb"], w=["kpe"])
        rms_norm_tile(C, cqf, "cqf", qng, "qng", cqn, "cqn", 3, RMS_EPS, 0)
        rms_norm_tile(C, ckvf, "ckvf", kvng, "kvng", ckvn, "ckvn", 2, RMS_EPS, 1)
        for h in range(2):
            for kc in range(2):
                P.mm(C.psb[h][:, :], wukb[:, kc, h * 128:(h + 1) * 128], ckvn[:, kc, :], kc == 0, kc == 1,
                     r=["wukb", "ckvn%d" % kc], w=["psb%d" % h])
            P.copy("act", knT[:, h, ts_], C.psb[h][:, :], r=["psb%d" % h], w=["knT"])
        for i in range(4):
            bk = i % 2
            for kc in range(2):
                P.mm(C.psb[bk][:, 0:256], ckvn[:, kc, i * 128:(i + 1) * 128], wuvb[:, kc, :], kc == 0, kc == 1,
                     r=["wuvb", "ckvn%d" % kc], w=["psb%d" % bk])
            P.copy("act", vb[:, t * 4 + i, :], C.psb[bk][:, 0:256], r=["psb%d" % bk], w=["vb"])
        for h in range(2):
            for kc in range(3):
                P.mm(C.psb[0][:, :], wuqnb[:, kc, h * 128:(h + 1) * 128], cqn[:, kc, :], kc == 0, kc == 2,
                     r=["wuqnb", "cqn%d" % kc], w=["psb0"])
            P.act(qnb[:, h, :], C.psb[0][:, :], AF.Copy, r=["psb0"], w=["qnb%d" % h], scale=QSCALE)
            for j in range(2):
                for kc in range(3):
                    c0 = h * 128 + j * 64
                    P.mm(C.psb[1][0:64, :] if j == 0 else C.psb[0][0:64, :],
                         wuqrb[:, kc, c0:c0 + 64], cqn[:, kc, :], kc == 0, kc == 2,
                         r=["wuqrb", "cqn%d" % kc], w=["psb1" if j == 0 else "psb0"])
            P.tt("dve", ra[:, :], C.psb[1][0:64, :], cs[:, :], ALU.mult, r=["psb1", "rt_cs"], w=["ra"])
            P.tt("dve", rbb[:, :], C.psb[0][0:64, :], sn[:, :], ALU.mult, r=["psb0", "rt_sn"], w=["rbb"])
            P.stt(qpb[:, h, :], ra[:, :], 1.0, rbb[:, :], ALU.mult, ALU.add, r=["ra", "rbb"], w=["qpb%d" % h])
            P.ts("pool", qpb[:, h, :], qpb[:, h, :], QSCALE, None, ALU.mult, r=["qpb%d" % h], w=["qpb%d" % h])
        nkb = 4 * t + 4
        for h in range(2):
            po, kpo = C.psb[4 + 2 * h], "psb%d" % (4 + 2 * h)
            pl, kpl = C.psb[5 + 2 * h], "psb%d" % (5 + 2 * h)
            for kb in range(nkb):
                ks_ = slice(kb * 128, (kb + 1) * 128)
                bs_ = 2 + (pti % 2)
                pT, kpt = pTs[pti % 2], "pT%d" % (pti % 2)
                pti += 1
                P.mm(C.psb[bs_][:, :], knT[:, h, ks_], qnb[:, h, :], True, False, r=["knT", "qnb%d" % h], w=["psb%d" % bs_])
                P.mm(C.psb[bs_][:, :], kpe[:, ks_], qpb[:, h, :], False, True, r=["kpe", "qpb%d" % h], w=["psb%d" % bs_])
                P.act(pT[:, :], C.psb[bs_][:, :], AF.Exp, r=["psb%d" % bs_], w=[kpt])
                if kb >= 4 * t:
                    P.tt("pool", pT[:, :], pT[:, :], maskb[:, kb - 4 * t, :], ALU.mult, r=[kpt, "maskb"], w=[kpt])
                P.mm(po[:, :], vb[:, kb, h * 128:(h + 1) * 128], pT[:, :], kb == 0, kb == nkb - 1, r=["vb", kpt], w=[kpo])
                P.mm(pl[:, :], onesb[:, :], pT[:, :], kb == 0, kb == nkb - 1, r=["onesb", kpt], w=[kpl])
            P.emit("dve", lambda e, pl=pl: e.reciprocal(rl[:, :], pl[:, :]), r=[kpl], w=["rl"])
            P.tt("dve", ob[:, h, :], po[:, :], rl[:, :], ALU.mult, r=[kpo, "rl"], w=["ob%d" % h])
        P.dma(ov[:, :, ts_], ob[:, :, :], r=["ob0", "ob1"], is_out=True)
    P.finalize()
    return nc


def mla_consts():
    inv32 = (10000.0 ** (-np.arange(32, dtype=np.float32) / np.float32(32))).astype(np.float32)
    inv = np.zeros((64, 2), np.float32)
    inv[:, 0] = np.concatenate([inv32, inv32])
    inv[:, 1] = np.concatenate([-np.ones(32), np.ones(32)])
    j = np.arange(128)[:, None, None]
    d = np.arange(4)[None, :, None]
    i = np.arange(512)[None, None, :]
    mask = (d * 128 + j <= i).astype(np.float32).reshape(128, 2048)
    return dict(inv=inv, mask=np.ascontiguousarray(mask))


_PROGS = {}
LRU_BLOCKS_OF = [[0, 1, 2], [3, 4, 5], [6, 7, 0], [8, 9, 0]]
LRU_VALID = [3, 3, 2, 2]


def _prog(name, fn, *a):
    key = (name,) + a
    if key not in _PROGS:
        _PROGS[key] = fn(*a)
    return _PROGS[key]


def _c(a):
    return np.ascontiguousarray(a, dtype=np.float32)


def _run(nc, ins):
    res = run_bass_kernel_spmd(nc, ins, core_ids=list(range(NCORE)))
    return res.results


def _mixer_ret(h, positions, w_in, gn_g):
    nc = _prog("ret", build_ret, S)
    ins = []
    hT = [_c(h[b].T) for b in range(B)]
    for c in range(NCORE):
        b, hd = c // 4, c % 4
        d = dict(hT=hT[b], wq=_c(w_in[:, hd * 256:(hd + 1) * 256]), wk=_c(w_in[:, 1024 + hd * 256:1024 + (hd + 1) * 256]),
                 wv=_c(w_in[:, 2048 + hd * 512:2048 + (hd + 1) * 512]), wg=_c(w_in[:, 4096 + hd * 512:4096 + (hd + 1) * 512]),
                 gng=_c(gn_g[hd * 512:(hd + 1) * 512].reshape(4, 128).T),
                 pos=np.ascontiguousarray(positions[b:b + 1], dtype=np.int32))
        d.update(ret_consts(hd))
        ins.append(d)
    res = _run(nc, ins)
    y = np.empty((B, S, 2048), np.float32)
    for c in range(NCORE):
        b, hd = c // 4, c % 4
        y[b][:, hd * 512:(hd + 1) * 512] = res[c]["yT"].T
    return y


def _mixer_lru(h, w_in, conv_w, conv_b, w_a, b_a, w_x, b_x, lam):
    nc = _prog("lru", build_lru, S)
    ins = []
    hT = [_c(h[b].T) for b in range(B)]
    for c in range(NCORE):
        b, g = c // 4, c % 4
        bl = LRU_BLOCKS_OF[g]
        cols = np.concatenate([np.arange(k * 128, (k + 1) * 128) for k in bl])
        vec = np.stack([conv_w[0][cols], conv_w[1][cols], conv_w[2][cols], conv_w[3][cols], conv_b[cols],
                        b_a[cols], b_x[cols], lam[cols]], -1)
        ins.append(dict(hT=hT[b], wgt=_c(w_in[:, cols]), wrc=_c(w_in[:, 1280 + cols]),
                        wa=_c(w_a[bl].transpose(1, 0, 2)), wx=_c(w_x[bl].transpose(1, 0, 2)),
                        vec=_c(vec.reshape(3, 128, 8).transpose(1, 0, 2))))
    res = _run(nc, ins)
    y = np.empty((B, S, 1280), np.float32)
    for c in range(NCORE):
        b, g = c // 4, c % 4
        for k in range(LRU_VALID[g]):
            blk = LRU_BLOCKS_OF[g][k]
            y[b][:, blk * 128:(blk + 1) * 128] = res[c]["yT"][k * 128:(k + 1) * 128].T
    return y


def _mixer_mla(h, positions, w_in, qn, kvn, w_uq, w_ukv):
    nc = _prog("mla", build_mla, S)
    ins = []
    hT = [_c(h[b].T) for b in range(B)]
    sw = np.concatenate([np.arange(672, 704), np.arange(640, 672)])
    cst = mla_consts()
    for c in range(NCORE):
        b, hp = c // 4, c % 4
        heads = [2 * hp, 2 * hp + 1]
        d = dict(hT=hT[b], wcq=_c(w_in[:, 0:384]), wckv=_c(w_in[:, 384:640]),
                 wkpe=_c(np.concatenate([w_in[:, 640:704], w_in[:, sw]], 1)),
                 wuqn=_c(np.concatenate([w_uq[:, k * 192:k * 192 + 128] for k in heads], 1)),
                 wuqr=_c(np.concatenate([np.concatenate([w_uq[:, k * 192 + 128:k * 192 + 192], w_uq[:, k * 192 + 160:k * 192 + 192],
                                                         w_uq[:, k * 192 + 128:k * 192 + 160]], 1) for k in heads], 1)),
                 wuk=_c(np.concatenate([w_ukv[:, k * 256:k * 256 + 128] for k in heads], 1)),
                 wuv=_c(np.concatenate([w_ukv[:, k * 256 + 128:k * 256 + 256] for k in heads], 1)),
                 qng=_c(qn.reshape(3, 128).T), kvng=_c(kvn.reshape(2, 128).T),
                 pos=np.ascontiguousarray(positions[b:b + 1], dtype=np.int32))
        d.update(cst)
        ins.append(d)
    res = _run(nc, ins)
    y = np.empty((B, S, 1024), np.float32)
    for c in range(NCORE):
        b, hp = c // 4, c % 4
        y[b][:, hp * 256:(hp + 1) * 256] = res[c]["oT"].T
    return y


def _post(h, y, wo, lng, lnb, w1, w2):
    F = y.shape[-1]
    nc = _prog("post", build_post, F, SEG)
    hf = h.reshape(B * S, D)
    yf = y.reshape(B * S, F)
    ins = []
    wo, w1, w2, lng, lnb = _c(wo), _c(w1), _c(w2), _c(lng), _c(lnb)
    for c in range(NCORE):
        sl = slice(c * SEG, (c + 1) * SEG)
        ins.append(dict(yT=_c(yf[sl].T), rT=_c(hf[sl].T), wo=wo, w1=w1, w2=w2, lng=lng, lnb=lnb))
    res = _run(nc, ins)
    out = np.empty((B * S, D), np.float32)
    for c in range(NCORE):
        out[c * SEG:(c + 1) * SEG] = res[c]["oT"].T
    return out.reshape(B, S, D)


def kernel(x, positions, ret_w_in, ret_gn_g, ret_w_o,
           lru_w_in, lru_conv_w, lru_conv_b, lru_w_a, lru_b_a, lru_w_x, lru_b_x, lru_lam, lru_w_o,
           mla_w_in, mla_q_norm, mla_kv_norm, mla_w_uq, mla_w_ukv, mla_w_o,
           ln_g, ln_b, mlp_w1, mlp_w2):
    A = lambda a: np.asarray(a)
    h = np.asarray(x, dtype=np.float32)
    positions = A(positions)
    for i in range(DEPTH):
        kind, j = i % 3, i // 3
        if kind == 0:
            y = _mixer_ret(h, positions, A(ret_w_in)[j], A(ret_gn_g)[j])
            wo = A(ret_w_o)[j]
        elif kind == 1:
            y = _mixer_lru(h, A(lru_w_in)[j], A(lru_conv_w)[j], A(lru_conv_b)[j], A(lru_w_a)[j], A(lru_b_a)[j],
                           A(lru_w_x)[j], A(lru_b_x)[j], A(lru_lam)[j])
            wo = A(lru_w_o)[j]
        else:
            y = _mixer_mla(h, positions, A(mla_w_in)[j], A(mla_q_norm)[j], A(mla_kv_norm)[j], A(mla_w_uq)[j], A(mla_w_ukv)[j])
            wo = A(mla_w_o)[j]
        h = _post(h, y, wo, A(ln_g)[i], A(ln_b)[i], A(mlp_w1)[i], A(mlp_w2)[i])
    return h
```

```python
import numpy as np
from contextlib import ExitStack
import concourse.bass as bass
import concourse.mybir as mybir
from concourse.bass_utils import run_bass_kernel_spmd

F32 = mybir.dt.float32
BF16 = mybir.dt.bfloat16
I32 = mybir.dt.int32
AF = mybir.ActivationFunctionType
ALU = mybir.AluOpType
AX = mybir.AxisListType

D = 1024
S = 8192
B = 2
DEPTH = 4
ALPHA = (2.0 * DEPTH) ** 0.25
LN_EPS = 1e-5
NCORE = 8
TT = 512
SEG = 2048

ENGS = ("pe", "act", "dve", "pool", "sp")
SAME_ENG_SYNC = True


class Ins:
    __slots__ = ("eng", "fn", "deps", "sig", "sem", "val", "dma", "gidx")

    def __init__(self, eng, fn, dma, gidx):
        self.eng = eng
        self.fn = fn
        self.deps = set()
        self.sig = False
        self.sem = None
        self.val = 0
        self.dma = dma
        self.gidx = gidx


_PHASE_ID = [0]


class Prog:
    def __init__(self, nc, n_dma_sems=24):
        self.nc = nc
        _PHASE_ID[0] += 1
        self.pid = _PHASE_ID[0]
        self.all = []
        self.last_w = {}
        self.readers = {}
        self.stack = ExitStack()
        self.n_dma_sems = n_dma_sems
        self.out_dmas = []

    def sb(self, name, shape, dt):
        return self.stack.enter_context(self.nc.sbuf_tensor("%s_p%d" % (name, self.pid), list(shape), dt))

    def ps(self, name, shape, dt=F32):
        return self.stack.enter_context(self.nc.psum_tensor("%s_p%d" % (name, self.pid), list(shape), dt))

    def emit(self, eng, fn, r=(), w=(), dma=False, out=False):
        ins = Ins(eng, fn, dma, len(self.all))
        deps = set()
        for k in r:
            lw = self.last_w.get(k)
            if lw is not None:
                deps.add(lw)
        for k in w:
            lw = self.last_w.get(k)
            if lw is not None:
                deps.add(lw)
            for rd in self.readers.get(k, ()):
                deps.add(rd)
        for k in r:
            self.readers.setdefault(k, []).append(ins)
        for k in w:
            self.last_w[k] = ins
            self.readers[k] = []
        deps.discard(ins)
        for d in deps:
            if d.eng == eng and not d.dma:
                if eng == "pe" or eng == "sp" or not SAME_ENG_SYNC:
                    continue
            ins.deps.add(d)
        self.all.append(ins)
        if out:
            self.out_dmas.append(ins)
        return ins

    def dma(self, out, in_, r=(), w=(), eng="sp", is_out=False, slow=False):
        if slow:
            return self.emit(eng, lambda e: e.dma_start(out=out, in_=in_, allow_slow_non_contiguous=True),
                             r, w, dma=True, out=is_out)
        return self.emit(eng, lambda e: e.dma_start(out=out, in_=in_), r, w, dma=True, out=is_out)

    def mm(self, out, lhsT, rhs, start, stop, r=(), w=()):
        return self.emit("pe", lambda e: e.matmul(out, lhsT, rhs, start=start, stop=stop), r, w)

    def act(self, out, in_, func, r=(), w=(), bias=None, scale=None, eng="act"):
        kw = {}
        if bias is not None:
            kw["bias"] = bias
        if scale is not None:
            kw["scale"] = scale
        return self.emit("act", lambda e: e.activation(out, in_, func, **kw), r, w)

    def tt(self, eng, out, in0, in1, op, r=(), w=()):
        return self.emit(eng, lambda e: e.tensor_tensor(out, in0, in1, op), r, w)

    def ts(self, eng, out, in0, s1, s2, op0, op1=None, r=(), w=()):
        if op1 is None:
            return self.emit(eng, lambda e: e.tensor_scalar(out, in0, s1, None, op0), r, w)
        return self.emit(eng, lambda e: e.tensor_scalar(out, in0, s1, s2, op0, op1), r, w)

    def stt(self, out, in0, scalar, in1, op0, op1, r=(), w=()):
        return self.emit("dve", lambda e: e.scalar_tensor_tensor(out, in0, scalar, in1, op0, op1), r, w)

    def copy(self, eng, out, in_, r=(), w=()):
        if eng == "act":
            return self.emit("act", lambda e: e.activation(out, in_, AF.Copy), r, w)
        return self.emit(eng, lambda e: e.tensor_copy(out, in_), r, w)

    def memset(self, eng, ap, val, w=()):
        return self.emit(eng, lambda e: e.memset(ap, val), (), w)

    def finalize(self):
        nc = self.nc
        dma_prev = [None] * self.n_dma_sems
        di = 0
        for ins in self.all:
            if ins.dma:
                s = di % self.n_dma_sems
                di += 1
                ins.sem = ("dma", s)
                ins.sig = True
                if dma_prev[s] is not None:
                    ins.deps.add(dma_prev[s])
                dma_prev[s] = ins
        for ins in self.all:
            for d in ins.deps:
                d.sig = True
        cnt = {}
        for ins in self.all:
            if not ins.sig:
                continue
            if not ins.dma:
                ins.sem = ("eng", ins.eng)
            c = cnt.get(ins.sem, 0) + (16 if ins.dma else 1)
            cnt[ins.sem] = c
            ins.val = c
        sems = {}
        for key in sorted(cnt, key=str):
            sems[key] = self.stack.enter_context(nc.semaphore("s%d_%s_%s" % ((self.pid,) + key)))
        per = {e: [i for i in self.all if i.eng == e] for e in ENGS}
        final = list(self.out_dmas)

        def run(engobj, lst, extra_final=()):
            have = {}
            for ins in lst:
                need = {}
                for d in ins.deps:
                    if need.get(d.sem, 0) < d.val:
                        need[d.sem] = d.val
                for sk, v in need.items():
                    if have.get(sk, 0) < v:
                        engobj.wait_ge(sems[sk], v)
                        have[sk] = v
                bi = ins.fn(engobj)
                if ins.sig:
                    bi.then_inc(sems[ins.sem], 16 if ins.dma else 1)
            for d in extra_final:
                if have.get(d.sem, 0) < d.val:
                    engobj.wait_ge(sems[d.sem], d.val)
                    have[d.sem] = d.val

        with nc.Block() as block:
            @block.gpsimd
            def _(e):
                for s in sems.values():
                    e.sem_clear(s)

        with nc.Block() as block:
            @block.sync
            def _(e):
                run(e, per["sp"], final)

            if per["pe"]:
                @block.tensor
                def _(e):
                    run(e, per["pe"])
            if per["act"]:
                @block.scalar
                def _(e):
                    run(e, per["act"])
            if per["dve"]:
                @block.vector
                def _(e):
                    run(e, per["dve"])
            if per["pool"]:
                @block.gpsimd
                def _(e):
                    run(e, per["pool"])
        self.stack.close()


class Ctx:
    def __init__(self, P, nbanks=8, ln_banks=(6, 7), bf_bank=False, n_wst=2, n_wbf=2):
        self.P = P
        self.psb = [P.ps("psb%d" % i, [128, 512]) for i in range(nbanks)]
        self.pst = P.ps("pst", [128, 1024], BF16) if bf_bank else None
        self.ln_banks = ln_banks
        self.wst = [P.sb("wst%d" % i, [128, 8, 512], F32) for i in range(n_wst)]
        self.wbf = [P.sb("wbf%d" % i, [128, 8, 512], BF16) for i in range(n_wbf)]
        self.wi = 0
        self.ones = P.sb("ones_f", [128, 128], F32)
        P.memset("pool", self.ones[:, :], 1.0, w=["ones_f"])
        self.sq = [P.sb("sq%d" % i, [128, 512], F32) for i in range(2)]
        self.sqi = 0
        self.st = {n: P.sb("st_" + n, [128, 512], F32) for n in ("mean", "msq", "var", "rstd", "nmr")}
        self.tmp = [P.sb("tmp%d" % i, [128, 512], F32) for i in range(2)]
        self.tmpi = 0

    def load_wblock(self, wdram_pkn, kc0, nkc, c0, ncol=512):
        P = self.P
        i = self.wi % 2
        self.wi += 1
        st, bf = self.wst[i], self.wbf[i]
        ks, kb = "wst%d" % i, "wbf%d" % i
        P.dma(st[:, 0:nkc, 0:ncol], wdram_pkn[:, kc0:kc0 + nkc, c0:c0 + ncol], w=[ks])
        P.copy("pool", bf[:, 0:nkc, 0:ncol], st[:, 0:nkc, 0:ncol], r=[ks], w=[kb])
        return bf, kb

    def load_resident(self, wdram_pkn, dst, dkey, nkc, ncol):
        P = self.P
        i = self.wi % len(self.wst)
        self.wi += 1
        st = self.wst[i]
        ks = "wst%d" % i
        P.dma(st[:, 0:nkc, 0:ncol], wdram_pkn[:, 0:nkc, 0:ncol], w=[ks])
        P.copy("pool", dst[:, 0:nkc, 0:ncol], st[:, 0:nkc, 0:ncol], r=[ks], w=[dkey])


def layer_norm_tile(C, z, zkey, g_ap, b_ap, out_f, okey_f, out_b, okey_b, nch=8, T=TT, eps=LN_EPS, gkey="lnp"):
    P = C.P
    nfeat = float(nch * 128)
    ps1, k1 = C.psb[C.ln_banks[0]], "psb%d" % C.ln_banks[0]
    ps2, k2 = C.psb[C.ln_banks[1]], "psb%d" % C.ln_banks[1]
    for c in range(nch):
        sq = C.sq[C.sqi % 2]
        ksq = "sq%d" % (C.sqi % 2)
        C.sqi += 1
        P.act(sq[:, 0:T], z[:, c, 0:T], AF.Square, r=[zkey + str(c)], w=[ksq])
        P.mm(ps1[:, 0:T], C.ones[:, :], z[:, c, 0:T], c == 0, c == nch - 1, r=["ones_f", zkey + str(c)], w=[k1])
        P.mm(ps2[:, 0:T], C.ones[:, :], sq[:, 0:T], c == 0, c == nch - 1, r=["ones_f", ksq], w=[k2])
    mean, msq, var, rstd, nmr = (C.st[n] for n in ("mean", "msq", "var", "rstd", "nmr"))
    P.act(mean[:, 0:T], ps1[:, 0:T], AF.Copy, r=[k1], w=["st_mean"], scale=1.0 / nfeat)
    P.tt("dve", msq[:, 0:T], mean[:, 0:T], mean[:, 0:T], ALU.mult, r=["st_mean"], w=["st_msq"])
    P.stt(var[:, 0:T], ps2[:, 0:T], 1.0 / nfeat, msq[:, 0:T], ALU.mult, ALU.subtract, r=[k2, "st_msq"], w=["st_var"])
    P.ts("dve", var[:, 0:T], var[:, 0:T], eps, None, ALU.add, r=["st_var"], w=["st_var"])
    P.act(rstd[:, 0:T], var[:, 0:T], AF.Ln, r=["st_var"], w=["st_rstd"])
    P.act(rstd[:, 0:T], rstd[:, 0:T], AF.Exp, r=["st_rstd"], w=["st_rstd"], scale=-0.5)
    P.stt(nmr[:, 0:T], mean[:, 0:T], -1.0, rstd[:, 0:T], ALU.mult, ALU.mult, r=["st_mean", "st_rstd"], w=["st_nmr"])
    for c in range(nch):
        t = C.tmp[C.tmpi % 2]
        kt = "tmp%d" % (C.tmpi % 2)
        C.tmpi += 1
        P.tt("dve", t[:, 0:T], z[:, c, 0:T], rstd[:, 0:T], ALU.mult, r=[zkey + str(c), "st_rstd"], w=[kt])
        P.tt("pool", t[:, 0:T], t[:, 0:T], nmr[:, 0:T], ALU.add, r=[kt, "st_nmr"], w=[kt])
        if b_ap is not None:
            P.act(out_f[:, c, 0:T], t[:, 0:T], AF.Identity, r=[kt, gkey], w=[okey_f + str(c)],
                  bias=b_ap[:, c:c + 1], scale=g_ap[:, c:c + 1])
        else:
            P.act(out_f[:, c, 0:T], t[:, 0:T], AF.Copy, r=[kt, gkey], w=[okey_f + str(c)],
                  scale=g_ap[:, c:c + 1])
        if out_b is not None:
            P.copy("pool", out_b[:, c, 0:T], out_f[:, c, 0:T], r=[okey_f + str(c)], w=[okey_b + str(c)])


def build_post(F, ntok=SEG):
    nc = bass.Bass("TRN2", target_bir_lowering=False)
    FC = F // 128
    yT = nc.dram_tensor("yT", [F, ntok], F32, kind="ExternalInput").ap()
    rT = nc.dram_tensor("rT", [D, ntok], F32, kind="ExternalInput").ap()
    W = {}
    for n, shp in (("wo", [F, D]), ("w1", [D, 4 * D]), ("w2", [4 * D, D]), ("lng", [128, 2, 8]), ("lnb", [128, 2, 8])):
        W[n] = nc.dram_tensor(n, shp, F32, kind="ExternalInput").ap()
    oT = nc.dram_tensor("oT", [D, ntok], F32, kind="ExternalOutput").ap()
    yv = yT.rearrange("(c p) t -> p c t", p=128)
    rv = rT.rearrange("(c p) t -> p c t", p=128)
    ov = oT.rearrange("(c p) t -> p c t", p=128)
    emit_post(nc, W, list(range(FC)), FC,
              ysrc=lambda e, cache, t, c0, n: yv[:, c0:c0 + n, t * TT:(t + 1) * TT], y_bf16=False,
              resid=None, rsrc=lambda t: rv[:, :, t * TT:(t + 1) * TT],
              odst=lambda t: ov[:, :, t * TT:(t + 1) * TT], hdst=None, ntok=ntok)
    return nc


def emit_post(nc, W, ychunks, FCY, ysrc, y_bf16, resid, rsrc, odst, hdst, ntok=SEG):
    wo, w1, w2, lng, lnb = (W[k] for k in ("wo", "w1", "w2", "lng", "lnb"))
    FW = len(ychunks)
    P = Prog(nc)
    C = Ctx(P)
    wov = wo.rearrange("(c p) n -> p c n", p=128)
    w1v = w1.rearrange("(c p) n -> p c n", p=128)
    w2v = w2.rearrange("(c p) n -> p c n", p=128)
    lnp_g = P.sb("lnp_g", [128, 2, 8], F32)
    lnp_b = P.sb("lnp_b", [128, 2, 8], F32)
    P.dma(lnp_g[:, :, :], lng, w=["lnp"])
    P.dma(lnp_b[:, :, :], lnb, w=["lnp"])
    yst = None if y_bf16 else P.sb("yst", [128, 8, TT], F32)
    yb = P.sb("yb", [128, FCY, TT], BF16)
    zloc = None if resid is not None else P.sb("z", [128, 8, TT], F32)
    hb = P.sb("hb", [128, 8, TT], BF16)
    hid = P.sb("hid", [128, 32, TT], BF16)
    rl = [P.sb("rl%d" % i, [128, TT], F32) for i in range(2)]
    Z8 = ["z%d" % i for i in range(8)]
    ycache = {}
    for t in range(ntok // TT):
        t0 = t * TT
        for c0 in range(0, FCY, 8):
            n = min(8, FCY - c0)
            if y_bf16:
                P.emit("sp", lambda e, t=t, c0=c0, n=n: e.dma_start(out=yb[:, c0:c0 + n, :], in_=ysrc(e, ycache, t, c0, n)),
                       (), ["yb"], dma=True)
            else:
                P.emit("sp", lambda e, t=t, c0=c0, n=n: e.dma_start(out=yst[:, 0:n, :], in_=ysrc(e, ycache, t, c0, n)),
                       (), ["yst"], dma=True)
                P.copy("dve", yb[:, c0:c0 + n, :], yst[:, 0:n, :], r=["yst"], w=["yb"])
        if resid is None:
            z = zloc
            P.dma(z[:, :, :], rsrc(t), w=Z8, eng="pool")
        else:
            z = resid[:, :, t0:t0 + TT]
        blocks = [(k0, min(8, FW - k0)) for k0 in range(0, FW, 8)]
        for half in range(2):
            for bi, (k0, n) in enumerate(blocks):
                wb, kb = C.load_wblock(wov, k0, n, half * 512)
                for m in range(4):
                    for kc in range(n):
                        P.mm(C.psb[m][:, :], wb[:, kc, m * 128:(m + 1) * 128], yb[:, ychunks[k0 + kc], :],
                             bi == 0 and kc == 0, bi == len(blocks) - 1 and kc == n - 1,
                             r=[kb, "yb"], w=["psb%d" % m])
            for m in range(4):
                c = half * 4 + m
                P.stt(z[:, c, :], z[:, c, :], ALPHA, C.psb[m][:, :], ALU.mult, ALU.add,
                      r=["z%d" % c, "psb%d" % m], w=["z%d" % c])
        layer_norm_tile(C, z, "z", lnp_g[:, 0, :], lnp_b[:, 0, :], z, "z", hb, "hb")
        for nb in range(8):
            wb, kb = C.load_wblock(w1v, 0, 8, nb * 512)
            for m in range(4):
                pb = C.psb[m]
                kp = "psb%d" % m
                for kc in range(8):
                    P.mm(pb[:, :], wb[:, kc, m * 128:(m + 1) * 128], hb[:, kc, :], kc == 0, kc == 7,
                         r=[kb, "hb%d" % kc], w=[kp])
                rr = rl[m % 2]
                kr = "rl%d" % (m % 2)
                P.act(rr[:, :], pb[:, :], AF.Relu, r=[kp], w=[kr])
                P.tt("dve", hid[:, nb * 4 + m, :], rr[:, :], rr[:, :], ALU.mult, r=[kr], w=["hid"])
        for half in range(2):
            for kb4 in range(4):
                wb, kb = C.load_wblock(w2v, kb4 * 8, 8, half * 512)
                for m in range(4):
                    for kc in range(8):
                        P.mm(C.psb[m][:, :], wb[:, kc, m * 128:(m + 1) * 128], hid[:, kb4 * 8 + kc, :],
                             kb4 == 0 and kc == 0, kb4 == 3 and kc == 7, r=[kb, "hid"], w=["psb%d" % m])
            for m in range(4):
                c = half * 4 + m
                P.stt(z[:, c, :], z[:, c, :], ALPHA, C.psb[m][:, :], ALU.mult, ALU.add,
                      r=["z%d" % c, "psb%d" % m], w=["z%d" % c])
        layer_norm_tile(C, z, "z", lnp_g[:, 1, :], lnp_b[:, 1, :], z, "z", hb if hdst is not None else None, "hb")
        if odst is not None:
            P.dma(odst(t), z[:, :, :], r=Z8, is_out=True)
        if hdst is not None:
            P.dma(hdst(t), hb[:, :, :], r=["hb%d" % i for i in range(8)], is_out=True)
    P.finalize()


def rope_tables(P, pos_dram, t0, T, inv_ap, bufs, sign_ap=None):
    import math
    npart = bufs["np"]
    posi, a, b, ki, cs, sn = (bufs[k] for k in ("posi", "a", "b", "ki", "cs", "sn"))
    sl = slice(0, npart)
    P.dma(posi[sl, 0:T], bass.AP(pos_dram, t0, [[0, npart], [1, T]]), w=["rt_posi"])
    P.copy("dve", a[sl, 0:T], posi[sl, 0:T], r=["rt_posi"], w=["rt_a"])
    P.ts("dve", a[sl, 0:T], a[sl, 0:T], inv_ap, None, ALU.mult, r=["rt_a", "rt_inv"], w=["rt_a"])
    P.ts("dve", b[sl, 0:T], a[sl, 0:T], 1.0 / (2 * math.pi), None, ALU.mult, r=["rt_a"], w=["rt_b"])
    P.copy("dve", ki[sl, 0:T], b[sl, 0:T], r=["rt_b"], w=["rt_ki"])
    P.copy("dve", b[sl, 0:T], ki[sl, 0:T], r=["rt_ki"], w=["rt_b"])
    c1 = float(np.float32(6.28125))
    c2 = float(2 * math.pi - 6.28125)
    P.stt(a[sl, 0:T], b[sl, 0:T], -c1, a[sl, 0:T], ALU.mult, ALU.add, r=["rt_a", "rt_b"], w=["rt_a"])
    P.stt(a[sl, 0:T], b[sl, 0:T], -c2, a[sl, 0:T], ALU.mult, ALU.add, r=["rt_a", "rt_b"], w=["rt_a"])
    P.ts("dve", b[sl, 0:T], a[sl, 0:T], math.pi / 2, None, ALU.add, r=["rt_a"], w=["rt_b"])
    P.ts("dve", cs[sl, 0:T], b[sl, 0:T], math.pi, None, ALU.is_gt, r=["rt_b"], w=["rt_cs"])
    P.stt(b[sl, 0:T], cs[sl, 0:T], -2 * math.pi, b[sl, 0:T], ALU.mult, ALU.add, r=["rt_b", "rt_cs"], w=["rt_b"])
    lim = 3.14159
    P.ts("dve", b[sl, 0:T], b[sl, 0:T], lim, -lim, ALU.min, ALU.max, r=["rt_b"], w=["rt_b"])
    P.ts("dve", a[sl, 0:T], a[sl, 0:T], lim, -lim, ALU.min, ALU.max, r=["rt_a"], w=["rt_a"])
    P.act(cs[sl, 0:T], b[sl, 0:T], AF.Sin, r=["rt_b"], w=["rt_cs"])
    if sign_ap is None:
        P.act(sn[sl, 0:T], a[sl, 0:T], AF.Sin, r=["rt_a"], w=["rt_sn"])
    else:
        P.act(sn[sl, 0:T], a[sl, 0:T], AF.Sin, r=["rt_a", "rt_inv"], w=["rt_sn"], scale=sign_ap)


def make_identity_bf(P, idb, key):
    P.memset("pool", idb[:, :], 0.0, w=[key])
    P.emit("pool", lambda e: e.affine_select(idb[:, :], idb[:, :], [[-1, 128]], ALU.not_equal, 1.0,
                                             base=0, channel_multiplier=1), r=[key], w=[key])


def build_ret(nseq=S):
    nc = bass.Bass("TRN2", target_bir_lowering=False)
    W = {}
    hT = nc.dram_tensor("hT", [D, nseq], F32, kind="ExternalInput").ap()
    for n, shp in (("wq", [D, 256]), ("wk", [D, 256]), ("wv", [D, 512]), ("wg", [D, 512]), ("gng", [128, 4]),
                   ("inv", [128, 1]), ("maskT", [128, 128]), ("qdec", [128, 512]), ("kdec", [128, 1]), ("cdec", [128, 1])):
        W[n] = nc.dram_tensor(n, shp, F32, kind="ExternalInput").ap()
    W["pos"] = nc.dram_tensor("pos", [1, nseq], I32, kind="ExternalInput")
    yT = nc.dram_tensor("yT", [512, nseq], F32, kind="ExternalOutput").ap()
    hv = hT.rearrange("(c p) t -> p c t", p=128)
    yv = yT.rearrange("(c p) t -> p c t", p=128)
    emit_ret(nc, W, lambda t: hv[:, :, t * TT:(t + 1) * TT], lambda t: yv[:, :, t * TT:(t + 1) * TT], nseq, False, False)
    return nc


def emit_ret(nc, W, hsrc, ydst, nseq, in_bf16, out_bf16):
    wq, wk, wv, wg, gng_d, inv_d, maskT_d, qdec_d, kdec_d, cdec_d, pos = (
        W[k] for k in ("wq", "wk", "wv", "wg", "gng", "inv", "maskT", "qdec", "kdec", "cdec", "pos"))
    P = Prog(nc)
    C = Ctx(P, nbanks=7, ln_banks=(4, 5), bf_bank=True, n_wst=1, n_wbf=0)
    pk = lambda w: w.rearrange("(c p) n -> p c n", p=128)
    wqb = P.sb("wqb", [128, 8, 256], BF16)
    wkb = P.sb("wkb", [128, 8, 256], BF16)
    wvb = P.sb("wvb", [128, 8, 512], BF16)
    wgb = P.sb("wgb", [128, 8, 512], BF16)
    C.load_resident(pk(wq), wqb, "wqb", 8, 256)
    C.load_resident(pk(wk), wkb, "wkb", 8, 256)
    C.load_resident(pk(wv), wvb, "wvb", 8, 512)
    C.load_resident(pk(wg), wgb, "wgb", 8, 512)
    gng = P.sb("gng_s", [128, 4], F32)
    inv = P.sb("inv_s", [128, 1], F32)
    maskT = P.sb("maskT_s", [128, 128], F32)
    qdec = P.sb("qdec_s", [128, 512], F32)
    kdec = P.sb("kdec_s", [128, 1], F32)
    cdec = P.sb("cdec_s", [128, 1], F32)
    P.dma(gng[:, :], gng_d, w=["gng"])
    P.dma(inv[:, :], inv_d, w=["rt_inv"])
    P.dma(maskT[:, :], maskT_d, w=["maskT"])
    P.dma(qdec[:, :], qdec_d, w=["qdec"])
    P.dma(kdec[:, :], kdec_d, w=["kdec"])
    P.dma(cdec[:, :], cdec_d, w=["cdec"])
    idb = P.sb("idb", [128, 128], BF16)
    make_identity_bf(P, idb, "idb")
    rb = {"np": 128, "posi": P.sb("rt_posi", [128, TT], I32), "a": P.sb("rt_a", [128, TT], F32),
          "b": P.sb("rt_b", [128, TT], F32), "ki": P.sb("rt_ki", [128, TT], I32),
          "cs": P.sb("rt_cs", [128, TT], F32), "sn": P.sb("rt_sn", [128, TT], F32)}
    csk = P.sb("csk", [128, TT], F32)
    snk = P.sb("snk", [128, TT], F32)
    xst = C.wst[0]
    xb = P.sb("xb", [128, 8, TT], BF16)
    yob = P.sb("yob", [128, 4, TT], BF16) if out_bf16 else None
    qf = P.sb("qf", [128, 2, TT], F32)
    qb = P.sb("qb", [128, 2, TT], BF16)
    qdb = P.sb("qdb", [128, 2, TT], BF16)
    kb = P.sb("kb", [128, 2, TT], BF16)
    kdT = P.sb("kdT", [128, 4, 256], BF16)
    vb = P.sb("vb", [128, 4, 512], BF16)
    sg = P.sb("sg", [128, 4, TT], F32)
    of = P.sb("of", [128, 4, TT], F32)
    ra = P.sb("ra", [128, TT], F32)
    rbb = P.sb("rbb", [128, TT], F32)
    pT = P.sb("pT", [128, 128], BF16)
    st = P.sb("st", [128, 2, 512], F32)
    stb = P.sb("stb", [128, 2, 512], BF16)
    P.memset("pool", st[:, :, :], 0.0, w=["st0", "st1"])
    P.memset("pool", stb[:, :, :], 0.0, w=["stb0", "stb1"])
    X8 = ["xb%d" % i for i in range(8)]
    for t in range(nseq // TT):
        t0 = t * TT
        if in_bf16:
            for k in range(4):
                P.dma(xb[:, 2 * k:2 * k + 2, :], hsrc(t)[:, k, :, :], w=["xb%d" % (2 * k), "xb%d" % (2 * k + 1)])
        else:
            P.dma(xst[:, :, :], hsrc(t), w=["wst0"])
            for kc in range(8):
                P.copy("dve" if kc % 2 == 0 else "pool", xb[:, kc, :], xst[:, kc, :], r=["wst0"], w=["xb%d" % kc])
        rope_tables(P, pos, t0, TT, inv[:, 0:1], rb)
        cs, sn = rb["cs"], rb["sn"]
        P.ts("pool", csk[:, :], cs[:, :], 1.0 / 16.0, None, ALU.mult, r=["rt_cs"], w=["csk"])
        P.ts("pool", snk[:, :], sn[:, :], 1.0 / 16.0, None, ALU.mult, r=["rt_sn"], w=["snk"])
        for m in range(2):
            for kc in range(8):
                P.mm(C.psb[m][:, :], wqb[:, kc, m * 128:(m + 1) * 128], xb[:, kc, :], kc == 0, kc == 7,
                     r=["wqb", "xb%d" % kc], w=["psb%d" % m])
        P.tt("dve", ra[:, :], C.psb[0][:, :], cs[:, :], ALU.mult, r=["psb0", "rt_cs"], w=["ra"])
        P.tt("dve", rbb[:, :], C.psb[1][:, :], sn[:, :], ALU.mult, r=["psb1", "rt_sn"], w=["rbb"])
        P.tt("pool", qf[:, 0, :], ra[:, :], rbb[:, :], ALU.subtract, r=["ra", "rbb"], w=["qf0"])
        P.tt("dve", ra[:, :], C.psb[0][:, :], sn[:, :], ALU.mult, r=["psb0", "rt_sn"], w=["ra"])
        P.tt("dve", rbb[:, :], C.psb[1][:, :], cs[:, :], ALU.mult, r=["psb1", "rt_cs"], w=["rbb"])
        P.tt("pool", qf[:, 1, :], ra[:, :], rbb[:, :], ALU.add, r=["ra", "rbb"], w=["qf1"])
        for m in range(2):
            P.copy("act", qb[:, m, :], qf[:, m, :], r=["qf%d" % m], w=["qb%d" % m])
            P.tt("pool", qdb[:, m, :], qf[:, m, :], qdec[:, :], ALU.mult, r=["qf%d" % m, "qdec"], w=["qdb%d" % m])
        for m in range(2):
            for kc in range(8):
                P.mm(C.psb[2 + m][:, :], wkb[:, kc, m * 128:(m + 1) * 128], xb[:, kc, :], kc == 0, kc == 7,
                     r=["wkb", "xb%d" % kc], w=["psb%d" % (2 + m)])
        P.tt("dve", ra[:, :], C.psb[2][:, :], csk[:, :], ALU.mult, r=["psb2", "csk"], w=["ra"])
        P.tt("dve", rbb[:, :], C.psb[3][:, :], snk[:, :], ALU.mult, r=["psb3", "snk"], w=["rbb"])
        P.tt("pool", kb[:, 0, :], ra[:, :], rbb[:, :], ALU.subtract, r=["ra", "rbb"], w=["kb0"])
        P.tt("dve", ra[:, :], C.psb[2][:, :], snk[:, :], ALU.mult, r=["psb2", "snk"], w=["ra"])
        P.tt("dve", rbb[:, :], C.psb[3][:, :], csk[:, :], ALU.mult, r=["psb3", "csk"], w=["rbb"])
        P.tt("pool", kb[:, 1, :], ra[:, :], rbb[:, :], ALU.add, r=["ra", "rbb"], w=["kb1"])
        for m in range(4):
            bk = m % 4
            for kc in range(8):
                P.mm(C.psb[bk][:, :], wgb[:, kc, m * 128:(m + 1) * 128], xb[:, kc, :], kc == 0, kc == 7,
                     r=["wgb", "xb%d" % kc], w=["psb%d" % bk])
            P.act(sg[:, m, :], C.psb[bk][:, :], AF.Silu, r=["psb%d" % bk], w=["sg%d" % m])
        for i in range(4):
            bk = i % 4
            for kc in range(8):
                P.mm(C.psb[bk][:, :], xb[:, kc, i * 128:(i + 1) * 128], wvb[:, kc, :], kc == 0, kc == 7,
                     r=["wvb", "xb%d" % kc], w=["psb%d" % bk])
            P.copy("act", vb[:, i, :], C.psb[bk][:, :], r=["psb%d" % bk], w=["vb%d" % i])
        for i in range(4):
            for m in range(2):
                col = (i * 2 + m) * 128
                P.emit("pe", lambda e, col=col, m=m, i=i: e.transpose(C.pst[:, col:col + 128], kb[:, m, i * 128:(i + 1) * 128], idb[:, :]),
                       r=["kb%d" % m, "idb"], w=["pst"])
        P.ts("dve", kdT[:, :, :], C.pst[:, :].rearrange("p (i d) -> p i d", i=4), kdec[:, 0:1], None, ALU.mult,
             r=["pst", "kdec"], w=["kdT%d" % i for i in range(4)])
        for i in range(4):
            cl = slice(i * 128, (i + 1) * 128)
            for m in range(2):
                P.mm(C.psb[4][:, 0:128], kb[:, m, cl], qb[:, m, cl], m == 0, m == 1,
                     r=["kb%d" % m, "qb%d" % m], w=["psb4"])
            P.tt("dve", pT[:, :], C.psb[4][:, 0:128], maskT[:, :], ALU.mult, r=["psb4", "maskT"], w=["pT"])
            for vc in range(4):
                vs = slice(vc * 128, (vc + 1) * 128)
                P.mm(C.psb[5][:, vs], vb[:, i, vs], pT[:, :], True, False, r=["vb%d" % i, "pT"], w=["psb5"])
                for m in range(2):
                    P.mm(C.psb[5][:, vs], stb[:, m, vs], qdb[:, m, cl], False, m == 1,
                         r=["stb%d" % m, "qdb%d" % m], w=["psb5"])
            P.emit("act", lambda e, cl=cl: e.activation(of[:, :, cl], C.psb[5][:, :].rearrange("p (c t) -> p c t", c=4), AF.Copy),
                   r=["psb5"], w=["of%d" % c for c in range(4)])
            for m in range(2):
                ds_ = slice(m * 128, (m + 1) * 128)
                P.mm(C.psb[6][:, :], kdT[:, i, ds_], vb[:, i, :], True, True, r=["kdT%d" % i, "vb%d" % i], w=["psb6"])
                P.stt(st[:, m, :], st[:, m, :], cdec[:, 0:1], C.psb[6][:, :], ALU.mult, ALU.add,
                      r=["st%d" % m, "cdec", "psb6"], w=["st%d" % m])
                P.copy("pool", stb[:, m, :], st[:, m, :], r=["st%d" % m], w=["stb%d" % m])
        layer_norm_tile(C, of, "of", gng, None, of, "of", None, None, nch=4, gkey="gng")
        yo, ky = (yob, "yob") if out_bf16 else (of, "of")
        for c in range(4):
            P.tt("dve" if c % 2 == 0 else "pool", yo[:, c, :], of[:, c, :], sg[:, c, :], ALU.mult,
                 r=["of%d" % c, "sg%d" % c], w=[ky + str(c)])
        P.dma(ydst(t), yo[:, :, :], r=[ky + str(c) for c in range(4)], is_out=True)
    P.finalize()


def ret_consts(h):
    idx = np.arange(128, dtype=np.float64)
    lg = np.log1p(-np.exp2(-5.0 - h))
    diff = idx[None, :] - idx[:, None]
    maskT = np.where(diff >= 0, np.exp(lg * np.maximum(diff, 0.0)), 0.0)
    qd = np.exp(lg * (idx + 1.0))
    qdec = np.broadcast_to(np.tile(qd, 4)[None, :], (128, 512))
    kdec = np.exp(lg * (127.0 - idx))[:, None]
    cdec = np.full((128, 1), np.exp(lg * 128.0))
    inv = (10000.0 ** (-np.arange(128, dtype=np.float32) / np.float32(128))).astype(np.float32)[:, None]
    f = lambda a: np.ascontiguousarray(a, dtype=np.float32)
    return dict(maskT=f(maskT), qdec=f(qdec), kdec=f(kdec), cdec=f(cdec), inv=f(inv))


NBLK = 3


def build_lru(nseq=S):
    nc = bass.Bass("TRN2", target_bir_lowering=False)
    W3 = NBLK * 128
    W = {}
    hT = nc.dram_tensor("hT", [D, nseq], F32, kind="ExternalInput").ap()
    for n, shp in (("wgt", [D, W3]), ("wrc", [D, W3]), ("wa", [128, NBLK, 128]), ("wx", [128, NBLK, 128]), ("vec", [128, NBLK, 8])):
        W[n] = nc.dram_tensor(n, shp, F32, kind="ExternalInput").ap()
    yT = nc.dram_tensor("yT", [W3, nseq], F32, kind="ExternalOutput").ap()
    hv = hT.rearrange("(c p) t -> p c t", p=128)
    yv = yT.rearrange("(c p) t -> p c t", p=128)
    emit_lru(nc, W, lambda t: hv[:, :, t * TT:(t + 1) * TT], lambda t: yv[:, :, t * TT:(t + 1) * TT], nseq, False, False)
    return nc


def emit_lru(nc, W, hsrc, ydst, nseq, in_bf16, out_bf16):
    W3 = NBLK * 128
    wgt, wrc, wa_d, wx_d, vec_d = (W[k] for k in ("wgt", "wrc", "wa", "wx", "vec"))
    P = Prog(nc)
    C = Ctx(P, nbanks=8, n_wst=1, n_wbf=0)
    pk = lambda w: w.rearrange("(c p) n -> p c n", p=128)
    wgb = P.sb("wgb", [128, 8, W3], BF16)
    wrb = P.sb("wrb", [128, 8, W3], BF16)
    C.load_resident(pk(wgt), wgb, "wgb", 8, W3)
    C.load_resident(pk(wrc), wrb, "wrb", 8, W3)
    waf = P.sb("waf", [128, NBLK, 128], F32)
    wxf = P.sb("wxf", [128, NBLK, 128], F32)
    wab = P.sb("wab", [128, NBLK, 128], BF16)
    wxb = P.sb("wxb", [128, NBLK, 128], BF16)
    P.dma(waf[:, :, :], wa_d, w=["waf"])
    P.dma(wxf[:, :, :], wx_d, w=["wxf"])
    P.copy("dve", wab[:, :, :], waf[:, :, :], r=["waf"], w=["wab"])
    P.copy("dve", wxb[:, :, :], wxf[:, :, :], r=["wxf"], w=["wxb"])
    vec = P.sb("vec_s", [128, NBLK, 8], F32)
    P.dma(vec[:, :, :], vec_d, w=["vec"])
    der = P.sb("der", [128, NBLK, 4], F32)
    P.act(der[:, :, 0], vec[:, :, 7], AF.Exp, r=["vec"], w=["der"], scale=-1.0)
    P.ts("dve", der[:, :, 0], der[:, :, 0], 1.0, None, ALU.add, r=["der"], w=["der"])
    P.act(der[:, :, 0], der[:, :, 0], AF.Ln, r=["der"], w=["der"])
    P.ts("dve", der[:, :, 1], der[:, :, 0], -8.0, None, ALU.mult, r=["der"], w=["der"])
    P.ts("dve", der[:, :, 2], der[:, :, 0], -16.0, None, ALU.mult, r=["der"], w=["der"])
    P.ts("dve", der[:, :, 3], der[:, :, 0], 8.0, None, ALU.mult, r=["der"], w=["der"])
    xst = C.wst[0]
    xb = P.sb("xb", [128, 8, TT], BF16)
    rec = P.sb("rec", [128, NBLK, TT + 4], F32)
    P.memset("pool", rec[:, :, :], 0.0, w=["rec%d" % b for b in range(NBLK)])
    hprev = P.sb("hprev", [128, NBLK], F32)
    P.memset("pool", hprev[:, :], 0.0, w=["hprev%d" % b for b in range(NBLK)])
    gt = P.sb("gt", [128, TT], F32)
    g2 = P.sb("g2", [128, TT], F32)
    u = P.sb("u", [128, TT], F32)
    ub = P.sb("ub", [128, TT], BF16)
    rr = P.sb("rr", [128, TT], F32)
    ii = P.sb("ii", [128, TT], F32)
    aa = P.sb("aa", [128, TT], F32)
    ee = P.sb("ee", [128, TT], F32)
    th = P.sb("th", [128, TT], F32)
    hs = P.sb("hs", [128, TT], F32)
    yo = P.sb("yo", [128, NBLK, TT], BF16 if out_bf16 else F32)
    for t in range(nseq // TT):
        t0 = t * TT
        if in_bf16:
            for k in range(4):
                P.dma(xb[:, 2 * k:2 * k + 2, :], hsrc(t)[:, k, :, :], w=["xb%d" % (2 * k), "xb%d" % (2 * k + 1)])
        else:
            P.dma(xst[:, :, :], hsrc(t), w=["wst0"])
            for kc in range(8):
                P.copy("dve" if kc % 2 == 0 else "pool", xb[:, kc, :], xst[:, kc, :], r=["wst0"], w=["xb%d" % kc])
        for b in range(NBLK):
            bs = slice(b * 128, (b + 1) * 128)
            kr = "rec%d" % b
            for kc in range(8):
                P.mm(C.psb[0][:, :], wgb[:, kc, bs], xb[:, kc, :], kc == 0, kc == 7, r=["wgb", "xb%d" % kc], w=["psb0"])
            P.act(g2[:, :], C.psb[0][:, :], AF.Square, r=["psb0"], w=["g2"])
            P.ts("dve", g2[:, :], g2[:, :], 0.044715, 1.0, ALU.mult, ALU.add, r=["g2"], w=["g2"])
            P.tt("dve", g2[:, :], g2[:, :], C.psb[0][:, :], ALU.mult, r=["g2", "psb0"], w=["g2"])
            P.act(g2[:, :], g2[:, :], AF.Sigmoid, r=["g2"], w=["g2"], scale=1.5957691216057308)
            P.tt("dve", gt[:, :], g2[:, :], C.psb[0][:, :], ALU.mult, r=["g2", "psb0"], w=["gt"])
            for kc in range(8):
                P.mm(C.psb[1][:, :], wrb[:, kc, bs], xb[:, kc, :], kc == 0, kc == 7, r=["wrb", "xb%d" % kc], w=["psb1"])
            P.copy("act", rec[:, b, 3:3 + TT], C.psb[1][:, :], r=["psb1"], w=[kr])
            P.ts("dve", u[:, :], rec[:, b, 3:3 + TT], vec[:, b, 3:4], vec[:, b, 4:5], ALU.mult, ALU.add, r=[kr, "vec"], w=["u"])
            for k in range(3):
                P.stt(u[:, :], rec[:, b, k:k + TT], vec[:, b, k:k + 1], u[:, :], ALU.mult, ALU.add, r=[kr, "vec", "u"], w=["u"])
            P.copy("pool", rec[:, b, 0:3], rec[:, b, TT:TT + 3], r=[kr], w=[kr])
            P.copy("pool", ub[:, :], u[:, :], r=["u"], w=["ub"])
            P.mm(C.psb[2][:, :], wab[:, b, :], ub[:, :], True, True, r=["wab", "ub"], w=["psb2"])
            P.mm(C.psb[3][:, :], wxb[:, b, :], ub[:, :], True, True, r=["wxb", "ub"], w=["psb3"])
            P.act(rr[:, :], C.psb[2][:, :], AF.Sigmoid, r=["psb2", "vec"], w=["rr"], bias=vec[:, b, 5:6])
            P.act(ii[:, :], C.psb[3][:, :], AF.Sigmoid, r=["psb3", "vec"], w=["ii"], bias=vec[:, b, 6:7])
            P.act(aa[:, :], rr[:, :], AF.Exp, r=["rr", "der"], w=["aa"], scale=der[:, b, 1:2])
            P.act(ee[:, :], rr[:, :], AF.Exp, r=["rr", "der"], w=["ee"], scale=der[:, b, 2:3])
            P.act(th[:, :], rr[:, :], AF.Tanh, r=["rr", "der"], w=["th"], scale=der[:, b, 3:4])
            P.stt(ee[:, :], ee[:, :], 1.0, th[:, :], ALU.add, ALU.mult, r=["ee", "th"], w=["ee"])
            P.act(ee[:, :], ee[:, :], AF.Sqrt, r=["ee"], w=["ee"])
            P.tt("pool", ii[:, :], ii[:, :], u[:, :], ALU.mult, r=["ii", "u"], w=["ii"])
            P.tt("pool", ii[:, :], ii[:, :], ee[:, :], ALU.mult, r=["ii", "ee"], w=["ii"])
            P.emit("dve", lambda e, b=b: e.tensor_tensor_scan(hs[:, :], aa[:, :], ii[:, :], hprev[:, b:b + 1], ALU.mult, ALU.add),
                   r=["aa", "ii", "hprev%d" % b], w=["hs"])
            P.copy("dve", hprev[:, b:b + 1], hs[:, TT - 1:TT], r=["hs"], w=["hprev%d" % b])
            P.tt("pool", yo[:, b, :], hs[:, :], gt[:, :], ALU.mult, r=["hs", "gt"], w=["yo%d" % b])
        P.dma(ydst(t), yo[:, :, :], r=["yo%d" % b for b in range(NBLK)], is_out=True)
    P.finalize()


def rms_norm_tile(C, z, zkey, g_ap, gkey, out_b, okey, nch, eps, bank, T=TT):
    P = C.P
    ps, kp = C.psb[bank], "psb%d" % bank
    for c in range(nch):
        sq = C.sq[C.sqi % 2]
        ksq = "sq%d" % (C.sqi % 2)
        C.sqi += 1
        P.act(sq[:, 0:T], z[:, c, 0:T], AF.Square, r=[zkey + str(c)], w=[ksq])
        P.mm(ps[:, 0:T], C.ones[:, :], sq[:, 0:T], c == 0, c == nch - 1, r=["ones_f", ksq], w=[kp])
    rstd = C.st["rstd"]
    P.ts("dve", rstd[:, 0:T], ps[:, 0:T], 1.0 / (nch * 128.0), eps, ALU.mult, ALU.add, r=[kp], w=["st_rstd"])
    P.act(rstd[:, 0:T], rstd[:, 0:T], AF.Ln, r=["st_rstd"], w=["st_rstd"])
    P.act(rstd[:, 0:T], rstd[:, 0:T], AF.Exp, r=["st_rstd"], w=["st_rstd"], scale=-0.5)
    for c in range(nch):
        t = C.tmp[C.tmpi % 2]
        kt = "tmp%d" % (C.tmpi % 2)
        C.tmpi += 1
        P.tt("dve", t[:, 0:T], z[:, c, 0:T], rstd[:, 0:T], ALU.mult, r=[zkey + str(c), "st_rstd"], w=[kt])
        P.ts("pool", out_b[:, c, 0:T], t[:, 0:T], g_ap[:, c:c + 1], None, ALU.mult, r=[kt, gkey], w=[okey + str(c)])


QSCALE = 192.0 ** -0.5
RMS_EPS = 1e-6


def build_mla(nseq=S):
    nc = bass.Bass("TRN2", target_bir_lowering=False)
    W = {}
    hT = nc.dram_tensor("hT", [D, nseq], F32, kind="ExternalInput").ap()
    for n, shp in (("wcq", [D, 384]), ("wckv", [D, 256]), ("wkpe", [D, 128]), ("wuqn", [384, 256]), ("wuqr", [384, 256]),
                   ("wuk", [256, 256]), ("wuv", [256, 256]), ("qng", [128, 3]), ("kvng", [128, 2]), ("inv", [64, 2]),
                   ("mask", [128, 4 * 512])):
        W[n] = nc.dram_tensor(n, shp, F32, kind="ExternalInput").ap()
    W["pos"] = nc.dram_tensor("pos", [1, nseq], I32, kind="ExternalInput")
    oT = nc.dram_tensor("oT", [256, nseq], F32, kind="ExternalOutput").ap()
    hv = hT.rearrange("(c p) t -> p c t", p=128)
    ov = oT.rearrange("(c p) t -> p c t", p=128)
    emit_mla(nc, W, lambda t: hv[:, :, t * TT:(t + 1) * TT], lambda t: ov[:, :, t * TT:(t + 1) * TT], nseq, False, False)
    return nc


def emit_mla(nc, W, hsrc, ydst, nseq, in_bf16, out_bf16):
    NKB = nseq // 128
    wcq, wckv, wkpe, wuqn, wuqr, wuk, wuv, qng_d, kvng_d, inv_d, mask_d, pos = (
        W[k] for k in ("wcq", "wckv", "wkpe", "wuqn", "wuqr", "wuk", "wuv", "qng", "kvng", "inv", "mask", "pos"))
    P = Prog(nc)
    C = Ctx(P, nbanks=8, n_wst=1, n_wbf=0)
    pk = lambda w: w.rearrange("(c p) n -> p c n", p=128)
    wcqb = P.sb("wcqb", [128, 8, 384], BF16)
    wckvb = P.sb("wckvb", [128, 8, 256], BF16)
    wkpeb = P.sb("wkpeb", [128, 8, 128], BF16)
    wuqnb = P.sb("wuqnb", [128, 3, 256], BF16)
    wuqrb = P.sb("wuqrb", [128, 3, 256], BF16)
    wukb = P.sb("wukb", [128, 2, 256], BF16)
    wuvb = P.sb("wuvb", [128, 2, 256], BF16)
    C.load_resident(pk(wcq), wcqb, "wcqb", 8, 384)
    C.load_resident(pk(wckv), wckvb, "wckvb", 8, 256)
    C.load_resident(pk(wkpe), wkpeb, "wkpeb", 8, 128)
    C.load_resident(pk(wuqn), wuqnb, "wuqnb", 3, 256)
    C.load_resident(pk(wuqr), wuqrb, "wuqrb", 3, 256)
    C.load_resident(pk(wuk), wukb, "wukb", 2, 256)
    C.load_resident(pk(wuv), wuvb, "wuvb", 2, 256)
    qng = P.sb("qng_s", [128, 3], F32)
    kvng = P.sb("kvng_s", [128, 2], F32)
    inv = P.sb("inv_s", [64, 2], F32)
    P.dma(qng[:, :], qng_d, w=["qng"])
    P.dma(kvng[:, :], kvng_d, w=["kvng"])
    P.dma(inv[:, :], inv_d, w=["rt_inv"])
    maskb = P.sb("maskb", [128, 4, 512], BF16)
    C.load_resident(mask_d.rearrange("p (c n) -> p c n", c=4), maskb, "maskb", 4, 512)
    onesb = P.sb("onesb", [128, 128], BF16)
    P.memset("pool", onesb[:, :], 1.0, w=["onesb"])
    rb = {"np": 64, "posi": P.sb("rt_posi", [64, TT], I32), "a": P.sb("rt_a", [64, TT], F32),
          "b": P.sb("rt_b", [64, TT], F32), "ki": P.sb("rt_ki", [64, TT], I32),
          "cs": P.sb("rt_cs", [64, TT], F32), "sn": P.sb("rt_sn", [64, TT], F32)}
    knT = P.sb("knT", [128, 2, nseq], BF16)
    kpe = P.sb("kpe", [64, nseq], BF16)
    vb = P.sb("vb", [128, NKB, 256], BF16)
    xst = C.wst[0]
    xb = P.sb("xb", [128, 8, TT], BF16)
    cqf = P.sb("cqf", [128, 3, TT], F32)
    ckvf = P.sb("ckvf", [128, 2, TT], F32)
    cqn = P.sb("cqn", [128, 3, TT], BF16)
    ckvn = P.sb("ckvn", [128, 2, TT], BF16)
    ra = P.sb("ra", [64, TT], F32)
    rbb = P.sb("rbb", [64, TT], F32)
    qnb = P.sb("qnb", [128, 2, TT], BF16)
    qpb = P.sb("qpb", [64, 2, TT], BF16)
    pTs = [P.sb("pT%d" % i, [128, TT], BF16) for i in range(2)]
    rl = P.sb("rl", [128, TT], F32)
    ob = P.sb("ob", [128, 2, TT], BF16 if out_bf16 else F32)
    pti = 0
    for t in range(nseq // TT):
        t0 = t * TT
        ts_ = slice(t0, t0 + TT)
        if in_bf16:
            for k in range(4):
                P.dma(xb[:, 2 * k:2 * k + 2, :], hsrc(t)[:, k, :, :], w=["xb%d" % (2 * k), "xb%d" % (2 * k + 1)])
        else:
            P.dma(xst[:, :, :], hsrc(t), w=["wst0"])
            for kc in range(8):
                P.copy("dve" if kc % 2 == 0 else "pool", xb[:, kc, :], xst[:, kc, :], r=["wst0"], w=["xb%d" % kc])
        rope_tables(P, pos, t0, TT, inv[:, 0:1], rb, sign_ap=inv[:, 1:2])
        cs, sn = rb["cs"], rb["sn"]
        for m in range(3):
            bk = m % 2
            for kc in range(8):
                P.mm(C.psb[bk][:, :], wcqb[:, kc, m * 128:(m + 1) * 128], xb[:, kc, :], kc == 0, kc == 7,
                     r=["wcqb", "xb%d" % kc], w=["psb%d" % bk])
            P.copy("act", cqf[:, m, :], C.psb[bk][:, :], r=["psb%d" % bk], w=["cqf%d" % m])
        for m in range(2):
            bk = (m + 1) % 2
            for kc in range(8):
                P.mm(C.psb[bk][:, :], wckvb[:, kc, m * 128:(m + 1) * 128], xb[:, kc, :], kc == 0, kc == 7,
                     r=["wckvb", "xb%d" % kc], w=["psb%d" % bk])
            P.copy("act", ckvf[:, m, :], C.psb[bk][:, :], r=["psb%d" % bk], w=["ckvf%d" % m])
        for j in range(2):
            for kc in range(8):
                P.mm(C.psb[j][0:64, :], wkpeb[:, kc, j * 64:(j + 1) * 64], xb[:, kc, :], kc == 0, kc == 7,
                     r=["wkpeb", "xb%d" % kc], w=["psb%d" % j])
        P.tt("dve", ra[:, :], C.psb[0][0:64, :], cs[:, :], ALU.mult, r=["psb0", "rt_cs"], w=["ra"])
        P.tt("dve", rbb[:, :], C.psb[1][0:64, :], sn[:, :], ALU.mult, r=["psb1", "rt_sn"], w=["rbb"])
        P.tt("pool", kpe[:, ts_], ra[:, :], rbb[:, :], ALU.add, r=["ra", "rbb"], w=["kpe"])
        rms_norm_tile(C, cqf, "cqf", qng, "qng", cqn, "cqn", 3, RMS_EPS, 0)
        rms_norm_tile(C, ckvf, "ckvf", kvng, "kvng", ckvn, "ckvn", 2, RMS_EPS, 1)
        for h in range(2):
            for kc in range(2):
                P.mm(C.psb[h][:, :], wukb[:, kc, h * 128:(h + 1) * 128], ckvn[:, kc, :], kc == 0, kc == 1,
                     r=["wukb", "ckvn%d" % kc], w=["psb%d" % h])
            P.copy("act", knT[:, h, ts_], C.psb[h][:, :], r=["psb%d" % h], w=["knT"])
        for i in range(4):
            bk = i % 2
            for kc in range(2):
                P.mm(C.psb[bk][:, 0:256], ckvn[:, kc, i * 128:(i + 1) * 128], wuvb[:, kc, :], kc == 0, kc == 1,
                     r=["wuvb", "ckvn%d" % kc], w=["psb%d" % bk])
            P.copy("act", vb[:, t * 4 + i, :], C.psb[bk][:, 0:256], r=["psb%d" % bk], w=["vb"])
        for h in range(2):
            for kc in range(3):
                P.mm(C.psb[0][:, :], wuqnb[:, kc, h * 128:(h + 1) * 128], cqn[:, kc, :], kc == 0, kc == 2,
                     r=["wuqnb", "cqn%d" % kc], w=["psb0"])
            P.act(qnb[:, h, :], C.psb[0][:, :], AF.Copy, r=["psb0"], w=["qnb%d" % h], scale=QSCALE)
            for j in range(2):
                for kc in range(3):
                    c0 = h * 128 + j * 64
                    P.mm(C.psb[1][0:64, :] if j == 0 else C.psb[0][0:64, :],
                         wuqrb[:, kc, c0:c0 + 64], cqn[:, kc, :], kc == 0, kc == 2,
                         r=["wuqrb", "cqn%d" % kc], w=["psb1" if j == 0 else "psb0"])
            P.tt("dve", ra[:, :], C.psb[1][0:64, :], cs[:, :], ALU.mult, r=["psb1", "rt_cs"], w=["ra"])
            P.tt("dve", rbb[:, :], C.psb[0][0:64, :], sn[:, :], ALU.mult, r=["psb0", "rt_sn"], w=["rbb"])
            P.stt(qpb[:, h, :], ra[:, :], 1.0, rbb[:, :], ALU.mult, ALU.add, r=["ra", "rbb"], w=["qpb%d" % h])
            P.ts("pool", qpb[:, h, :], qpb[:, h, :], QSCALE, None, ALU.mult, r=["qpb%d" % h], w=["qpb%d" % h])
        nkb = 4 * t + 4
        for h in range(2):
            po, kpo = C.psb[4 + 2 * h], "psb%d" % (4 + 2 * h)
            pl, kpl = C.psb[5 + 2 * h], "psb%d" % (5 + 2 * h)
            for kb in range(nkb):
                ks_ = slice(kb * 128, (kb + 1) * 128)
                bs_ = 2 + (pti % 2)
                pT, kpt = pTs[pti % 2], "pT%d" % (pti % 2)
                pti += 1
                P.mm(C.psb[bs_][:, :], knT[:, h, ks_], qnb[:, h, :], True, False, r=["knT", "qnb%d" % h], w=["psb%d" % bs_])
                P.mm(C.psb[bs_][:, :], kpe[:, ks_], qpb[:, h, :], False, True, r=["kpe", "qpb%d" % h], w=["psb%d" % bs_])
                P.act(pT[:, :], C.psb[bs_][:, :], AF.Exp, r=["psb%d" % bs_], w=[kpt])
                if kb >= 4 * t:
                    P.tt("pool", pT[:, :], pT[:, :], maskb[:, kb - 4 * t, :], ALU.mult, r=[kpt, "maskb"], w=[kpt])
                P.mm(po[:, :], vb[:, kb, h * 128:(h + 1) * 128], pT[:, :], kb == 0, kb == nkb - 1, r=["vb", kpt], w=[kpo])
                P.mm(pl[:, :], onesb[:, :], pT[:, :], kb == 0, kb == nkb - 1, r=["onesb", kpt], w=[kpl])
            P.emit("dve", lambda e, pl=pl: e.reciprocal(rl[:, :], pl[:, :]), r=[kpl], w=["rl"])
            P.tt("dve", ob[:, h, :], po[:, :], rl[:, :], ALU.mult, r=[kpo, "rl"], w=["ob%d" % h])
        P.dma(ydst(t), ob[:, :, :], r=["ob0", "ob1"], is_out=True)
    P.finalize()


def mla_consts():
    inv32 = (10000.0 ** (-np.arange(32, dtype=np.float32) / np.float32(32))).astype(np.float32)
    inv = np.zeros((64, 2), np.float32)
    inv[:, 0] = np.concatenate([inv32, inv32])
    inv[:, 1] = np.concatenate([-np.ones(32), np.ones(32)])
    j = np.arange(128)[:, None, None]
    d = np.arange(4)[None, :, None]
    i = np.arange(512)[None, None, :]
    mask = (d * 128 + j <= i).astype(np.float32).reshape(128, 2048)
    return dict(inv=inv, mask=np.ascontiguousarray(mask))


GROUPS = [[0, 1, 2, 3], [4, 5, 6, 7]]
LRU_BLOCKS_OF = [[0, 1, 2], [3, 4, 5], [6, 7, 0], [8, 9, 0]]
LRU_VALID = [3, 3, 2, 2]
RET_W = (("wq", [D, 256]), ("wk", [D, 256]), ("wv", [D, 512]), ("wg", [D, 512]), ("gng", [128, 4]))
RET_C = (("inv", [128, 1]), ("maskT", [128, 128]), ("qdec", [128, 512]), ("kdec", [128, 1]), ("cdec", [128, 1]))
LRU_W = (("wgt", [D, 384]), ("wrc", [D, 384]), ("wa", [128, NBLK, 128]), ("wx", [128, NBLK, 128]), ("vec", [128, NBLK, 8]))
MLA_W = (("wcq", [D, 384]), ("wckv", [D, 256]), ("wkpe", [D, 128]), ("wuqn", [384, 256]), ("wuqr", [384, 256]),
         ("wuk", [256, 256]), ("wuv", [256, 256]), ("qng", [128, 3]), ("kvng", [128, 2]), ("inv", [64, 2]),
         ("mask", [128, 4 * 512]))
POST_F = [2048, 1280, 1024, 2048]


def emit_allgather(nc, src, dst, rows):
    n = src.shape[0] // rows
    assert n * rows == src.shape[0] and dst.shape[0] == 4 * src.shape[0]
    with nc.semaphore("ccs%d" % _PHASE_ID[0]) as cc, nc.Block() as block:
        _PHASE_ID[0] += 1

        @block.gpsimd
        def _(g):
            g.sem_clear(cc)
            for k in range(n):
                g.collective_compute("AllGather", ALU.bypass, replica_groups=GROUPS,
                                     ins=[src[k * rows:(k + 1) * rows, :]],
                                     outs=[dst[k * 4 * rows:(k + 1) * 4 * rows, :]]).then_inc(cc, 1)
            g.wait_ge(cc, n)


def emit_copy_phase(nc, pairs):
    P = Prog(nc)
    for o, i in pairs:
        P.dma(o, i, is_out=True)
    P.finalize()


def build_fused(nlayers=DEPTH):
    nc = bass.Bass("TRN2", target_bir_lowering=False)
    ext = lambda n, shp, dt=F32: nc.dram_tensor(n, shp, dt, kind="ExternalInput")
    xT = ext("xT", [D, SEG]).ap()
    pos = ext("pos", [1, S], I32)
    Wm = []
    for i in range(DEPTH):
        kind = i % 3
        W = {}
        if kind == 0:
            for n, shp in RET_W:
                W[n] = ext("L%d_%s" % (i, n), shp).ap()
            for n, shp in RET_C:
                if i == 0:
                    W[n] = ext("retc_" + n, shp).ap()
                else:
                    W[n] = Wm[0][n]
        elif kind == 1:
            for n, shp in LRU_W:
                W[n] = ext("L%d_%s" % (i, n), shp).ap()
        else:
            for n, shp in MLA_W:
                W[n] = ext("L%d_%s" % (i, n), shp).ap()
        W["pos"] = pos
        Wm.append(W)
    Wp = []
    for i in range(DEPTH):
        W = {}
        for n, shp in (("wo", [len(_post_ychunks(i)[0]) * 128, D]), ("w1", [D, 4 * D]), ("w2", [4 * D, D]),
                       ("lng", [128, 2, 8]), ("lnb", [128, 2, 8])):
            W[n] = ext("P%d_%s" % (i, n), shp).ap()
        Wp.append(W)
    oT = nc.dram_tensor("oT", [D, SEG], F32, kind="ExternalOutput").ap()
    hbd = nc.dram_tensor("hbd", [D, SEG], BF16).ap()
    hfull = nc.dram_tensor("hfull", [4 * D, SEG], BF16).ap()
    FCORE = [512, 384, 256, 512]
    yd = [nc.dram_tensor("yd%d" % i, [4 * FCORE[i], SEG], BF16).ap() for i in range(DEPTH)]
    ya = [nc.dram_tensor("ya%d" % i, [16 * FCORE[i], SEG], BF16).ap() for i in range(DEPTH)]
    ym = [nc.dram_tensor("ym%d" % i, [4 * FCORE[i], SEG], BF16).ap() for i in range(DEPTH)]
    rspill = nc.dram_tensor("rspill", [D, SEG], F32).ap()
    hbv = hbd.rearrange("(c p) t -> p c t", p=128)
    ov = oT.rearrange("(c p) t -> p c t", p=128)
    xv = xT.rearrange("(c p) t -> p c t", p=128)
    rsv = rspill.rearrange("(c p) t -> p c t", p=128)

    hfv = hfull.rearrange("(k r j p) t -> r p k j t", k=4, r=4, j=2, p=128)

    def hsrc(t):
        r, cl = t // 4, (t % 4) * TT
        return hfv[r][:, :, :, cl:cl + TT]

    def run_token_phases(stack_layers):
        pass

    def alloc_resid():
        st = ExitStack()
        return st, st.enter_context(nc.sbuf_tensor("resid%d" % _PHASE_ID[0], [128, 8, SEG], F32))

    rstack, resid = alloc_resid()
    P = Prog(nc)
    xb0 = P.sb("xb0", [128, 8, TT], BF16)
    for t in range(SEG // TT):
        sl = slice(t * TT, (t + 1) * TT)
        P.dma(resid[:, :, sl], xv[:, :, sl], w=["r%d" % t])
        P.copy("dve", xb0[:, :, :], resid[:, :, sl], r=["r%d" % t], w=["xb0"])
        P.dma(hbv[:, :, sl], xb0[:, :, :], r=["xb0"], is_out=True)
    P.finalize()
    for i in range(nlayers):
        kind = i % 3
        emit_allgather(nc, hbd, hfull, 256)
        ydv = yd[i].rearrange("(s c p) t -> s p c t", s=4, p=128)
        ydst = lambda t, ydv=ydv: ydv[t // 4][:, :, (t % 4) * TT:(t % 4 + 1) * TT]
        if kind == 2:
            emit_copy_phase(nc, [(rsv[:, :, :], resid[:, :, :])])
            rstack.close()
        if kind == 0:
            emit_ret(nc, Wm[i], hsrc, ydst, S, True, True)
        elif kind == 1:
            emit_lru(nc, Wm[i], hsrc, ydst, S, True, True)
        else:
            emit_mla(nc, Wm[i], hsrc, ydst, S, True, True)
        if kind == 2:
            rstack, resid = alloc_resid()
            emit_copy_phase(nc, [(resid[:, :, :], rsv[:, :, :])])
        emit_allgather(nc, yd[i], ya[i], FCORE[i] // 2)
        emit_select_seg(nc, ya[i], ym[i])
        yav = ym[i].rearrange("(c p) t -> p c t", p=128)
        ychunks, FCY = _post_ychunks(i)
        last = i == nlayers - 1
        emit_post(nc, Wp[i], ychunks, FCY,
                  ysrc=lambda e, cache, t, c0, n, yav=yav: yav[:, c0:c0 + n, t * TT:(t + 1) * TT],
                  y_bf16=True, resid=resid, rsrc=None,
                  odst=(lambda t: ov[:, :, t * TT:(t + 1) * TT]) if last else None,
                  hdst=None if last else (lambda t: hbv[:, :, t * TT:(t + 1) * TT]), ntok=SEG)
    rstack.close()
    return nc


def emit_select_seg(nc, ya2d, ym2d):
    src = ya2d.rearrange("(k r f) t -> k r (f t)", k=8, r=4)
    dst = ym2d.rearrange("(r kk f) t -> kk r (f t)", r=4, kk=2)
    src2 = ya2d.rearrange("(k rf) t -> k (rf t)", k=8)
    with nc.semaphore("sel%d" % _PHASE_ID[0]) as sem, nc.Block() as block:
        _PHASE_ID[0] += 1

        @block.sync
        def _(e):
            e.sem_clear(sem)
            seg2 = e.snap((e.partition_id() % 4) * 2)
            for kk in range(2):
                e.dma_start(out=dst[kk], in_=src2[bass.ds(seg2 + kk, 1), :].rearrange("o (r n) -> (o r) n", r=4)
                            ).then_inc(sem, 16)
            e.wait_ge(sem, 32)


def _post_ychunks(i):
    kind = i % 3
    if kind == 0:
        return list(range(16)), 16
    if kind == 2:
        return list(range(8)), 8
    m = {}
    for g in range(4):
        for k in range(LRU_VALID[g]):
            m[LRU_BLOCKS_OF[g][k]] = g * 3 + k
    return [m[k] for k in range(10)], 12


_NC = {}


def _c(a):
    return np.ascontiguousarray(a, dtype=np.float32)


def _ret_inputs(w_in, gn_g, hd):
    return dict(wq=_c(w_in[:, hd * 256:(hd + 1) * 256]), wk=_c(w_in[:, 1024 + hd * 256:1024 + (hd + 1) * 256]),
                wv=_c(w_in[:, 2048 + hd * 512:2048 + (hd + 1) * 512]), wg=_c(w_in[:, 4096 + hd * 512:4096 + (hd + 1) * 512]),
                gng=_c(gn_g[hd * 512:(hd + 1) * 512].reshape(4, 128).T))


def _lru_inputs(w_in, conv_w, conv_b, w_a, b_a, w_x, b_x, lam, g):
    bl = LRU_BLOCKS_OF[g]
    cols = np.concatenate([np.arange(k * 128, (k + 1) * 128) for k in bl])
    vec = np.stack([conv_w[0][cols], conv_w[1][cols], conv_w[2][cols], conv_w[3][cols], conv_b[cols],
                    b_a[cols], b_x[cols], lam[cols]], -1)
    return dict(wgt=_c(w_in[:, cols]), wrc=_c(w_in[:, 1280 + cols]),
                wa=_c(w_a[bl].transpose(1, 0, 2)), wx=_c(w_x[bl].transpose(1, 0, 2)),
                vec=_c(vec.reshape(3, 128, 8).transpose(1, 0, 2)))


def _mla_inputs(w_in, qn, kvn, w_uq, w_ukv, hp):
    heads = [2 * hp, 2 * hp + 1]
    sw = np.concatenate([np.arange(672, 704), np.arange(640, 672)])
    d = dict(wcq=_c(w_in[:, 0:384]), wckv=_c(w_in[:, 384:640]),
             wkpe=_c(np.concatenate([w_in[:, 640:704], w_in[:, sw]], 1)),
             wuqn=_c(np.concatenate([w_uq[:, k * 192:k * 192 + 128] for k in heads], 1)),
             wuqr=_c(np.concatenate([np.concatenate([w_uq[:, k * 192 + 128:k * 192 + 192], w_uq[:, k * 192 + 160:k * 192 + 192],
                                                     w_uq[:, k * 192 + 128:k * 192 + 160]], 1) for k in heads], 1)),
             wuk=_c(np.concatenate([w_ukv[:, k * 256:k * 256 + 128] for k in heads], 1)),
             wuv=_c(np.concatenate([w_ukv[:, k * 256 + 128:k * 256 + 256] for k in heads], 1)),
             qng=_c(qn.reshape(3, 128).T), kvng=_c(kvn.reshape(2, 128).T))
    d.update(mla_consts())
    return d


def _ln_arr(v):
    return _c(v.reshape(2, 8, 128).transpose(2, 0, 1))


def kernel(x, positions, ret_w_in, ret_gn_g, ret_w_o,
           lru_w_in, lru_conv_w, lru_conv_b, lru_w_a, lru_b_a, lru_w_x, lru_b_x, lru_lam, lru_w_o,
           mla_w_in, mla_q_norm, mla_kv_norm, mla_w_uq, mla_w_ukv, mla_w_o,
           ln_g, ln_b, mlp_w1, mlp_w2):
    A = np.asarray
    x = A(x, dtype=np.float32)
    positions = A(positions)
    if "nc" not in _NC:
        _NC["nc"] = build_fused()
    nc = _NC["nc"]
    xf = x.reshape(B * S, D)
    wos = [A(ret_w_o)[0], A(lru_w_o)[0], A(mla_w_o)[0], A(ret_w_o)[1]]
    shared = {}
    for i in range(DEPTH):
        shared["P%d_wo" % i] = _c(wos[i])
        shared["P%d_w1" % i] = _c(A(mlp_w1)[i])
        shared["P%d_w2" % i] = _c(A(mlp_w2)[i])
        shared["P%d_lng" % i] = _ln_arr(A(ln_g)[i])
        shared["P%d_lnb" % i] = _ln_arr(A(ln_b)[i])
    ins = []
    for c in range(NCORE):
        b, g = c // 4, c % 4
        d = dict(shared)
        d["xT"] = _c(xf[c * SEG:(c + 1) * SEG].T)
        d["pos"] = np.ascontiguousarray(positions[b:b + 1], dtype=np.int32)
        for i in range(DEPTH):
            kind, j = i % 3, i // 3
            if kind == 0:
                w = _ret_inputs(A(ret_w_in)[j], A(ret_gn_g)[j], g)
                if i == 0:
                    for k, v in ret_consts(g).items():
                        d["retc_" + k] = v
            elif kind == 1:
                w = _lru_inputs(A(lru_w_in)[j], A(lru_conv_w)[j], A(lru_conv_b)[j], A(lru_w_a)[j], A(lru_b_a)[j],
                                A(lru_w_x)[j], A(lru_b_x)[j], A(lru_lam)[j], g)
            else:
                w = _mla_inputs(A(mla_w_in)[j], A(mla_q_norm)[j], A(mla_kv_norm)[j], A(mla_w_uq)[j], A(mla_w_ukv)[j], g)
            for k, v in w.items():
                d["L%d_%s" % (i, k)] = v
        ins.append(d)
    res = run_bass_kernel_spmd(nc, ins, core_ids=list(range(NCORE))).results
    out = np.empty((B * S, D), np.float32)
    for c in range(NCORE):
        out[c * SEG:(c + 1) * SEG] = res[c]["oT"].T
    return out.reshape(B, S, D)
```

```python
import numpy as np
from contextlib import ExitStack
import concourse.bass as bass
import concourse.mybir as mybir
from concourse.bass_utils import run_bass_kernel_spmd

F32 = mybir.dt.float32
BF16 = mybir.dt.bfloat16
I32 = mybir.dt.int32
AF = mybir.ActivationFunctionType
ALU = mybir.AluOpType
AX = mybir.AxisListType

D = 1024
S = 8192
B = 2
DEPTH = 4
ALPHA = (2.0 * DEPTH) ** 0.25
LN_EPS = 1e-5
NCORE = 8
TT = 512
SEG = 2048

ENGS = ("pe", "act", "dve", "pool", "sp")
SAME_ENG_SYNC = True


class Ins:
    __slots__ = ("eng", "fn", "deps", "sig", "sem", "val", "dma", "gidx")

    def __init__(self, eng, fn, dma, gidx):
        self.eng = eng
        self.fn = fn
        self.deps = set()
        self.sig = False
        self.sem = None
        self.val = 0
        self.dma = dma
        self.gidx = gidx


_PHASE_ID = [0]


class Prog:
    def __init__(self, nc, n_dma_sems=24):
        self.nc = nc
        _PHASE_ID[0] += 1
        self.pid = _PHASE_ID[0]
        self.all = []
        self.last_w = {}
        self.readers = {}
        self.stack = ExitStack()
        self.n_dma_sems = n_dma_sems
        self.out_dmas = []

    def sb(self, name, shape, dt):
        return self.stack.enter_context(self.nc.sbuf_tensor("%s_p%d" % (name, self.pid), list(shape), dt))

    def ps(self, name, shape, dt=F32):
        return self.stack.enter_context(self.nc.psum_tensor("%s_p%d" % (name, self.pid), list(shape), dt))

    def emit(self, eng, fn, r=(), w=(), dma=False, out=False):
        ins = Ins(eng, fn, dma, len(self.all))
        deps = set()
        for k in r:
            lw = self.last_w.get(k)
            if lw is not None:
                deps.add(lw)
        for k in w:
            lw = self.last_w.get(k)
            if lw is not None:
                deps.add(lw)
            for rd in self.readers.get(k, ()):
                deps.add(rd)
        for k in r:
            self.readers.setdefault(k, []).append(ins)
        for k in w:
            self.last_w[k] = ins
            self.readers[k] = []
        deps.discard(ins)
        for d in deps:
            if d.eng == eng and not d.dma:
                if eng == "pe" or eng == "sp" or not SAME_ENG_SYNC:
                    continue
            ins.deps.add(d)
        self.all.append(ins)
        if out:
            self.out_dmas.append(ins)
        return ins

    def dma(self, out, in_, r=(), w=(), eng="sp", is_out=False, slow=False):
        if slow:
            return self.emit(eng, lambda e: e.dma_start(out=out, in_=in_, allow_slow_non_contiguous=True),
                             r, w, dma=True, out=is_out)
        return self.emit(eng, lambda e: e.dma_start(out=out, in_=in_), r, w, dma=True, out=is_out)

    def mm(self, out, lhsT, rhs, start, stop, r=(), w=()):
        return self.emit("pe", lambda e: e.matmul(out, lhsT, rhs, start=start, stop=stop), r, w)

    def act(self, out, in_, func, r=(), w=(), bias=None, scale=None, eng="act"):
        kw = {}
        if bias is not None:
            kw["bias"] = bias
        if scale is not None:
            kw["scale"] = scale
        return self.emit("act", lambda e: e.activation(out, in_, func, **kw), r, w)

    def tt(self, eng, out, in0, in1, op, r=(), w=()):
        return self.emit(eng, lambda e: e.tensor_tensor(out, in0, in1, op), r, w)

    def ts(self, eng, out, in0, s1, s2, op0, op1=None, r=(), w=()):
        if op1 is None:
            return self.emit(eng, lambda e: e.tensor_scalar(out, in0, s1, None, op0), r, w)
        return self.emit(eng, lambda e: e.tensor_scalar(out, in0, s1, s2, op0, op1), r, w)

    def stt(self, out, in0, scalar, in1, op0, op1, r=(), w=()):
        return self.emit("dve", lambda e: e.scalar_tensor_tensor(out, in0, scalar, in1, op0, op1), r, w)

    def copy(self, eng, out, in_, r=(), w=()):
        if eng == "act":
            return self.emit("act", lambda e: e.activation(out, in_, AF.Copy), r, w)
        return self.emit(eng, lambda e: e.tensor_copy(out, in_), r, w)

    def memset(self, eng, ap, val, w=()):
        return self.emit(eng, lambda e: e.memset(ap, val), (), w)

    def finalize(self):
        nc = self.nc
        dma_prev = [None] * self.n_dma_sems
        di = 0
        for ins in self.all:
            if ins.dma:
                s = di % self.n_dma_sems
                di += 1
                ins.sem = ("dma", s)
                ins.sig = True
                if dma_prev[s] is not None:
                    ins.deps.add(dma_prev[s])
                dma_prev[s] = ins
        for ins in self.all:
            for d in ins.deps:
                d.sig = True
        cnt = {}
        for ins in self.all:
            if not ins.sig:
                continue
            if not ins.dma:
                ins.sem = ("eng", ins.eng)
            c = cnt.get(ins.sem, 0) + (16 if ins.dma else 1)
            cnt[ins.sem] = c
            ins.val = c
        sems = {}
        for key in sorted(cnt, key=str):
            sems[key] = self.stack.enter_context(nc.semaphore("s%d_%s_%s" % ((self.pid,) + key)))
        per = {e: [i for i in self.all if i.eng == e] for e in ENGS}
        final = list(self.out_dmas)

        def run(engobj, lst, extra_final=()):
            have = {}
            for ins in lst:
                need = {}
                for d in ins.deps:
                    if need.get(d.sem, 0) < d.val:
                        need[d.sem] = d.val
                for sk, v in need.items():
                    if have.get(sk, 0) < v:
                        engobj.wait_ge(sems[sk], v)
                        have[sk] = v
                bi = ins.fn(engobj)
                if ins.sig:
                    bi.then_inc(sems[ins.sem], 16 if ins.dma else 1)
            for d in extra_final:
                if have.get(d.sem, 0) < d.val:
                    engobj.wait_ge(sems[d.sem], d.val)
                    have[d.sem] = d.val

        with nc.Block() as block:
            @block.gpsimd
            def _(e):
                for s in sems.values():
                    e.sem_clear(s)

        with nc.Block() as block:
            @block.sync
            def _(e):
                run(e, per["sp"], final)

            if per["pe"]:
                @block.tensor
                def _(e):
                    run(e, per["pe"])
            if per["act"]:
                @block.scalar
                def _(e):
                    run(e, per["act"])
            if per["dve"]:
                @block.vector
                def _(e):
                    run(e, per["dve"])
            if per["pool"]:
                @block.gpsimd
                def _(e):
                    run(e, per["pool"])
        self.stack.close()


class Ctx:
    def __init__(self, P, nbanks=8, ln_banks=(6, 7), bf_bank=False, n_wst=2, n_wbf=2):
        self.P = P
        self.psb = [P.ps("psb%d" % i, [128, 512]) for i in range(nbanks)]
        self.pst = P.ps("pst", [128, 1024], BF16) if bf_bank else None
        self.ln_banks = ln_banks
        self.wst = [P.sb("wst%d" % i, [128, 8, 512], F32) for i in range(n_wst)]
        self.wbf = [P.sb("wbf%d" % i, [128, 8, 512], BF16) for i in range(n_wbf)]
        self.wi = 0
        self.ones = P.sb("ones_f", [128, 128], F32)
        P.memset("pool", self.ones[:, :], 1.0, w=["ones_f"])
        self.sq = [P.sb("sq%d" % i, [128, 512], F32) for i in range(2)]
        self.sqi = 0
        self.st = {n: P.sb("st_" + n, [128, 512], F32) for n in ("mean", "msq", "var", "rstd", "nmr")}
        self.tmp = [P.sb("tmp%d" % i, [128, 512], F32) for i in range(2)]
        self.tmpi = 0

    def load_wblock(self, wdram_pkn, kc0, nkc, c0, ncol=512):
        P = self.P
        i = self.wi % 2
        self.wi += 1
        st, bf = self.wst[i], self.wbf[i]
        ks, kb = "wst%d" % i, "wbf%d" % i
        P.dma(st[:, 0:nkc, 0:ncol], wdram_pkn[:, kc0:kc0 + nkc, c0:c0 + ncol], w=[ks])
        a = (nkc * 3 + 7) // 8
        b = (nkc * 6 + 7) // 8
        for eng, lo, hi, sfx in (("pool", 0, a, "a"), ("act", a, b, "b"), ("dve", b, nkc, "c")):
            if hi > lo:
                P.copy(eng, bf[:, lo:hi, 0:ncol], st[:, lo:hi, 0:ncol], r=[ks], w=[kb + sfx])
        return bf, (lambda kc: kb + ("a" if kc < a else ("b" if kc < b else "c")))

    def load_resident(self, wdram_pkn, dst, dkey, nkc, ncol):
        P = self.P
        i = self.wi % len(self.wst)
        self.wi += 1
        st = self.wst[i]
        ks = "wst%d" % i
        P.dma(st[:, 0:nkc, 0:ncol], wdram_pkn[:, 0:nkc, 0:ncol], w=[ks])
        P.copy("pool", dst[:, 0:nkc, 0:ncol], st[:, 0:nkc, 0:ncol], r=[ks], w=[dkey])


def layer_norm_tile(C, z, zkey, g_ap, b_ap, out_f, okey_f, out_b, okey_b, nch=8, T=TT, eps=LN_EPS, gkey="lnp"):
    P = C.P
    nfeat = float(nch * 128)
    ps1, k1 = C.psb[C.ln_banks[0]], "psb%d" % C.ln_banks[0]
    ps2, k2 = C.psb[C.ln_banks[1]], "psb%d" % C.ln_banks[1]
    for c in range(nch):
        sq = C.sq[C.sqi % 2]
        ksq = "sq%d" % (C.sqi % 2)
        C.sqi += 1
        P.act(sq[:, 0:T], z[:, c, 0:T], AF.Square, r=[zkey + str(c)], w=[ksq])
        P.mm(ps1[:, 0:T], C.ones[:, :], z[:, c, 0:T], c == 0, c == nch - 1, r=["ones_f", zkey + str(c)], w=[k1])
        P.mm(ps2[:, 0:T], C.ones[:, :], sq[:, 0:T], c == 0, c == nch - 1, r=["ones_f", ksq], w=[k2])
    mean, msq, var, rstd, nmr = (C.st[n] for n in ("mean", "msq", "var", "rstd", "nmr"))
    P.act(mean[:, 0:T], ps1[:, 0:T], AF.Copy, r=[k1], w=["st_mean"], scale=1.0 / nfeat)
    P.tt("dve", msq[:, 0:T], mean[:, 0:T], mean[:, 0:T], ALU.mult, r=["st_mean"], w=["st_msq"])
    P.stt(var[:, 0:T], ps2[:, 0:T], 1.0 / nfeat, msq[:, 0:T], ALU.mult, ALU.subtract, r=[k2, "st_msq"], w=["st_var"])
    P.ts("dve", var[:, 0:T], var[:, 0:T], eps, None, ALU.add, r=["st_var"], w=["st_var"])
    P.act(rstd[:, 0:T], var[:, 0:T], AF.Ln, r=["st_var"], w=["st_rstd"])
    P.act(rstd[:, 0:T], rstd[:, 0:T], AF.Exp, r=["st_rstd"], w=["st_rstd"], scale=-0.5)
    P.stt(nmr[:, 0:T], mean[:, 0:T], -1.0, rstd[:, 0:T], ALU.mult, ALU.mult, r=["st_mean", "st_rstd"], w=["st_nmr"])
    for c in range(nch):
        t = C.tmp[C.tmpi % 2]
        kt = "tmp%d" % (C.tmpi % 2)
        C.tmpi += 1
        P.tt("dve", t[:, 0:T], z[:, c, 0:T], rstd[:, 0:T], ALU.mult, r=[zkey + str(c), "st_rstd"], w=[kt])
        P.tt("pool", t[:, 0:T], t[:, 0:T], nmr[:, 0:T], ALU.add, r=[kt, "st_nmr"], w=[kt])
        if b_ap is not None:
            P.act(out_f[:, c, 0:T], t[:, 0:T], AF.Identity, r=[kt, gkey], w=[okey_f + str(c)],
                  bias=b_ap[:, c:c + 1], scale=g_ap[:, c:c + 1])
        else:
            P.act(out_f[:, c, 0:T], t[:, 0:T], AF.Copy, r=[kt, gkey], w=[okey_f + str(c)],
                  scale=g_ap[:, c:c + 1])
        if out_b is not None:
            P.copy("pool", out_b[:, c, 0:T], out_f[:, c, 0:T], r=[okey_f + str(c)], w=[okey_b + str(c)])


def build_post(F, ntok=SEG):
    nc = bass.Bass("TRN2", target_bir_lowering=False)
    FC = F // 128
    yT = nc.dram_tensor("yT", [F, ntok], F32, kind="ExternalInput").ap()
    rT = nc.dram_tensor("rT", [D, ntok], F32, kind="ExternalInput").ap()
    W = {}
    for n, shp in (("wo", [F, D]), ("w1", [D, 4 * D]), ("w2", [4 * D, D]), ("lng", [128, 2, 8]), ("lnb", [128, 2, 8])):
        W[n] = nc.dram_tensor(n, shp, F32, kind="ExternalInput").ap()
    oT = nc.dram_tensor("oT", [D, ntok], F32, kind="ExternalOutput").ap()
    yv = yT.rearrange("(c p) t -> p c t", p=128)
    rv = rT.rearrange("(c p) t -> p c t", p=128)
    ov = oT.rearrange("(c p) t -> p c t", p=128)
    emit_post(nc, W, list(range(FC)), FC,
              ysrc=lambda e, cache, t, c0, n: yv[:, c0:c0 + n, t * TT:(t + 1) * TT], y_bf16=False,
              resid=None, rsrc=lambda t: rv[:, :, t * TT:(t + 1) * TT],
              odst=lambda t: ov[:, :, t * TT:(t + 1) * TT], hdst=None, ntok=ntok)
    return nc


def emit_post(nc, W, ychunks, FCY, ysrc, y_bf16, resid, rsrc, odst, hdst, ntok=SEG):
    wo, w1, w2, lng, lnb = (W[k] for k in ("wo", "w1", "w2", "lng", "lnb"))
    FW = len(ychunks)
    P = Prog(nc)
    C = Ctx(P)
    wov = wo.rearrange("(c p) n -> p c n", p=128)
    w1v = w1.rearrange("(c p) n -> p c n", p=128)
    w2v = w2.rearrange("(c p) n -> p c n", p=128)
    lnp_g = P.sb("lnp_g", [128, 2, 8], F32)
    lnp_b = P.sb("lnp_b", [128, 2, 8], F32)
    P.dma(lnp_g[:, :, :], lng, w=["lnp"])
    P.dma(lnp_b[:, :, :], lnb, w=["lnp"])
    yst = None if y_bf16 else P.sb("yst", [128, 8, TT], F32)
    yb = P.sb("yb", [128, FCY, TT], BF16)
    zloc = None if resid is not None else P.sb("z", [128, 8, TT], F32)
    hb = P.sb("hb", [128, 8, TT], BF16)
    hid = P.sb("hid", [128, 32, TT], BF16)
    rl = [P.sb("rl%d" % i, [128, TT], F32) for i in range(2)]
    Z8 = ["z%d" % i for i in range(8)]
    ycache = {}
    for t in range(ntok // TT):
        t0 = t * TT
        for c0 in range(0, FCY, 8):
            n = min(8, FCY - c0)
            if y_bf16:
                P.emit("sp", lambda e, t=t, c0=c0, n=n: e.dma_start(out=yb[:, c0:c0 + n, :], in_=ysrc(e, ycache, t, c0, n)),
                       (), ["yb"], dma=True)
            else:
                P.emit("sp", lambda e, t=t, c0=c0, n=n: e.dma_start(out=yst[:, 0:n, :], in_=ysrc(e, ycache, t, c0, n)),
                       (), ["yst"], dma=True)
                P.copy("dve", yb[:, c0:c0 + n, :], yst[:, 0:n, :], r=["yst"], w=["yb"])
        if resid is None:
            z = zloc
            P.dma(z[:, :, :], rsrc(t), w=Z8, eng="pool")
        else:
            z = resid[:, :, t0:t0 + TT]
        blocks = [(k0, min(8, FW - k0)) for k0 in range(0, FW, 8)]
        for half in range(2):
            for bi, (k0, n) in enumerate(blocks):
                wb, kb = C.load_wblock(wov, k0, n, half * 512)
                for m in range(4):
                    for kc in range(n):
                        P.mm(C.psb[m][:, :], wb[:, kc, m * 128:(m + 1) * 128], yb[:, ychunks[k0 + kc], :],
                             bi == 0 and kc == 0, bi == len(blocks) - 1 and kc == n - 1,
                             r=[kb(kc), "yb"], w=["psb%d" % m])
            for m in range(4):
                c = half * 4 + m
                P.stt(z[:, c, :], z[:, c, :], ALPHA, C.psb[m][:, :], ALU.mult, ALU.add,
                      r=["z%d" % c, "psb%d" % m], w=["z%d" % c])
        layer_norm_tile(C, z, "z", lnp_g[:, 0, :], lnp_b[:, 0, :], z, "z", hb, "hb")
        for nb in range(8):
            wb, kb = C.load_wblock(w1v, 0, 8, nb * 512)
            for m in range(4):
                pb = C.psb[m]
                kp = "psb%d" % m
                for kc in range(8):
                    P.mm(pb[:, :], wb[:, kc, m * 128:(m + 1) * 128], hb[:, kc, :], kc == 0, kc == 7,
                         r=[kb(kc), "hb%d" % kc], w=[kp])
                rr = rl[m % 2]
                kr = "rl%d" % (m % 2)
                P.act(rr[:, :], pb[:, :], AF.Relu, r=[kp], w=[kr])
                P.tt("dve", hid[:, nb * 4 + m, :], rr[:, :], rr[:, :], ALU.mult, r=[kr], w=["hid"])
        for half in range(2):
            for kb4 in range(4):
                wb, kb = C.load_wblock(w2v, kb4 * 8, 8, half * 512)
                for m in range(4):
                    for kc in range(8):
                        P.mm(C.psb[m][:, :], wb[:, kc, m * 128:(m + 1) * 128], hid[:, kb4 * 8 + kc, :],
                             kb4 == 0 and kc == 0, kb4 == 3 and kc == 7, r=[kb(kc), "hid"], w=["psb%d" % m])
            for m in range(4):
                c = half * 4 + m
                P.stt(z[:, c, :], z[:, c, :], ALPHA, C.psb[m][:, :], ALU.mult, ALU.add,
                      r=["z%d" % c, "psb%d" % m], w=["z%d" % c])
        layer_norm_tile(C, z, "z", lnp_g[:, 1, :], lnp_b[:, 1, :], z, "z", hb if hdst is not None else None, "hb")
        if odst is not None:
            P.dma(odst(t), z[:, :, :], r=Z8, is_out=True)
        if hdst is not None:
            P.dma(hdst(t), hb[:, :, :], r=["hb%d" % i for i in range(8)], is_out=True)
    P.finalize()


def rope_tables(P, pos_dram, t0, T, inv_ap, bufs, sign_ap=None):
    import math
    npart = bufs["np"]
    posi, a, b, ki, cs, sn = (bufs[k] for k in ("posi", "a", "b", "ki", "cs", "sn"))
    sl = slice(0, npart)
    P.dma(posi[sl, 0:T], bass.AP(pos_dram, t0, [[0, npart], [1, T]]), w=["rt_posi"])
    P.copy("dve", a[sl, 0:T], posi[sl, 0:T], r=["rt_posi"], w=["rt_a"])
    P.ts("dve", a[sl, 0:T], a[sl, 0:T], inv_ap, None, ALU.mult, r=["rt_a", "rt_inv"], w=["rt_a"])
    P.ts("dve", b[sl, 0:T], a[sl, 0:T], 1.0 / (2 * math.pi), None, ALU.mult, r=["rt_a"], w=["rt_b"])
    P.copy("dve", ki[sl, 0:T], b[sl, 0:T], r=["rt_b"], w=["rt_ki"])
    P.copy("dve", b[sl, 0:T], ki[sl, 0:T], r=["rt_ki"], w=["rt_b"])
    c1 = float(np.float32(6.28125))
    c2 = float(2 * math.pi - 6.28125)
    P.stt(a[sl, 0:T], b[sl, 0:T], -c1, a[sl, 0:T], ALU.mult, ALU.add, r=["rt_a", "rt_b"], w=["rt_a"])
    P.stt(a[sl, 0:T], b[sl, 0:T], -c2, a[sl, 0:T], ALU.mult, ALU.add, r=["rt_a", "rt_b"], w=["rt_a"])
    P.ts("dve", b[sl, 0:T], a[sl, 0:T], math.pi / 2, None, ALU.add, r=["rt_a"], w=["rt_b"])
    P.ts("dve", cs[sl, 0:T], b[sl, 0:T], math.pi, None, ALU.is_gt, r=["rt_b"], w=["rt_cs"])
    P.stt(b[sl, 0:T], cs[sl, 0:T], -2 * math.pi, b[sl, 0:T], ALU.mult, ALU.add, r=["rt_b", "rt_cs"], w=["rt_b"])
    lim = 3.14159
    P.ts("dve", b[sl, 0:T], b[sl, 0:T], lim, -lim, ALU.min, ALU.max, r=["rt_b"], w=["rt_b"])
    P.ts("dve", a[sl, 0:T], a[sl, 0:T], lim, -lim, ALU.min, ALU.max, r=["rt_a"], w=["rt_a"])
    P.act(cs[sl, 0:T], b[sl, 0:T], AF.Sin, r=["rt_b"], w=["rt_cs"])
    if sign_ap is None:
        P.act(sn[sl, 0:T], a[sl, 0:T], AF.Sin, r=["rt_a"], w=["rt_sn"])
    else:
        P.act(sn[sl, 0:T], a[sl, 0:T], AF.Sin, r=["rt_a", "rt_inv"], w=["rt_sn"], scale=sign_ap)


def make_identity_bf(P, idb, key):
    P.memset("pool", idb[:, :], 0.0, w=[key])
    P.emit("pool", lambda e: e.affine_select(idb[:, :], idb[:, :], [[-1, 128]], ALU.not_equal, 1.0,
                                             base=0, channel_multiplier=1), r=[key], w=[key])


def build_ret(nseq=S):
    nc = bass.Bass("TRN2", target_bir_lowering=False)
    W = {}
    hT = nc.dram_tensor("hT", [D, nseq], F32, kind="ExternalInput").ap()
    for n, shp in (("wq", [D, 256]), ("wk", [D, 256]), ("wv", [D, 512]), ("wg", [D, 512]), ("gng", [128, 4]),
                   ("inv", [128, 1]), ("maskT", [128, 2048]), ("qdec", [128, 512]), ("kdec", [128, 4]), ("cdec", [128, 1])):
        W[n] = nc.dram_tensor(n, shp, F32, kind="ExternalInput").ap()
    W["pos"] = nc.dram_tensor("pos", [1, nseq], I32, kind="ExternalInput")
    yT = nc.dram_tensor("yT", [512, nseq], F32, kind="ExternalOutput").ap()
    hv = hT.rearrange("(c p) t -> p c t", p=128)
    yv = yT.rearrange("(c p) t -> p c t", p=128)
    emit_ret(nc, W, lambda t: hv[:, :, t * TT:(t + 1) * TT], lambda t: yv[:, :, t * TT:(t + 1) * TT], nseq, False, False)
    return nc


def emit_ret(nc, W, hsrc, ydst, nseq, in_bf16, out_bf16):
    wq, wk, wv, wg, gng_d, inv_d, maskT_d, qdec_d, kdec_d, cdec_d, pos = (
        W[k] for k in ("wq", "wk", "wv", "wg", "gng", "inv", "maskT", "qdec", "kdec", "cdec", "pos"))
    P = Prog(nc)
    C = Ctx(P, nbanks=7, ln_banks=(4, 5), bf_bank=True, n_wst=1, n_wbf=0)
    pk = lambda w: w.rearrange("(c p) n -> p c n", p=128)
    wqb = P.sb("wqb", [128, 8, 256], BF16)
    wkb = P.sb("wkb", [128, 8, 256], BF16)
    wvb = P.sb("wvb", [128, 8, 512], BF16)
    wgb = P.sb("wgb", [128, 8, 512], BF16)
    C.load_resident(pk(wq), wqb, "wqb", 8, 256)
    C.load_resident(pk(wk), wkb, "wkb", 8, 256)
    C.load_resident(pk(wv), wvb, "wvb", 8, 512)
    C.load_resident(pk(wg), wgb, "wgb", 8, 512)
    gng = P.sb("gng_s", [128, 4], F32)
    inv = P.sb("inv_s", [128, 1], F32)
    maskT = P.sb("maskT_s", [128, 4, 512], F32)
    qdec = P.sb("qdec_s", [128, 512], F32)
    kdec = P.sb("kdec_s", [128, 4], F32)
    cdec = P.sb("cdec_s", [128, 1], F32)
    P.dma(gng[:, :], gng_d, w=["gng"])
    P.dma(inv[:, :], inv_d, w=["rt_inv"])
    P.dma(maskT[:, :, :], maskT_d.rearrange("p (k t) -> p k t", k=4), w=["maskT"])
    P.dma(qdec[:, :], qdec_d, w=["qdec"])
    P.dma(kdec[:, :], kdec_d, w=["kdec"])
    P.dma(cdec[:, :], cdec_d, w=["cdec"])
    idb = P.sb("idb", [128, 128], BF16)
    make_identity_bf(P, idb, "idb")
    rb = {"np": 128, "posi": P.sb("rt_posi", [128, TT], I32), "a": P.sb("rt_a", [128, TT], F32),
          "b": P.sb("rt_b", [128, TT], F32), "ki": P.sb("rt_ki", [128, TT], I32),
          "cs": P.sb("rt_cs", [128, TT], F32), "sn": P.sb("rt_sn", [128, TT], F32)}
    csk = P.sb("csk", [128, TT], F32)
    snk = P.sb("snk", [128, TT], F32)
    xst = C.wst[0]
    xb = P.sb("xb", [128, 8, TT], BF16)
    yob = P.sb("yob", [128, 4, TT], BF16) if out_bf16 else None
    qf = P.sb("qf", [128, 2, TT], F32)
    qb = P.sb("qb", [128, 2, TT], BF16)
    qdb = P.sb("qdb", [128, 2, TT], BF16)
    kb = P.sb("kb", [128, 2, TT], BF16)
    kdT = P.sb("kdT", [128, 4, 256], BF16)
    vb = P.sb("vb", [128, 4, 512], BF16)
    sg = P.sb("sg", [128, 4, TT], F32)
    of = P.sb("of", [128, 4, TT], F32)
    ra = P.sb("ra", [128, TT], F32)
    rbb = P.sb("rbb", [128, TT], F32)
    pT = P.sb("pT", [128, 4, TT], BF16)
    st = P.sb("st", [128, 2, 512], F32)
    stb = P.sb("stb", [128, 2, 512], BF16)
    P.memset("pool", st[:, :, :], 0.0, w=["st0", "st1"])
    P.memset("pool", stb[:, :, :], 0.0, w=["stb0", "stb1"])
    X8 = ["xb%d" % i for i in range(8)]
    for t in range(nseq // TT):
        t0 = t * TT
        if in_bf16:
            for k in range(4):
                P.dma(xb[:, 2 * k:2 * k + 2, :], hsrc(t)[:, k, :, :], w=["xb%d" % (2 * k), "xb%d" % (2 * k + 1)])
        else:
            P.dma(xst[:, :, :], hsrc(t), w=["wst0"])
            for kc in range(8):
                P.copy("dve" if kc % 2 == 0 else "pool", xb[:, kc, :], xst[:, kc, :], r=["wst0"], w=["xb%d" % kc])
        rope_tables(P, pos, t0, TT, inv[:, 0:1], rb)
        cs, sn = rb["cs"], rb["sn"]
        P.ts("pool", csk[:, :], cs[:, :], 1.0 / 16.0, None, ALU.mult, r=["rt_cs"], w=["csk"])
        P.ts("pool", snk[:, :], sn[:, :], 1.0 / 16.0, None, ALU.mult, r=["rt_sn"], w=["snk"])
        for m in range(2):
            for kc in range(8):
                P.mm(C.psb[m][:, :], wqb[:, kc, m * 128:(m + 1) * 128], xb[:, kc, :], kc == 0, kc == 7,
                     r=["wqb", "xb%d" % kc], w=["psb%d" % m])
        P.tt("dve", ra[:, :], C.psb[0][:, :], cs[:, :], ALU.mult, r=["psb0", "rt_cs"], w=["ra"])
        P.tt("dve", rbb[:, :], C.psb[1][:, :], sn[:, :], ALU.mult, r=["psb1", "rt_sn"], w=["rbb"])
        P.tt("pool", qf[:, 0, :], ra[:, :], rbb[:, :], ALU.subtract, r=["ra", "rbb"], w=["qf0"])
        P.tt("dve", ra[:, :], C.psb[0][:, :], sn[:, :], ALU.mult, r=["psb0", "rt_sn"], w=["ra"])
        P.tt("dve", rbb[:, :], C.psb[1][:, :], cs[:, :], ALU.mult, r=["psb1", "rt_cs"], w=["rbb"])
        P.tt("pool", qf[:, 1, :], ra[:, :], rbb[:, :], ALU.add, r=["ra", "rbb"], w=["qf1"])
        for m in range(2):
            P.copy("act", qb[:, m, :], qf[:, m, :], r=["qf%d" % m], w=["qb%d" % m])
            P.tt("pool", qdb[:, m, :], qf[:, m, :], qdec[:, :], ALU.mult, r=["qf%d" % m, "qdec"], w=["qdb%d" % m])
        for m in range(2):
            for kc in range(8):
                P.mm(C.psb[2 + m][:, :], wkb[:, kc, m * 128:(m + 1) * 128], xb[:, kc, :], kc == 0, kc == 7,
                     r=["wkb", "xb%d" % kc], w=["psb%d" % (2 + m)])
        P.tt("dve", ra[:, :], C.psb[2][:, :], csk[:, :], ALU.mult, r=["psb2", "csk"], w=["ra"])
        P.tt("dve", rbb[:, :], C.psb[3][:, :], snk[:, :], ALU.mult, r=["psb3", "snk"], w=["rbb"])
        P.tt("pool", kb[:, 0, :], ra[:, :], rbb[:, :], ALU.subtract, r=["ra", "rbb"], w=["kb0"])
        P.tt("dve", ra[:, :], C.psb[2][:, :], snk[:, :], ALU.mult, r=["psb2", "snk"], w=["ra"])
        P.tt("dve", rbb[:, :], C.psb[3][:, :], csk[:, :], ALU.mult, r=["psb3", "csk"], w=["rbb"])
        P.tt("pool", kb[:, 1, :], ra[:, :], rbb[:, :], ALU.add, r=["ra", "rbb"], w=["kb1"])
        for m in range(4):
            bk = m % 4
            for kc in range(8):
                P.mm(C.psb[bk][:, :], wgb[:, kc, m * 128:(m + 1) * 128], xb[:, kc, :], kc == 0, kc == 7,
                     r=["wgb", "xb%d" % kc], w=["psb%d" % bk])
            P.act(sg[:, m, :], C.psb[bk][:, :], AF.Silu, r=["psb%d" % bk], w=["sg%d" % m])
        for i in range(4):
            bk = i % 4
            for kc in range(8):
                P.mm(C.psb[bk][:, :], xb[:, kc, i * 128:(i + 1) * 128], wvb[:, kc, :], kc == 0, kc == 7,
                     r=["wvb", "xb%d" % kc], w=["psb%d" % bk])
            P.copy("act", vb[:, i, :], C.psb[bk][:, :], r=["psb%d" % bk], w=["vb%d" % i])
        for i in range(4):
            for m in range(2):
                col = (i * 2 + m) * 128
                P.emit("pe", lambda e, col=col, m=m, i=i: e.transpose(C.pst[:, col:col + 128], kb[:, m, i * 128:(i + 1) * 128], idb[:, :]),
                       r=["kb%d" % m, "idb"], w=["pst"])
        for i in range(4):
            P.ts("dve", kdT[:, i, :], C.pst[:, i * 256:(i + 1) * 256], kdec[:, i:i + 1], None, ALU.mult,
                 r=["pst", "kdec"], w=["kdT%d" % i])
        for kblk in range(4):
            cl = slice(kblk * 128, (kblk + 1) * 128)
            qs = slice(kblk * 128, TT)
            bk = 4 + (kblk % 2)
            for m in range(2):
                P.mm(C.psb[bk][:, qs], kb[:, m, cl], qb[:, m, qs], m == 0, m == 1,
                     r=["kb%d" % m, "qb%d" % m], w=["psb%d" % bk])
            P.tt("dve", pT[:, kblk, qs], C.psb[bk][:, qs], maskT[:, kblk, qs], ALU.mult,
                 r=["psb%d" % bk, "maskT"], w=["pT%d" % kblk])
        for vc in range(4):
            vs = slice(vc * 128, (vc + 1) * 128)
            bk = 6 if vc % 2 == 0 else 3
            for m in range(2):
                P.mm(C.psb[bk][:, :], stb[:, m, vs], qdb[:, m, :], m == 0, False,
                     r=["stb%d" % m, "qdb%d" % m], w=["psb%d" % bk])
            for kblk in range(4):
                qs = slice(kblk * 128, TT)
                P.mm(C.psb[bk][:, qs], vb[:, kblk, vs], pT[:, kblk, qs], False, kblk == 3,
                     r=["vb%d" % kblk, "pT%d" % kblk], w=["psb%d" % bk])
            P.copy("act", of[:, vc, :], C.psb[bk][:, :], r=["psb%d" % bk], w=["of%d" % vc])
        for m in range(2):
            ds_ = slice(m * 128, (m + 1) * 128)
            bk = 4 + m
            for i in range(4):
                P.mm(C.psb[bk][:, :], kdT[:, i, ds_], vb[:, i, :], i == 0, i == 3, r=["kdT%d" % i, "vb%d" % i], w=["psb%d" % bk])
            P.stt(st[:, m, :], st[:, m, :], cdec[:, 0:1], C.psb[bk][:, :], ALU.mult, ALU.add,
                  r=["st%d" % m, "cdec", "psb%d" % bk], w=["st%d" % m])
            P.copy("pool", stb[:, m, :], st[:, m, :], r=["st%d" % m], w=["stb%d" % m])
        layer_norm_tile(C, of, "of", gng, None, of, "of", None, None, nch=4, gkey="gng")
        yo, ky = (yob, "yob") if out_bf16 else (of, "of")
        for c in range(4):
            P.tt("dve" if c % 2 == 0 else "pool", yo[:, c, :], of[:, c, :], sg[:, c, :], ALU.mult,
                 r=["of%d" % c, "sg%d" % c], w=[ky + str(c)])
        P.dma(ydst(t), yo[:, :, :], r=[ky + str(c) for c in range(4)], is_out=True)
    P.finalize()


def ret_consts(h):
    lg = np.log1p(-np.exp2(-5.0 - h))
    i = np.arange(512, dtype=np.float64)
    j = np.arange(128, dtype=np.float64)
    k = np.arange(4, dtype=np.float64)
    diff = i[None, None, :] - (128.0 * k[None, :, None] + j[:, None, None])
    maskT = np.where(diff >= 0, np.exp(lg * np.maximum(diff, 0.0)), 0.0).reshape(128, 2048)
    qdec = np.broadcast_to(np.exp(lg * (i + 1.0))[None, :], (128, 512))
    kdec = np.exp(lg * (511.0 - (128.0 * k[None, :] + j[:, None])))
    cdec = np.full((128, 1), np.exp(lg * 512.0))
    inv = (10000.0 ** (-np.arange(128, dtype=np.float32) / np.float32(128))).astype(np.float32)[:, None]
    f = lambda a: np.ascontiguousarray(a, dtype=np.float32)
    return dict(maskT=f(maskT), qdec=f(qdec), kdec=f(kdec), cdec=f(cdec), inv=f(inv))


NBLK = 3


def build_lru(nseq=S):
    nc = bass.Bass("TRN2", target_bir_lowering=False)
    W3 = NBLK * 128
    W = {}
    hT = nc.dram_tensor("hT", [D, nseq], F32, kind="ExternalInput").ap()
    for n, shp in (("wgt", [D, W3]), ("wrc", [D, W3]), ("wa", [128, NBLK, 128]), ("wx", [128, NBLK, 128]), ("vec", [128, NBLK, 8])):
        W[n] = nc.dram_tensor(n, shp, F32, kind="ExternalInput").ap()
    yT = nc.dram_tensor("yT", [W3, nseq], F32, kind="ExternalOutput").ap()
    hv = hT.rearrange("(c p) t -> p c t", p=128)
    yv = yT.rearrange("(c p) t -> p c t", p=128)
    emit_lru(nc, W, lambda t: hv[:, :, t * TT:(t + 1) * TT], lambda t: yv[:, :, t * TT:(t + 1) * TT], nseq, False, False)
    return nc


def emit_lru(nc, W, hsrc, ydst, nseq, in_bf16, out_bf16):
    W3 = NBLK * 128
    wgt, wrc, wa_d, wx_d, vec_d = (W[k] for k in ("wgt", "wrc", "wa", "wx", "vec"))
    P = Prog(nc)
    C = Ctx(P, nbanks=8, n_wst=1, n_wbf=0)
    pk = lambda w: w.rearrange("(c p) n -> p c n", p=128)
    wgb = P.sb("wgb", [128, 8, W3], BF16)
    wrb = P.sb("wrb", [128, 8, W3], BF16)
    C.load_resident(pk(wgt), wgb, "wgb", 8, W3)
    C.load_resident(pk(wrc), wrb, "wrb", 8, W3)
    waf = P.sb("waf", [128, NBLK, 128], F32)
    wxf = P.sb("wxf", [128, NBLK, 128], F32)
    wab = P.sb("wab", [128, NBLK, 128], BF16)
    wxb = P.sb("wxb", [128, NBLK, 128], BF16)
    P.dma(waf[:, :, :], wa_d, w=["waf"])
    P.dma(wxf[:, :, :], wx_d, w=["wxf"])
    P.copy("dve", wab[:, :, :], waf[:, :, :], r=["waf"], w=["wab"])
    P.copy("dve", wxb[:, :, :], wxf[:, :, :], r=["wxf"], w=["wxb"])
    vec = P.sb("vec_s", [128, NBLK, 8], F32)
    P.dma(vec[:, :, :], vec_d, w=["vec"])
    der = P.sb("der", [128, NBLK, 4], F32)
    P.act(der[:, :, 0], vec[:, :, 7], AF.Exp, r=["vec"], w=["der"], scale=-1.0)
    P.ts("dve", der[:, :, 0], der[:, :, 0], 1.0, None, ALU.add, r=["der"], w=["der"])
    P.act(der[:, :, 0], der[:, :, 0], AF.Ln, r=["der"], w=["der"])
    P.ts("dve", der[:, :, 1], der[:, :, 0], -8.0, None, ALU.mult, r=["der"], w=["der"])
    P.ts("dve", der[:, :, 2], der[:, :, 0], -16.0, None, ALU.mult, r=["der"], w=["der"])
    P.ts("dve", der[:, :, 3], der[:, :, 0], 8.0, None, ALU.mult, r=["der"], w=["der"])
    xst = C.wst[0]
    xb = P.sb("xb", [128, 8, TT], BF16)
    rec = P.sb("rec", [128, NBLK, TT + 4], F32)
    P.memset("pool", rec[:, :, :], 0.0, w=["rec%d" % b for b in range(NBLK)])
    hprev = P.sb("hprev", [128, NBLK], F32)
    P.memset("pool", hprev[:, :], 0.0, w=["hprev%d" % b for b in range(NBLK)])
    gt = P.sb("gt", [128, TT], F32)
    g2 = P.sb("g2", [128, TT], F32)
    u = P.sb("u", [128, TT], F32)
    ub = P.sb("ub", [128, TT], BF16)
    rr = P.sb("rr", [128, TT], F32)
    ii = P.sb("ii", [128, TT], F32)
    aa = P.sb("aa", [128, TT], F32)
    ee = P.sb("ee", [128, TT], F32)
    th = P.sb("th", [128, TT], F32)
    hs = P.sb("hs", [128, TT], F32)
    yo = P.sb("yo", [128, NBLK, TT], BF16 if out_bf16 else F32)
    for t in range(nseq // TT):
        t0 = t * TT
        if in_bf16:
            for k in range(4):
                P.dma(xb[:, 2 * k:2 * k + 2, :], hsrc(t)[:, k, :, :], w=["xb%d" % (2 * k), "xb%d" % (2 * k + 1)])
        else:
            P.dma(xst[:, :, :], hsrc(t), w=["wst0"])
            for kc in range(8):
                P.copy("dve" if kc % 2 == 0 else "pool", xb[:, kc, :], xst[:, kc, :], r=["wst0"], w=["xb%d" % kc])
        for b in range(NBLK):
            bs = slice(b * 128, (b + 1) * 128)
            kr = "rec%d" % b
            for kc in range(8):
                P.mm(C.psb[0][:, :], wgb[:, kc, bs], xb[:, kc, :], kc == 0, kc == 7, r=["wgb", "xb%d" % kc], w=["psb0"])
            P.act(g2[:, :], C.psb[0][:, :], AF.Square, r=["psb0"], w=["g2"])
            P.ts("dve", g2[:, :], g2[:, :], 0.044715, 1.0, ALU.mult, ALU.add, r=["g2"], w=["g2"])
            P.tt("dve", g2[:, :], g2[:, :], C.psb[0][:, :], ALU.mult, r=["g2", "psb0"], w=["g2"])
            P.act(g2[:, :], g2[:, :], AF.Sigmoid, r=["g2"], w=["g2"], scale=1.5957691216057308)
            P.tt("dve", gt[:, :], g2[:, :], C.psb[0][:, :], ALU.mult, r=["g2", "psb0"], w=["gt"])
            for kc in range(8):
                P.mm(C.psb[1][:, :], wrb[:, kc, bs], xb[:, kc, :], kc == 0, kc == 7, r=["wrb", "xb%d" % kc], w=["psb1"])
            P.copy("act", rec[:, b, 3:3 + TT], C.psb[1][:, :], r=["psb1"], w=[kr])
            P.ts("dve", u[:, :], rec[:, b, 3:3 + TT], vec[:, b, 3:4], vec[:, b, 4:5], ALU.mult, ALU.add, r=[kr, "vec"], w=["u"])
            for k in range(3):
                P.stt(u[:, :], rec[:, b, k:k + TT], vec[:, b, k:k + 1], u[:, :], ALU.mult, ALU.add, r=[kr, "vec", "u"], w=["u"])
            P.copy("pool", rec[:, b, 0:3], rec[:, b, TT:TT + 3], r=[kr], w=[kr])
            P.copy("pool", ub[:, :], u[:, :], r=["u"], w=["ub"])
            P.mm(C.psb[2][:, :], wab[:, b, :], ub[:, :], True, True, r=["wab", "ub"], w=["psb2"])
            P.mm(C.psb[3][:, :], wxb[:, b, :], ub[:, :], True, True, r=["wxb", "ub"], w=["psb3"])
            P.act(rr[:, :], C.psb[2][:, :], AF.Sigmoid, r=["psb2", "vec"], w=["rr"], bias=vec[:, b, 5:6])
            P.act(ii[:, :], C.psb[3][:, :], AF.Sigmoid, r=["psb3", "vec"], w=["ii"], bias=vec[:, b, 6:7])
            P.act(aa[:, :], rr[:, :], AF.Exp, r=["rr", "der"], w=["aa"], scale=der[:, b, 1:2])
            P.act(ee[:, :], rr[:, :], AF.Exp, r=["rr", "der"], w=["ee"], scale=der[:, b, 2:3])
            P.act(th[:, :], rr[:, :], AF.Tanh, r=["rr", "der"], w=["th"], scale=der[:, b, 3:4])
            P.stt(ee[:, :], ee[:, :], 1.0, th[:, :], ALU.add, ALU.mult, r=["ee", "th"], w=["ee"])
            P.act(ee[:, :], ee[:, :], AF.Sqrt, r=["ee"], w=["ee"])
            P.tt("pool", ii[:, :], ii[:, :], u[:, :], ALU.mult, r=["ii", "u"], w=["ii"])
            P.tt("pool", ii[:, :], ii[:, :], ee[:, :], ALU.mult, r=["ii", "ee"], w=["ii"])
            P.emit("dve", lambda e, b=b: e.tensor_tensor_scan(hs[:, :], aa[:, :], ii[:, :], hprev[:, b:b + 1], ALU.mult, ALU.add),
                   r=["aa", "ii", "hprev%d" % b], w=["hs"])
            P.copy("dve", hprev[:, b:b + 1], hs[:, TT - 1:TT], r=["hs"], w=["hprev%d" % b])
            P.tt("pool", yo[:, b, :], hs[:, :], gt[:, :], ALU.mult, r=["hs", "gt"], w=["yo%d" % b])
        P.dma(ydst(t), yo[:, :, :], r=["yo%d" % b for b in range(NBLK)], is_out=True)
    P.finalize()


def rms_norm_tile(C, z, zkey, g_ap, gkey, out_b, okey, nch, eps, bank, T=TT):
    P = C.P
    ps, kp = C.psb[bank], "psb%d" % bank
    for c in range(nch):
        sq = C.sq[C.sqi % 2]
        ksq = "sq%d" % (C.sqi % 2)
        C.sqi += 1
        P.act(sq[:, 0:T], z[:, c, 0:T], AF.Square, r=[zkey + str(c)], w=[ksq])
        P.mm(ps[:, 0:T], C.ones[:, :], sq[:, 0:T], c == 0, c == nch - 1, r=["ones_f", ksq], w=[kp])
    rstd = C.st["rstd"]
    P.ts("dve", rstd[:, 0:T], ps[:, 0:T], 1.0 / (nch * 128.0), eps, ALU.mult, ALU.add, r=[kp], w=["st_rstd"])
    P.act(rstd[:, 0:T], rstd[:, 0:T], AF.Ln, r=["st_rstd"], w=["st_rstd"])
    P.act(rstd[:, 0:T], rstd[:, 0:T], AF.Exp, r=["st_rstd"], w=["st_rstd"], scale=-0.5)
    for c in range(nch):
        t = C.tmp[C.tmpi % 2]
        kt = "tmp%d" % (C.tmpi % 2)
        C.tmpi += 1
        P.tt("dve", t[:, 0:T], z[:, c, 0:T], rstd[:, 0:T], ALU.mult, r=[zkey + str(c), "st_rstd"], w=[kt])
        P.ts("pool", out_b[:, c, 0:T], t[:, 0:T], g_ap[:, c:c + 1], None, ALU.mult, r=[kt, gkey], w=[okey + str(c)])


QSCALE = 192.0 ** -0.5
RMS_EPS = 1e-6


def build_mla(nseq=S):
    nc = bass.Bass("TRN2", target_bir_lowering=False)
    W = {}
    hT = nc.dram_tensor("hT", [D, nseq], F32, kind="ExternalInput").ap()
    for n, shp in (("wcq", [D, 384]), ("wckv", [D, 256]), ("wkpe", [D, 128]), ("wuqn", [384, 256]), ("wuqr", [384, 256]),
                   ("wuk", [256, 256]), ("wuv", [256, 256]), ("qng", [128, 3]), ("kvng", [128, 2]), ("inv", [64, 2]),
                   ("mask", [128, 4 * 512])):
        W[n] = nc.dram_tensor(n, shp, F32, kind="ExternalInput").ap()
    W["pos"] = nc.dram_tensor("pos", [1, nseq], I32, kind="ExternalInput")
    oT = nc.dram_tensor("oT", [256, nseq], F32, kind="ExternalOutput").ap()
    hv = hT.rearrange("(c p) t -> p c t", p=128)
    ov = oT.rearrange("(c p) t -> p c t", p=128)
    emit_mla(nc, W, lambda t: hv[:, :, t * TT:(t + 1) * TT], lambda t: ov[:, :, t * TT:(t + 1) * TT], nseq, False, False)
    return nc


def emit_mla(nc, W, hsrc, ydst, nseq, in_bf16, out_bf16):
    NKB = nseq // 128
    wcq, wckv, wkpe, wuqn, wuqr, wuk, wuv, qng_d, kvng_d, inv_d, mask_d, pos = (
        W[k] for k in ("wcq", "wckv", "wkpe", "wuqn", "wuqr", "wuk", "wuv", "qng", "kvng", "inv", "mask", "pos"))
    P = Prog(nc)
    C = Ctx(P, nbanks=8, n_wst=1, n_wbf=0)
    pk = lambda w: w.rearrange("(c p) n -> p c n", p=128)
    wcqb = P.sb("wcqb", [128, 8, 384], BF16)
    wckvb = P.sb("wckvb", [128, 8, 256], BF16)
    wkpeb = P.sb("wkpeb", [128, 8, 128], BF16)
    wuqnb = P.sb("wuqnb", [128, 3, 256], BF16)
    wuqrb = P.sb("wuqrb", [128, 3, 256], BF16)
    wukb = P.sb("wukb", [128, 2, 256], BF16)
    wuvb = P.sb("wuvb", [128, 2, 256], BF16)
    C.load_resident(pk(wcq), wcqb, "wcqb", 8, 384)
    C.load_resident(pk(wckv), wckvb, "wckvb", 8, 256)
    C.load_resident(pk(wkpe), wkpeb, "wkpeb", 8, 128)
    C.load_resident(pk(wuqn), wuqnb, "wuqnb", 3, 256)
    C.load_resident(pk(wuqr), wuqrb, "wuqrb", 3, 256)
    C.load_resident(pk(wuk), wukb, "wukb", 2, 256)
    C.load_resident(pk(wuv), wuvb, "wuvb", 2, 256)
    qng = P.sb("qng_s", [128, 3], F32)
    kvng = P.sb("kvng_s", [128, 2], F32)
    inv = P.sb("inv_s", [64, 2], F32)
    P.dma(qng[:, :], qng_d, w=["qng"])
    P.dma(kvng[:, :], kvng_d, w=["kvng"])
    P.dma(inv[:, :], inv_d, w=["rt_inv"])
    maskb = P.sb("maskb", [128, 4, 512], BF16)
    C.load_resident(mask_d.rearrange("p (c n) -> p c n", c=4), maskb, "maskb", 4, 512)
    onesb = P.sb("onesb", [128, 128], BF16)
    P.memset("pool", onesb[:, :], 1.0, w=["onesb"])
    rb = {"np": 64, "posi": P.sb("rt_posi", [64, TT], I32), "a": P.sb("rt_a", [64, TT], F32),
          "b": P.sb("rt_b", [64, TT], F32), "ki": P.sb("rt_ki", [64, TT], I32),
          "cs": P.sb("rt_cs", [64, TT], F32), "sn": P.sb("rt_sn", [64, TT], F32)}
    knT = P.sb("knT", [128, 2, nseq], BF16)
    kpe = P.sb("kpe", [64, nseq], BF16)
    vb = P.sb("vb", [128, NKB, 256], BF16)
    xst = C.wst[0]
    xb = P.sb("xb", [128, 8, TT], BF16)
    cqf = P.sb("cqf", [128, 3, TT], F32)
    ckvf = P.sb("ckvf", [128, 2, TT], F32)
    cqn = P.sb("cqn", [128, 3, TT], BF16)
    ckvn = P.sb("ckvn", [128, 2, TT], BF16)
    ra = P.sb("ra", [64, TT], F32)
    rbb = P.sb("rbb", [64, TT], F32)
    qnb = P.sb("qnb", [128, 2, TT], BF16)
    qpb = P.sb("qpb", [64, 2, TT], BF16)
    pTs = [P.sb("pT%d" % i, [128, TT], BF16) for i in range(2)]
    rl = P.sb("rl", [128, TT], F32)
    ob = P.sb("ob", [128, 2, TT], BF16 if out_bf16 else F32)
    pti = 0
    for t in range(nseq // TT):
        t0 = t * TT
        ts_ = slice(t0, t0 + TT)
        if in_bf16:
            for k in range(4):
                P.dma(xb[:, 2 * k:2 * k + 2, :], hsrc(t)[:, k, :, :], w=["xb%d" % (2 * k), "xb%d" % (2 * k + 1)])
        else:
            P.dma(xst[:, :, :], hsrc(t), w=["wst0"])
            for kc in range(8):
                P.copy("dve" if kc % 2 == 0 else "pool", xb[:, kc, :], xst[:, kc, :], r=["wst0"], w=["xb%d" % kc])
        rope_tables(P, pos, t0, TT, inv[:, 0:1], rb, sign_ap=inv[:, 1:2])
        cs, sn = rb["cs"], rb["sn"]
        for m in range(3):
            bk = m % 2
            for kc in range(8):
                P.mm(C.psb[bk][:, :], wcqb[:, kc, m * 128:(m + 1) * 128], xb[:, kc, :], kc == 0, kc == 7,
                     r=["wcqb", "xb%d" % kc], w=["psb%d" % bk])
            P.copy("act", cqf[:, m, :], C.psb[bk][:, :], r=["psb%d" % bk], w=["cqf%d" % m])
        for m in range(2):
            bk = (m + 1) % 2
            for kc in range(8):
                P.mm(C.psb[bk][:, :], wckvb[:, kc, m * 128:(m + 1) * 128], xb[:, kc, :], kc == 0, kc == 7,
                     r=["wckvb", "xb%d" % kc], w=["psb%d" % bk])
            P.copy("act", ckvf[:, m, :], C.psb[bk][:, :], r=["psb%d" % bk], w=["ckvf%d" % m])
        for j in range(2):
            for kc in range(8):
                P.mm(C.psb[j][0:64, :], wkpeb[:, kc, j * 64:(j + 1) * 64], xb[:, kc, :], kc == 0, kc == 7,
                     r=["wkpeb", "xb%d" % kc], w=["psb%d" % j])
        P.tt("dve", ra[:, :], C.psb[0][0:64, :], cs[:, :], ALU.mult, r=["psb0", "rt_cs"], w=["ra"])
        P.tt("dve", rbb[:, :], C.psb[1][0:64, :], sn[:, :], ALU.mult, r=["psb1", "rt_sn"], w=["rbb"])
        P.tt("pool", kpe[:, ts_], ra[:, :], rbb[:, :], ALU.add, r=["ra", "rbb"], w=["kpe"])
        rms_norm_tile(C, cqf, "cqf", qng, "qng", cqn, "cqn", 3, RMS_EPS, 0)
        rms_norm_tile(C, ckvf, "ckvf", kvng, "kvng", ckvn, "ckvn", 2, RMS_EPS, 1)
        for h in range(2):
            for kc in range(2):
                P.mm(C.psb[h][:, :], wukb[:, kc, h * 128:(h + 1) * 128], ckvn[:, kc, :], kc == 0, kc == 1,
                     r=["wukb", "ckvn%d" % kc], w=["psb%d" % h])
            P.copy("act", knT[:, h, ts_], C.psb[h][:, :], r=["psb%d" % h], w=["knT"])
        for i in range(4):
            bk = i % 2
            for kc in range(2):
                P.mm(C.psb[bk][:, 0:256], ckvn[:, kc, i * 128:(i + 1) * 128], wuvb[:, kc, :], kc == 0, kc == 1,
                     r=["wuvb", "ckvn%d" % kc], w=["psb%d" % bk])
            P.copy("act", vb[:, t * 4 + i, :], C.psb[bk][:, 0:256], r=["psb%d" % bk], w=["vb"])
        for h in range(2):
            for kc in range(3):
                P.mm(C.psb[0][:, :], wuqnb[:, kc, h * 128:(h + 1) * 128], cqn[:, kc, :], kc == 0, kc == 2,
                     r=["wuqnb", "cqn%d" % kc], w=["psb0"])
            P.act(qnb[:, h, :], C.psb[0][:, :], AF.Copy, r=["psb0"], w=["qnb%d" % h], scale=QSCALE)
            for j in range(2):
                for kc in range(3):
                    c0 = h * 128 + j * 64
                    P.mm(C.psb[1][0:64, :] if j == 0 else C.psb[0][0:64, :],
                         wuqrb[:, kc, c0:c0 + 64], cqn[:, kc, :], kc == 0, kc == 2,
                         r=["wuqrb", "cqn%d" % kc], w=["psb1" if j == 0 else "psb0"])
            P.tt("dve", ra[:, :], C.psb[1][0:64, :], cs[:, :], ALU.mult, r=["psb1", "rt_cs"], w=["ra"])
            P.tt("dve", rbb[:, :], C.psb[0][0:64, :], sn[:, :], ALU.mult, r=["psb0", "rt_sn"], w=["rbb"])
            P.stt(qpb[:, h, :], ra[:, :], 1.0, rbb[:, :], ALU.mult, ALU.add, r=["ra", "rbb"], w=["qpb%d" % h])
            P.ts("pool", qpb[:, h, :], qpb[:, h, :], QSCALE, None, ALU.mult, r=["qpb%d" % h], w=["qpb%d" % h])
        nkb = 4 * t + 4
        for h in range(2):
            po, kpo = C.psb[4 + 2 * h], "psb%d" % (4 + 2 * h)
            pl, kpl = C.psb[5 + 2 * h], "psb%d" % (5 + 2 * h)
            for kb in range(nkb):
                ks_ = slice(kb * 128, (kb + 1) * 128)
                bs_ = 2 + (pti % 2)
                pT, kpt = pTs[pti % 2], "pT%d" % (pti % 2)
                pti += 1
                P.mm(C.psb[bs_][:, :], knT[:, h, ks_], qnb[:, h, :], True, False, r=["knT", "qnb%d" % h], w=["psb%d" % bs_])
                P.mm(C.psb[bs_][:, :], kpe[:, ks_], qpb[:, h, :], False, True, r=["kpe", "qpb%d" % h], w=["psb%d" % bs_])
                P.act(pT[:, :], C.psb[bs_][:, :], AF.Exp, r=["psb%d" % bs_], w=[kpt])
                if kb >= 4 * t:
                    P.tt("pool", pT[:, :], pT[:, :], maskb[:, kb - 4 * t, :], ALU.mult, r=[kpt, "maskb"], w=[kpt])
                P.mm(po[:, :], vb[:, kb, h * 128:(h + 1) * 128], pT[:, :], kb == 0, kb == nkb - 1, r=["vb", kpt], w=[kpo])
                P.mm(pl[:, :], onesb[:, :], pT[:, :], kb == 0, kb == nkb - 1, r=["onesb", kpt], w=[kpl])
            P.emit("dve", lambda e, pl=pl: e.reciprocal(rl[:, :], pl[:, :]), r=[kpl], w=["rl"])
            P.tt("dve", ob[:, h, :], po[:, :], rl[:, :], ALU.mult, r=[kpo, "rl"], w=["ob%d" % h])
        P.dma(ydst(t), ob[:, :, :], r=["ob0", "ob1"], is_out=True)
    P.finalize()


def mla_consts():
    inv32 = (10000.0 ** (-np.arange(32, dtype=np.float32) / np.float32(32))).astype(np.float32)
    inv = np.zeros((64, 2), np.float32)
    inv[:, 0] = np.concatenate([inv32, inv32])
    inv[:, 1] = np.concatenate([-np.ones(32), np.ones(32)])
    j = np.arange(128)[:, None, None]
    d = np.arange(4)[None, :, None]
    i = np.arange(512)[None, None, :]
    mask = (d * 128 + j <= i).astype(np.float32).reshape(128, 2048)
    return dict(inv=inv, mask=np.ascontiguousarray(mask))


GROUPS = [[0, 1, 2, 3], [4, 5, 6, 7]]
LRU_BLOCKS_OF = [[0, 1, 2], [3, 4, 5], [6, 7, 0], [8, 9, 0]]
LRU_VALID = [3, 3, 2, 2]
RET_W = (("wq", [D, 256]), ("wk", [D, 256]), ("wv", [D, 512]), ("wg", [D, 512]), ("gng", [128, 4]))
RET_C = (("inv", [128, 1]), ("maskT", [128, 2048]), ("qdec", [128, 512]), ("kdec", [128, 4]), ("cdec", [128, 1]))
LRU_W = (("wgt", [D, 384]), ("wrc", [D, 384]), ("wa", [128, NBLK, 128]), ("wx", [128, NBLK, 128]), ("vec", [128, NBLK, 8]))
MLA_W = (("wcq", [D, 384]), ("wckv", [D, 256]), ("wkpe", [D, 128]), ("wuqn", [384, 256]), ("wuqr", [384, 256]),
         ("wuk", [256, 256]), ("wuv", [256, 256]), ("qng", [128, 3]), ("kvng", [128, 2]), ("inv", [64, 2]),
         ("mask", [128, 4 * 512]))
POST_F = [2048, 1280, 1024, 2048]


def emit_allgather(nc, src, dst, rows):
    n = src.shape[0] // rows
    assert n * rows == src.shape[0] and dst.shape[0] == 4 * src.shape[0]
    with nc.semaphore("ccs%d" % _PHASE_ID[0]) as cc, nc.Block() as block:
        _PHASE_ID[0] += 1

        @block.gpsimd
        def _(g):
            g.sem_clear(cc)
            for k in range(n):
                g.collective_compute("AllGather", ALU.bypass, replica_groups=GROUPS,
                                     ins=[src[k * rows:(k + 1) * rows, :]],
                                     outs=[dst[k * 4 * rows:(k + 1) * 4 * rows, :]]).then_inc(cc, 1)
            g.wait_ge(cc, n)


def emit_copy_phase(nc, pairs):
    P = Prog(nc)
    for o, i in pairs:
        P.dma(o, i, is_out=True)
    P.finalize()


def build_fused(nlayers=DEPTH):
    nc = bass.Bass("TRN2", target_bir_lowering=False)
    ext = lambda n, shp, dt=F32: nc.dram_tensor(n, shp, dt, kind="ExternalInput")
    xT = ext("xT", [D, SEG]).ap()
    pos = ext("pos", [1, S], I32)
    Wm = []
    for i in range(DEPTH):
        kind = i % 3
        W = {}
        if kind == 0:
            for n, shp in RET_W:
                W[n] = ext("L%d_%s" % (i, n), shp).ap()
            for n, shp in RET_C:
                if i == 0:
                    W[n] = ext("retc_" + n, shp).ap()
                else:
                    W[n] = Wm[0][n]
        elif kind == 1:
            for n, shp in LRU_W:
                W[n] = ext("L%d_%s" % (i, n), shp).ap()
        else:
            for n, shp in MLA_W:
                W[n] = ext("L%d_%s" % (i, n), shp).ap()
        W["pos"] = pos
        Wm.append(W)
    Wp = []
    for i in range(DEPTH):
        W = {}
        for n, shp in (("wo", [len(_post_ychunks(i)[0]) * 128, D]), ("w1", [D, 4 * D]), ("w2", [4 * D, D]),
                       ("lng", [128, 2, 8]), ("lnb", [128, 2, 8])):
            W[n] = ext("P%d_%s" % (i, n), shp).ap()
        Wp.append(W)
    oT = nc.dram_tensor("oT", [D, SEG], F32, kind="ExternalOutput").ap()
    hbd = nc.dram_tensor("hbd", [D, SEG], BF16).ap()
    hfull = nc.dram_tensor("hfull", [4 * D, SEG], BF16).ap()
    FCORE = [512, 384, 256, 512]
    yd = [nc.dram_tensor("yd%d" % i, [4 * FCORE[i], SEG], BF16).ap() for i in range(DEPTH)]
    ya = [nc.dram_tensor("ya%d" % i, [16 * FCORE[i], SEG], BF16).ap() for i in range(DEPTH)]
    ym = [nc.dram_tensor("ym%d" % i, [4 * FCORE[i], SEG], BF16).ap() for i in range(DEPTH)]
    rspill = nc.dram_tensor("rspill", [D, SEG], F32).ap()
    hbv = hbd.rearrange("(c p) t -> p c t", p=128)
    ov = oT.rearrange("(c p) t -> p c t", p=128)
    xv = xT.rearrange("(c p) t -> p c t", p=128)
    rsv = rspill.rearrange("(c p) t -> p c t", p=128)

    hfv = hfull.rearrange("(k r j p) t -> r p k j t", k=4, r=4, j=2, p=128)

    def hsrc(t):
        r, cl = t // 4, (t % 4) * TT
        return hfv[r][:, :, :, cl:cl + TT]

    def run_token_phases(stack_layers):
        pass

    def alloc_resid():
        st = ExitStack()
        return st, st.enter_context(nc.sbuf_tensor("resid%d" % _PHASE_ID[0], [128, 8, SEG], F32))

    rstack, resid = alloc_resid()
    P = Prog(nc)
    xb0 = P.sb("xb0", [128, 8, TT], BF16)
    for t in range(SEG // TT):
        sl = slice(t * TT, (t + 1) * TT)
        P.dma(resid[:, :, sl], xv[:, :, sl], w=["r%d" % t])
        P.copy("dve", xb0[:, :, :], resid[:, :, sl], r=["r%d" % t], w=["xb0"])
        P.dma(hbv[:, :, sl], xb0[:, :, :], r=["xb0"], is_out=True)
    P.finalize()
    for i in range(nlayers):
        kind = i % 3
        emit_allgather(nc, hbd, hfull, 256)
        ydv = yd[i].rearrange("(s c p) t -> s p c t", s=4, p=128)
        ydst = lambda t, ydv=ydv: ydv[t // 4][:, :, (t % 4) * TT:(t % 4 + 1) * TT]
        if kind == 2:
            emit_copy_phase(nc, [(rsv[:, :, :], resid[:, :, :])])
            rstack.close()
        if kind == 0:
            emit_ret(nc, Wm[i], hsrc, ydst, S, True, True)
        elif kind == 1:
            emit_lru(nc, Wm[i], hsrc, ydst, S, True, True)
        else:
            emit_mla(nc, Wm[i], hsrc, ydst, S, True, True)
        if kind == 2:
            rstack, resid = alloc_resid()
            emit_copy_phase(nc, [(resid[:, :, :], rsv[:, :, :])])
        emit_allgather(nc, yd[i], ya[i], FCORE[i] // 2)
        emit_select_seg(nc, ya[i], ym[i])
        yav = ym[i].rearrange("(c p) t -> p c t", p=128)
        ychunks, FCY = _post_ychunks(i)
        last = i == nlayers - 1
        emit_post(nc, Wp[i], ychunks, FCY,
                  ysrc=lambda e, cache, t, c0, n, yav=yav: yav[:, c0:c0 + n, t * TT:(t + 1) * TT],
                  y_bf16=True, resid=resid, rsrc=None,
                  odst=(lambda t: ov[:, :, t * TT:(t + 1) * TT]) if last else None,
                  hdst=None if last else (lambda t: hbv[:, :, t * TT:(t + 1) * TT]), ntok=SEG)
    rstack.close()
    return nc


def emit_select_seg(nc, ya2d, ym2d):
    src = ya2d.rearrange("(k r f) t -> k r (f t)", k=8, r=4)
    dst = ym2d.rearrange("(r kk f) t -> kk r (f t)", r=4, kk=2)
    src2 = ya2d.rearrange("(k rf) t -> k (rf t)", k=8)
    with nc.semaphore("sel%d" % _PHASE_ID[0]) as sem, nc.Block() as block:
        _PHASE_ID[0] += 1

        @block.sync
        def _(e):
            e.sem_clear(sem)
            seg2 = e.snap((e.partition_id() % 4) * 2)
            for kk in range(2):
                e.dma_start(out=dst[kk], in_=src2[bass.ds(seg2 + kk, 1), :].rearrange("o (r n) -> (o r) n", r=4)
                            ).then_inc(sem, 16)
            e.wait_ge(sem, 32)


def _post_ychunks(i):
    kind = i % 3
    if kind == 0:
        return list(range(16)), 16
    if kind == 2:
        return list(range(8)), 8
    m = {}
    for g in range(4):
        for k in range(LRU_VALID[g]):
            m[LRU_BLOCKS_OF[g][k]] = g * 3 + k
    return [m[k] for k in range(10)], 12


_NC = {}


def _c(a):
    return np.ascontiguousarray(a, dtype=np.float32)


def _ret_inputs(w_in, gn_g, hd):
    return dict(wq=_c(w_in[:, hd * 256:(hd + 1) * 256]), wk=_c(w_in[:, 1024 + hd * 256:1024 + (hd + 1) * 256]),
                wv=_c(w_in[:, 2048 + hd * 512:2048 + (hd + 1) * 512]), wg=_c(w_in[:, 4096 + hd * 512:4096 + (hd + 1) * 512]),
                gng=_c(gn_g[hd * 512:(hd + 1) * 512].reshape(4, 128).T))


def _lru_inputs(w_in, conv_w, conv_b, w_a, b_a, w_x, b_x, lam, g):
    bl = LRU_BLOCKS_OF[g]
    cols = np.concatenate([np.arange(k * 128, (k + 1) * 128) for k in bl])
    vec = np.stack([conv_w[0][cols], conv_w[1][cols], conv_w[2][cols], conv_w[3][cols], conv_b[cols],
                    b_a[cols], b_x[cols], lam[cols]], -1)
    return dict(wgt=_c(w_in[:, cols]), wrc=_c(w_in[:, 1280 + cols]),
                wa=_c(w_a[bl].transpose(1, 0, 2)), wx=_c(w_x[bl].transpose(1, 0, 2)),
                vec=_c(vec.reshape(3, 128, 8).transpose(1, 0, 2)))


def _mla_inputs(w_in, qn, kvn, w_uq, w_ukv, hp):
    heads = [2 * hp, 2 * hp + 1]
    sw = np.concatenate([np.arange(672, 704), np.arange(640, 672)])
    d = dict(wcq=_c(w_in[:, 0:384]), wckv=_c(w_in[:, 384:640]),
             wkpe=_c(np.concatenate([w_in[:, 640:704], w_in[:, sw]], 1)),
             wuqn=_c(np.concatenate([w_uq[:, k * 192:k * 192 + 128] for k in heads], 1)),
             wuqr=_c(np.concatenate([np.concatenate([w_uq[:, k * 192 + 128:k * 192 + 192], w_uq[:, k * 192 + 160:k * 192 + 192],
                                                     w_uq[:, k * 192 + 128:k * 192 + 160]], 1) for k in heads], 1)),
             wuk=_c(np.concatenate([w_ukv[:, k * 256:k * 256 + 128] for k in heads], 1)),
             wuv=_c(np.concatenate([w_ukv[:, k * 256 + 128:k * 256 + 256] for k in heads], 1)),
             qng=_c(qn.reshape(3, 128).T), kvng=_c(kvn.reshape(2, 128).T))
    d.update(mla_consts())
    return d


def _ln_arr(v):
    return _c(v.reshape(2, 8, 128).transpose(2, 0, 1))


def kernel(x, positions, ret_w_in, ret_gn_g, ret_w_o,
           lru_w_in, lru_conv_w, lru_conv_b, lru_w_a, lru_b_a, lru_w_x, lru_b_x, lru_lam, lru_w_o,
           mla_w_in, mla_q_norm, mla_kv_norm, mla_w_uq, mla_w_ukv, mla_w_o,
           ln_g, ln_b, mlp_w1, mlp_w2):
    A = np.asarray
    x = A(x, dtype=np.float32)
    positions = A(positions)
    if "nc" not in _NC:
        _NC["nc"] = build_fused()
    nc = _NC["nc"]
    xf = x.reshape(B * S, D)
    wos = [A(ret_w_o)[0], A(lru_w_o)[0], A(mla_w_o)[0], A(ret_w_o)[1]]
    shared = {}
    for i in range(DEPTH):
        shared["P%d_wo" % i] = _c(wos[i])
        shared["P%d_w1" % i] = _c(A(mlp_w1)[i])
        shared["P%d_w2" % i] = _c(A(mlp_w2)[i])
        shared["P%d_lng" % i] = _ln_arr(A(ln_g)[i])
        shared["P%d_lnb" % i] = _ln_arr(A(ln_b)[i])
    ins = []
    for c in range(NCORE):
        b, g = c // 4, c % 4
        d = dict(shared)
        d["xT"] = _c(xf[c * SEG:(c + 1) * SEG].T)
        d["pos"] = np.ascontiguousarray(positions[b:b + 1], dtype=np.int32)
        for i in range(DEPTH):
            kind, j = i % 3, i // 3
            if kind == 0:
                w = _ret_inputs(A(ret_w_in)[j], A(ret_gn_g)[j], g)
                if i == 0:
                    for k, v in ret_consts(g).items():
                        d["retc_" + k] = v
            elif kind == 1:
                w = _lru_inputs(A(lru_w_in)[j], A(lru_conv_w)[j], A(lru_conv_b)[j], A(lru_w_a)[j], A(lru_b_a)[j],
                                A(lru_w_x)[j], A(lru_b_x)[j], A(lru_lam)[j], g)
            else:
                w = _mla_inputs(A(mla_w_in)[j], A(mla_q_norm)[j], A(mla_kv_norm)[j], A(mla_w_uq)[j], A(mla_w_ukv)[j], g)
            for k, v in w.items():
                d["L%d_%s" % (i, k)] = v
        ins.append(d)
    res = run_bass_kernel_spmd(nc, ins, core_ids=list(range(NCORE))).results
    out = np.empty((B * S, D), np.float32)
    for c in range(NCORE):
        out[c * SEG:(c + 1) * SEG] = res[c]["oT"].T
    return out.reshape(B, S, D)
```

```python
import numpy as np
from contextlib import ExitStack
import concourse.bass as bass
import concourse.mybir as mybir
from concourse.bass_utils import run_bass_kernel_spmd

F32 = mybir.dt.float32
BF16 = mybir.dt.bfloat16
I32 = mybir.dt.int32
AF = mybir.ActivationFunctionType
ALU = mybir.AluOpType
AX = mybir.AxisListType

D = 1024
S = 8192
B = 2
DEPTH = 4
ALPHA = (2.0 * DEPTH) ** 0.25
LN_EPS = 1e-5
NCORE = 8
TT = 512
SEG = 2048

ENGS = ("pe", "act", "dve", "pool", "sp")
SAME_ENG_SYNC = True


class Ins:
    __slots__ = ("eng", "fn", "deps", "sig", "sem", "val", "dma", "gidx")

    def __init__(self, eng, fn, dma, gidx):
        self.eng = eng
        self.fn = fn
        self.deps = set()
        self.sig = False
        self.sem = None
        self.val = 0
        self.dma = dma
        self.gidx = gidx


_PHASE_ID = [0]


class Prog:
    def __init__(self, nc, n_dma_sems=24):
        self.nc = nc
        _PHASE_ID[0] += 1
        self.pid = _PHASE_ID[0]
        self.all = []
        self.last_w = {}
        self.readers = {}
        self.stack = ExitStack()
        self.n_dma_sems = n_dma_sems
        self.out_dmas = []

    def sb(self, name, shape, dt):
        return self.stack.enter_context(self.nc.sbuf_tensor("%s_p%d" % (name, self.pid), list(shape), dt))

    def ps(self, name, shape, dt=F32):
        return self.stack.enter_context(self.nc.psum_tensor("%s_p%d" % (name, self.pid), list(shape), dt))

    def emit(self, eng, fn, r=(), w=(), dma=False, out=False):
        ins = Ins(eng, fn, dma, len(self.all))
        deps = set()
        for k in r:
            lw = self.last_w.get(k)
            if lw is not None:
                deps.add(lw)
        for k in w:
            lw = self.last_w.get(k)
            if lw is not None:
                deps.add(lw)
            for rd in self.readers.get(k, ()):
                deps.add(rd)
        for k in r:
            self.readers.setdefault(k, []).append(ins)
        for k in w:
            self.last_w[k] = ins
            self.readers[k] = []
        deps.discard(ins)
        for d in deps:
            if d.eng == eng and not d.dma:
                if eng == "pe" or eng == "sp" or not SAME_ENG_SYNC:
                    continue
            ins.deps.add(d)
        self.all.append(ins)
        if out:
            self.out_dmas.append(ins)
        return ins

    def dma(self, out, in_, r=(), w=(), eng="sp", is_out=False, slow=False):
        if slow:
            return self.emit(eng, lambda e: e.dma_start(out=out, in_=in_, allow_slow_non_contiguous=True),
                             r, w, dma=True, out=is_out)
        return self.emit(eng, lambda e: e.dma_start(out=out, in_=in_), r, w, dma=True, out=is_out)

    def mm(self, out, lhsT, rhs, start, stop, r=(), w=()):
        return self.emit("pe", lambda e: e.matmul(out, lhsT, rhs, start=start, stop=stop), r, w)

    def act(self, out, in_, func, r=(), w=(), bias=None, scale=None, eng="act"):
        kw = {}
        if bias is not None:
            kw["bias"] = bias
        if scale is not None:
            kw["scale"] = scale
        return self.emit("act", lambda e: e.activation(out, in_, func, **kw), r, w)

    def tt(self, eng, out, in0, in1, op, r=(), w=()):
        return self.emit(eng, lambda e: e.tensor_tensor(out, in0, in1, op), r, w)

    def ts(self, eng, out, in0, s1, s2, op0, op1=None, r=(), w=()):
        if op1 is None:
            return self.emit(eng, lambda e: e.tensor_scalar(out, in0, s1, None, op0), r, w)
        return self.emit(eng, lambda e: e.tensor_scalar(out, in0, s1, s2, op0, op1), r, w)

    def stt(self, out, in0, scalar, in1, op0, op1, r=(), w=()):
        return self.emit("dve", lambda e: e.scalar_tensor_tensor(out, in0, scalar, in1, op0, op1), r, w)

    def copy(self, eng, out, in_, r=(), w=()):
        if eng == "act":
            return self.emit("act", lambda e: e.activation(out, in_, AF.Copy), r, w)
        return self.emit(eng, lambda e: e.tensor_copy(out, in_), r, w)

    def memset(self, eng, ap, val, w=()):
        return self.emit(eng, lambda e: e.memset(ap, val), (), w)

    def finalize(self):
        nc = self.nc
        dma_prev = [None] * self.n_dma_sems
        di = 0
        for ins in self.all:
            if ins.dma:
                s = di % self.n_dma_sems
                di += 1
                ins.sem = ("dma", s)
                ins.sig = True
                if dma_prev[s] is not None:
                    ins.deps.add(dma_prev[s])
                dma_prev[s] = ins
        for ins in self.all:
            for d in ins.deps:
                d.sig = True
        cnt = {}
        for ins in self.all:
            if not ins.sig:
                continue
            if not ins.dma:
                ins.sem = ("eng", ins.eng)
            c = cnt.get(ins.sem, 0) + (16 if ins.dma else 1)
            cnt[ins.sem] = c
            ins.val = c
        sems = {}
        for key in sorted(cnt, key=str):
            sems[key] = self.stack.enter_context(nc.semaphore("s%d_%s_%s" % ((self.pid,) + key)))
        per = {e: [i for i in self.all if i.eng == e] for e in ENGS}
        final = list(self.out_dmas)

        def run(engobj, lst, extra_final=()):
            have = {}
            for ins in lst:
                need = {}
                for d in ins.deps:
                    if need.get(d.sem, 0) < d.val:
                        need[d.sem] = d.val
                for sk, v in need.items():
                    if have.get(sk, 0) < v:
                        engobj.wait_ge(sems[sk], v)
                        have[sk] = v
                bi = ins.fn(engobj)
                if ins.sig:
                    bi.then_inc(sems[ins.sem], 16 if ins.dma else 1)
            for d in extra_final:
                if have.get(d.sem, 0) < d.val:
                    engobj.wait_ge(sems[d.sem], d.val)
                    have[d.sem] = d.val

        with nc.Block() as block:
            @block.gpsimd
            def _(e):
                for s in sems.values():
                    e.sem_clear(s)

        with nc.Block() as block:
            @block.sync
            def _(e):
                run(e, per["sp"], final)

            if per["pe"]:
                @block.tensor
                def _(e):
                    run(e, per["pe"])
            if per["act"]:
                @block.scalar
                def _(e):
                    run(e, per["act"])
            if per["dve"]:
                @block.vector
                def _(e):
                    run(e, per["dve"])
            if per["pool"]:
                @block.gpsimd
                def _(e):
                    run(e, per["pool"])
        self.stack.close()


class Ctx:
    def __init__(self, P, nbanks=8, ln_banks=(6, 7), bf_bank=False, n_wst=2, n_wbf=2):
        self.P = P
        self.psb = [P.ps("psb%d" % i, [128, 512]) for i in range(nbanks)]
        self.pst = P.ps("pst", [128, 1024], BF16) if bf_bank else None
        self.ln_banks = ln_banks
        self.wst = [P.sb("wst%d" % i, [128, 8, 512], F32) for i in range(n_wst)]
        self.wbf = [P.sb("wbf%d" % i, [128, 8, 512], BF16) for i in range(n_wbf)]
        self.wi = 0
        self.ones = P.sb("ones_f", [128, 128], F32)
        P.memset("pool", self.ones[:, :], 1.0, w=["ones_f"])
        self.sq = [P.sb("sq%d" % i, [128, 512], F32) for i in range(2)]
        self.sqi = 0
        self.st = {n: P.sb("st_" + n, [128, 512], F32) for n in ("mean", "msq", "var", "rstd", "nmr")}
        self.tmp = [P.sb("tmp%d" % i, [128, 512], F32) for i in range(2)]
        self.tmpi = 0

    def load_wblock(self, wdram_pkn, kc0, nkc, c0, ncol=512):
        P = self.P
        i = self.wi % 2
        self.wi += 1
        st, bf = self.wst[i], self.wbf[i]
        ks, kb = "wst%d" % i, "wbf%d" % i
        P.dma(st[:, 0:nkc, 0:ncol], wdram_pkn[:, kc0:kc0 + nkc, c0:c0 + ncol], w=[ks])
        a = (nkc * 3 + 7) // 8
        b = (nkc * 6 + 7) // 8
        for eng, lo, hi, sfx in (("pool", 0, a, "a"), ("act", a, b, "b"), ("dve", b, nkc, "c")):
            if hi > lo:
                P.copy(eng, bf[:, lo:hi, 0:ncol], st[:, lo:hi, 0:ncol], r=[ks], w=[kb + sfx])
        return bf, (lambda kc: kb + ("a" if kc < a else ("b" if kc < b else "c")))

    def load_resident(self, wdram_pkn, dst, dkey, nkc, ncol):
        P = self.P
        i = self.wi % len(self.wst)
        self.wi += 1
        st = self.wst[i]
        ks = "wst%d" % i
        P.dma(st[:, 0:nkc, 0:ncol], wdram_pkn[:, 0:nkc, 0:ncol], w=[ks])
        P.copy("pool", dst[:, 0:nkc, 0:ncol], st[:, 0:nkc, 0:ncol], r=[ks], w=[dkey])


def layer_norm_tile(C, z, zkey, g_ap, b_ap, out_f, okey_f, out_b, okey_b, nch=8, T=TT, eps=LN_EPS, gkey="lnp"):
    P = C.P
    nfeat = float(nch * 128)
    ps1, k1 = C.psb[C.ln_banks[0]], "psb%d" % C.ln_banks[0]
    ps2, k2 = C.psb[C.ln_banks[1]], "psb%d" % C.ln_banks[1]
    for c in range(nch):
        sq = C.sq[C.sqi % 2]
        ksq = "sq%d" % (C.sqi % 2)
        C.sqi += 1
        P.act(sq[:, 0:T], z[:, c, 0:T], AF.Square, r=[zkey + str(c)], w=[ksq])
        P.mm(ps1[:, 0:T], C.ones[:, :], z[:, c, 0:T], c == 0, c == nch - 1, r=["ones_f", zkey + str(c)], w=[k1])
        P.mm(ps2[:, 0:T], C.ones[:, :], sq[:, 0:T], c == 0, c == nch - 1, r=["ones_f", ksq], w=[k2])
    mean, msq, var, rstd, nmr = (C.st[n] for n in ("mean", "msq", "var", "rstd", "nmr"))
    P.act(mean[:, 0:T], ps1[:, 0:T], AF.Copy, r=[k1], w=["st_mean"], scale=1.0 / nfeat)
    P.tt("dve", msq[:, 0:T], mean[:, 0:T], mean[:, 0:T], ALU.mult, r=["st_mean"], w=["st_msq"])
    P.stt(var[:, 0:T], ps2[:, 0:T], 1.0 / nfeat, msq[:, 0:T], ALU.mult, ALU.subtract, r=[k2, "st_msq"], w=["st_var"])
    P.ts("dve", var[:, 0:T], var[:, 0:T], eps, None, ALU.add, r=["st_var"], w=["st_var"])
    P.act(rstd[:, 0:T], var[:, 0:T], AF.Ln, r=["st_var"], w=["st_rstd"])
    P.act(rstd[:, 0:T], rstd[:, 0:T], AF.Exp, r=["st_rstd"], w=["st_rstd"], scale=-0.5)
    P.stt(nmr[:, 0:T], mean[:, 0:T], -1.0, rstd[:, 0:T], ALU.mult, ALU.mult, r=["st_mean", "st_rstd"], w=["st_nmr"])
    for c in range(nch):
        t = C.tmp[C.tmpi % 2]
        kt = "tmp%d" % (C.tmpi % 2)
        C.tmpi += 1
        P.tt("dve", t[:, 0:T], z[:, c, 0:T], rstd[:, 0:T], ALU.mult, r=[zkey + str(c), "st_rstd"], w=[kt])
        P.tt("pool", t[:, 0:T], t[:, 0:T], nmr[:, 0:T], ALU.add, r=[kt, "st_nmr"], w=[kt])
        if b_ap is not None:
            P.act(out_f[:, c, 0:T], t[:, 0:T], AF.Identity, r=[kt, gkey], w=[okey_f + str(c)],
                  bias=b_ap[:, c:c + 1], scale=g_ap[:, c:c + 1])
        else:
            P.act(out_f[:, c, 0:T], t[:, 0:T], AF.Copy, r=[kt, gkey], w=[okey_f + str(c)],
                  scale=g_ap[:, c:c + 1])
        if out_b is not None:
            P.copy("pool", out_b[:, c, 0:T], out_f[:, c, 0:T], r=[okey_f + str(c)], w=[okey_b + str(c)])


def build_post(F, ntok=SEG):
    nc = bass.Bass("TRN2", target_bir_lowering=False)
    FC = F // 128
    yT = nc.dram_tensor("yT", [F, ntok], F32, kind="ExternalInput").ap()
    rT = nc.dram_tensor("rT", [D, ntok], F32, kind="ExternalInput").ap()
    W = {}
    for n, shp in (("wo", [F, D]), ("w1", [D, 4 * D]), ("w2", [4 * D, D]), ("lng", [128, 2, 8]), ("lnb", [128, 2, 8])):
        W[n] = nc.dram_tensor(n, shp, F32, kind="ExternalInput").ap()
    oT = nc.dram_tensor("oT", [D, ntok], F32, kind="ExternalOutput").ap()
    yv = yT.rearrange("(c p) t -> p c t", p=128)
    rv = rT.rearrange("(c p) t -> p c t", p=128)
    ov = oT.rearrange("(c p) t -> p c t", p=128)
    emit_post(nc, W, list(range(FC)), FC,
              ysrc=lambda e, cache, t, c0, n: yv[:, c0:c0 + n, t * TT:(t + 1) * TT], y_bf16=False,
              resid=None, rsrc=lambda t: rv[:, :, t * TT:(t + 1) * TT],
              odst=lambda t: ov[:, :, t * TT:(t + 1) * TT], hdst=None, ntok=ntok)
    return nc


def emit_post(nc, W, ychunks, FCY, ysrc, y_bf16, resid, rsrc, odst, hdst, ntok=SEG):
    wo, w1, w2, lng, lnb = (W[k] for k in ("wo", "w1", "w2", "lng", "lnb"))
    FW = len(ychunks)
    P = Prog(nc)
    C = Ctx(P)
    wov = wo.rearrange("(c p) n -> p c n", p=128)
    w1v = w1.rearrange("(c p) n -> p c n", p=128)
    w2v = w2.rearrange("(c p) n -> p c n", p=128)
    lnp_g = P.sb("lnp_g", [128, 2, 8], F32)
    lnp_b = P.sb("lnp_b", [128, 2, 8], F32)
    P.dma(lnp_g[:, :, :], lng, w=["lnp"])
    P.dma(lnp_b[:, :, :], lnb, w=["lnp"])
    yst = None if y_bf16 else P.sb("yst", [128, 8, TT], F32)
    yb = P.sb("yb", [128, FCY, TT], BF16)
    zloc = None if resid is not None else P.sb("z", [128, 8, TT], F32)
    hb = P.sb("hb", [128, 8, TT], BF16)
    hid = P.sb("hid", [128, 32, TT], BF16)
    rl = [P.sb("rl%d" % i, [128, TT], F32) for i in range(2)]
    Z8 = ["z%d" % i for i in range(8)]
    ycache = {}
    for t in range(ntok // TT):
        t0 = t * TT
        for c0 in range(0, FCY, 8):
            n = min(8, FCY - c0)
            if y_bf16:
                P.emit("sp", lambda e, t=t, c0=c0, n=n: e.dma_start(out=yb[:, c0:c0 + n, :], in_=ysrc(e, ycache, t, c0, n)),
                       (), ["yb"], dma=True)
            else:
                P.emit("sp", lambda e, t=t, c0=c0, n=n: e.dma_start(out=yst[:, 0:n, :], in_=ysrc(e, ycache, t, c0, n)),
                       (), ["yst"], dma=True)
                P.copy("dve", yb[:, c0:c0 + n, :], yst[:, 0:n, :], r=["yst"], w=["yb"])
        if resid is None:
            z = zloc
            P.dma(z[:, :, :], rsrc(t), w=Z8, eng="pool")
        else:
            z = resid[:, :, t0:t0 + TT]
        blocks = [(k0, min(8, FW - k0)) for k0 in range(0, FW, 8)]
        for half in range(2):
            for bi, (k0, n) in enumerate(blocks):
                wb, kb = C.load_wblock(wov, k0, n, half * 512)
                for m in range(4):
                    for kc in range(n):
                        P.mm(C.psb[m][:, :], wb[:, kc, m * 128:(m + 1) * 128], yb[:, ychunks[k0 + kc], :],
                             bi == 0 and kc == 0, bi == len(blocks) - 1 and kc == n - 1,
                             r=[kb(kc), "yb"], w=["psb%d" % m])
            for m in range(4):
                c = half * 4 + m
                P.stt(z[:, c, :], z[:, c, :], ALPHA, C.psb[m][:, :], ALU.mult, ALU.add,
                      r=["z%d" % c, "psb%d" % m], w=["z%d" % c])
        layer_norm_tile(C, z, "z", lnp_g[:, 0, :], lnp_b[:, 0, :], z, "z", hb, "hb")
        for nb in range(8):
            wb, kb = C.load_wblock(w1v, 0, 8, nb * 512)
            for m in range(4):
                pb = C.psb[m]
                kp = "psb%d" % m
                for kc in range(8):
                    P.mm(pb[:, :], wb[:, kc, m * 128:(m + 1) * 128], hb[:, kc, :], kc == 0, kc == 7,
                         r=[kb(kc), "hb%d" % kc], w=[kp])
                rr = rl[m % 2]
                kr = "rl%d" % (m % 2)
                P.act(rr[:, :], pb[:, :], AF.Relu, r=[kp], w=[kr])
                P.tt("dve", hid[:, nb * 4 + m, :], rr[:, :], rr[:, :], ALU.mult, r=[kr], w=["hid"])
        for half in range(2):
            for kb4 in range(4):
                wb, kb = C.load_wblock(w2v, kb4 * 8, 8, half * 512)
                for m in range(4):
                    for kc in range(8):
                        P.mm(C.psb[m][:, :], wb[:, kc, m * 128:(m + 1) * 128], hid[:, kb4 * 8 + kc, :],
                             kb4 == 0 and kc == 0, kb4 == 3 and kc == 7, r=[kb(kc), "hid"], w=["psb%d" % m])
            for m in range(4):
                c = half * 4 + m
                P.stt(z[:, c, :], z[:, c, :], ALPHA, C.psb[m][:, :], ALU.mult, ALU.add,
                      r=["z%d" % c, "psb%d" % m], w=["z%d" % c])
        layer_norm_tile(C, z, "z", lnp_g[:, 1, :], lnp_b[:, 1, :], z, "z", hb if hdst is not None else None, "hb")
        if odst is not None:
            P.dma(odst(t), z[:, :, :], r=Z8, is_out=True)
        if hdst is not None:
            P.dma(hdst(t), hb[:, :, :], r=["hb%d" % i for i in range(8)], is_out=True)
    P.finalize()


def rope_tables(P, pos_dram, t0, T, inv_ap, bufs, sign_ap=None):
    import math
    npart = bufs["np"]
    posi, a, b, ki, cs, sn = (bufs[k] for k in ("posi", "a", "b", "ki", "cs", "sn"))
    sl = slice(0, npart)
    P.dma(posi[sl, 0:T], bass.AP(pos_dram, t0, [[0, npart], [1, T]]), w=["rt_posi"])
    P.copy("dve", a[sl, 0:T], posi[sl, 0:T], r=["rt_posi"], w=["rt_a"])
    P.ts("dve", a[sl, 0:T], a[sl, 0:T], inv_ap, None, ALU.mult, r=["rt_a", "rt_inv"], w=["rt_a"])
    P.ts("dve", b[sl, 0:T], a[sl, 0:T], 1.0 / (2 * math.pi), None, ALU.mult, r=["rt_a"], w=["rt_b"])
    P.copy("dve", ki[sl, 0:T], b[sl, 0:T], r=["rt_b"], w=["rt_ki"])
    P.copy("dve", b[sl, 0:T], ki[sl, 0:T], r=["rt_ki"], w=["rt_b"])
    c1 = float(np.float32(6.28125))
    c2 = float(2 * math.pi - 6.28125)
    P.stt(a[sl, 0:T], b[sl, 0:T], -c1, a[sl, 0:T], ALU.mult, ALU.add, r=["rt_a", "rt_b"], w=["rt_a"])
    P.stt(a[sl, 0:T], b[sl, 0:T], -c2, a[sl, 0:T], ALU.mult, ALU.add, r=["rt_a", "rt_b"], w=["rt_a"])
    P.ts("dve", b[sl, 0:T], a[sl, 0:T], math.pi / 2, None, ALU.add, r=["rt_a"], w=["rt_b"])
    P.ts("dve", cs[sl, 0:T], b[sl, 0:T], math.pi, None, ALU.is_gt, r=["rt_b"], w=["rt_cs"])
    P.stt(b[sl, 0:T], cs[sl, 0:T], -2 * math.pi, b[sl, 0:T], ALU.mult, ALU.add, r=["rt_b", "rt_cs"], w=["rt_b"])
    lim = 3.14159
    P.ts("dve", b[sl, 0:T], b[sl, 0:T], lim, -lim, ALU.min, ALU.max, r=["rt_b"], w=["rt_b"])
    P.ts("dve", a[sl, 0:T], a[sl, 0:T], lim, -lim, ALU.min, ALU.max, r=["rt_a"], w=["rt_a"])
    P.act(cs[sl, 0:T], b[sl, 0:T], AF.Sin, r=["rt_b"], w=["rt_cs"])
    if sign_ap is None:
        P.act(sn[sl, 0:T], a[sl, 0:T], AF.Sin, r=["rt_a"], w=["rt_sn"])
    else:
        P.act(sn[sl, 0:T], a[sl, 0:T], AF.Sin, r=["rt_a", "rt_inv"], w=["rt_sn"], scale=sign_ap)


def make_identity_bf(P, idb, key):
    P.memset("pool", idb[:, :], 0.0, w=[key])
    P.emit("pool", lambda e: e.affine_select(idb[:, :], idb[:, :], [[-1, 128]], ALU.not_equal, 1.0,
                                             base=0, channel_multiplier=1), r=[key], w=[key])


def build_ret(nseq=S):
    nc = bass.Bass("TRN2", target_bir_lowering=False)
    W = {}
    hT = nc.dram_tensor("hT", [D, nseq], F32, kind="ExternalInput").ap()
    for n, shp in (("wq", [D, 256]), ("wk", [D, 256]), ("wv", [D, 512]), ("wg", [D, 512]), ("gng", [128, 4]),
                   ("inv", [128, 1]), ("maskT", [128, 2048]), ("qdec", [128, 512]), ("kdec", [128, 4]), ("cdec", [128, 1])):
        W[n] = nc.dram_tensor(n, shp, F32, kind="ExternalInput").ap()
    W["pos"] = nc.dram_tensor("pos", [1, nseq], I32, kind="ExternalInput")
    yT = nc.dram_tensor("yT", [512, nseq], F32, kind="ExternalOutput").ap()
    hv = hT.rearrange("(c p) t -> p c t", p=128)
    yv = yT.rearrange("(c p) t -> p c t", p=128)
    emit_ret(nc, W, lambda t: hv[:, :, t * TT:(t + 1) * TT], lambda t: yv[:, :, t * TT:(t + 1) * TT], nseq, False, False)
    return nc


def emit_ret(nc, W, hsrc, ydst, nseq, in_bf16, out_bf16):
    wq, wk, wv, wg, gng_d, inv_d, maskT_d, qdec_d, kdec_d, cdec_d, pos = (
        W[k] for k in ("wq", "wk", "wv", "wg", "gng", "inv", "maskT", "qdec", "kdec", "cdec", "pos"))
    P = Prog(nc)
    C = Ctx(P, nbanks=7, ln_banks=(4, 5), bf_bank=True, n_wst=1, n_wbf=0)
    pk = lambda w: w.rearrange("(c p) n -> p c n", p=128)
    wqb = P.sb("wqb", [128, 8, 256], BF16)
    wkb = P.sb("wkb", [128, 8, 256], BF16)
    wvb = P.sb("wvb", [128, 8, 512], BF16)
    wgb = P.sb("wgb", [128, 8, 512], BF16)
    C.load_resident(pk(wq), wqb, "wqb", 8, 256)
    C.load_resident(pk(wk), wkb, "wkb", 8, 256)
    C.load_resident(pk(wv), wvb, "wvb", 8, 512)
    C.load_resident(pk(wg), wgb, "wgb", 8, 512)
    gng = P.sb("gng_s", [128, 4], F32)
    inv = P.sb("inv_s", [128, 1], F32)
    maskT = P.sb("maskT_s", [128, 4, 512], F32)
    qdec = P.sb("qdec_s", [128, 512], F32)
    kdec = P.sb("kdec_s", [128, 4], F32)
    cdec = P.sb("cdec_s", [128, 1], F32)
    P.dma(gng[:, :], gng_d, w=["gng"])
    P.dma(inv[:, :], inv_d, w=["rt_inv"])
    P.dma(maskT[:, :, :], maskT_d.rearrange("p (k t) -> p k t", k=4), w=["maskT"])
    P.dma(qdec[:, :], qdec_d, w=["qdec"])
    P.dma(kdec[:, :], kdec_d, w=["kdec"])
    P.dma(cdec[:, :], cdec_d, w=["cdec"])
    idb = P.sb("idb", [128, 128], BF16)
    make_identity_bf(P, idb, "idb")
    rb = {"np": 128, "posi": P.sb("rt_posi", [128, TT], I32), "a": P.sb("rt_a", [128, TT], F32),
          "b": P.sb("rt_b", [128, TT], F32), "ki": P.sb("rt_ki", [128, TT], I32),
          "cs": P.sb("rt_cs", [128, TT], F32), "sn": P.sb("rt_sn", [128, TT], F32)}
    csk = P.sb("csk", [128, TT], F32)
    snk = P.sb("snk", [128, TT], F32)
    xst = C.wst[0]
    xb = P.sb("xb", [128, 8, TT], BF16)
    yob = P.sb("yob", [128, 4, TT], BF16) if out_bf16 else None
    qf = P.sb("qf", [128, 2, TT], F32)
    qb = P.sb("qb", [128, 2, TT], BF16)
    qdb = P.sb("qdb", [128, 2, TT], BF16)
    kb = P.sb("kb", [128, 2, TT], BF16)
    kdT = P.sb("kdT", [128, 4, 256], BF16)
    vb = P.sb("vb", [128, 4, 512], BF16)
    sg = P.sb("sg", [128, 4, TT], F32)
    of = P.sb("of", [128, 4, TT], F32)
    ra = P.sb("ra", [128, TT], F32)
    rbb = P.sb("rbb", [128, TT], F32)
    pT = P.sb("pT", [128, 4, TT], BF16)
    st = P.sb("st", [128, 2, 512], F32)
    stb = P.sb("stb", [128, 2, 512], BF16)
    P.memset("pool", st[:, :, :], 0.0, w=["st0", "st1"])
    P.memset("pool", stb[:, :, :], 0.0, w=["stb0", "stb1"])
    X8 = ["xb%d" % i for i in range(8)]
    for t in range(nseq // TT):
        t0 = t * TT
        if in_bf16:
            for k in range(4):
                P.dma(xb[:, 2 * k:2 * k + 2, :], hsrc(t)[:, k, :, :], w=["xb%d" % (2 * k), "xb%d" % (2 * k + 1)])
        else:
            P.dma(xst[:, :, :], hsrc(t), w=["wst0"])
            for kc in range(8):
                P.copy("dve" if kc % 2 == 0 else "pool", xb[:, kc, :], xst[:, kc, :], r=["wst0"], w=["xb%d" % kc])
        rope_tables(P, pos, t0, TT, inv[:, 0:1], rb)
        cs, sn = rb["cs"], rb["sn"]
        P.ts("pool", csk[:, :], cs[:, :], 1.0 / 16.0, None, ALU.mult, r=["rt_cs"], w=["csk"])
        P.ts("pool", snk[:, :], sn[:, :], 1.0 / 16.0, None, ALU.mult, r=["rt_sn"], w=["snk"])
        for m in range(2):
            for kc in range(8):
                P.mm(C.psb[m][:, :], wqb[:, kc, m * 128:(m + 1) * 128], xb[:, kc, :], kc == 0, kc == 7,
                     r=["wqb", "xb%d" % kc], w=["psb%d" % m])
        P.tt("dve", ra[:, :], C.psb[0][:, :], cs[:, :], ALU.mult, r=["psb0", "rt_cs"], w=["ra"])
        P.tt("dve", rbb[:, :], C.psb[1][:, :], sn[:, :], ALU.mult, r=["psb1", "rt_sn"], w=["rbb"])
        P.tt("pool", qf[:, 0, :], ra[:, :], rbb[:, :], ALU.subtract, r=["ra", "rbb"], w=["qf0"])
        P.tt("dve", ra[:, :], C.psb[0][:, :], sn[:, :], ALU.mult, r=["psb0", "rt_sn"], w=["ra"])
        P.tt("dve", rbb[:, :], C.psb[1][:, :], cs[:, :], ALU.mult, r=["psb1", "rt_cs"], w=["rbb"])
        P.tt("pool", qf[:, 1, :], ra[:, :], rbb[:, :], ALU.add, r=["ra", "rbb"], w=["qf1"])
        for m in range(2):
            P.copy("act", qb[:, m, :], qf[:, m, :], r=["qf%d" % m], w=["qb%d" % m])
            P.tt("pool", qdb[:, m, :], qf[:, m, :], qdec[:, :], ALU.mult, r=["qf%d" % m, "qdec"], w=["qdb%d" % m])
        for m in range(2):
            for kc in range(8):
                P.mm(C.psb[2 + m][:, :], wkb[:, kc, m * 128:(m + 1) * 128], xb[:, kc, :], kc == 0, kc == 7,
                     r=["wkb", "xb%d" % kc], w=["psb%d" % (2 + m)])
        P.tt("dve", ra[:, :], C.psb[2][:, :], csk[:, :], ALU.mult, r=["psb2", "csk"], w=["ra"])
        P.tt("dve", rbb[:, :], C.psb[3][:, :], snk[:, :], ALU.mult, r=["psb3", "snk"], w=["rbb"])
        P.tt("pool", kb[:, 0, :], ra[:, :], rbb[:, :], ALU.subtract, r=["ra", "rbb"], w=["kb0"])
        P.tt("dve", ra[:, :], C.psb[2][:, :], snk[:, :], ALU.mult, r=["psb2", "snk"], w=["ra"])
        P.tt("dve", rbb[:, :], C.psb[3][:, :], csk[:, :], ALU.mult, r=["psb3", "csk"], w=["rbb"])
        P.tt("pool", kb[:, 1, :], ra[:, :], rbb[:, :], ALU.add, r=["ra", "rbb"], w=["kb1"])
        for m in range(4):
            bk = m % 4
            for kc in range(8):
                P.mm(C.psb[bk][:, :], wgb[:, kc, m * 128:(m + 1) * 128], xb[:, kc, :], kc == 0, kc == 7,
                     r=["wgb", "xb%d" % kc], w=["psb%d" % bk])
            P.act(sg[:, m, :], C.psb[bk][:, :], AF.Silu, r=["psb%d" % bk], w=["sg%d" % m])
        for i in range(4):
            bk = i % 4
            for kc in range(8):
                P.mm(C.psb[bk][:, :], xb[:, kc, i * 128:(i + 1) * 128], wvb[:, kc, :], kc == 0, kc == 7,
                     r=["wvb", "xb%d" % kc], w=["psb%d" % bk])
            P.copy("act", vb[:, i, :], C.psb[bk][:, :], r=["psb%d" % bk], w=["vb%d" % i])
        for i in range(4):
            for m in range(2):
                col = (i * 2 + m) * 128
                P.emit("pe", lambda e, col=col, m=m, i=i: e.transpose(C.pst[:, col:col + 128], kb[:, m, i * 128:(i + 1) * 128], idb[:, :]),
                       r=["kb%d" % m, "idb"], w=["pst"])
        for i in range(4):
            P.ts("dve", kdT[:, i, :], C.pst[:, i * 256:(i + 1) * 256], kdec[:, i:i + 1], None, ALU.mult,
                 r=["pst", "kdec"], w=["kdT%d" % i])
        for kblk in range(4):
            cl = slice(kblk * 128, (kblk + 1) * 128)
            qs = slice(kblk * 128, TT)
            bk = 4 + (kblk % 2)
            for m in range(2):
                P.mm(C.psb[bk][:, qs], kb[:, m, cl], qb[:, m, qs], m == 0, m == 1,
                     r=["kb%d" % m, "qb%d" % m], w=["psb%d" % bk])
            P.tt("dve", pT[:, kblk, qs], C.psb[bk][:, qs], maskT[:, kblk, qs], ALU.mult,
                 r=["psb%d" % bk, "maskT"], w=["pT%d" % kblk])
        for vc in range(4):
            vs = slice(vc * 128, (vc + 1) * 128)
            bk = 6 if vc % 2 == 0 else 3
            for m in range(2):
                P.mm(C.psb[bk][:, :], stb[:, m, vs], qdb[:, m, :], m == 0, False,
                     r=["stb%d" % m, "qdb%d" % m], w=["psb%d" % bk])
            for kblk in range(4):
                qs = slice(kblk * 128, TT)
                P.mm(C.psb[bk][:, qs], vb[:, kblk, vs], pT[:, kblk, qs], False, kblk == 3,
                     r=["vb%d" % kblk, "pT%d" % kblk], w=["psb%d" % bk])
            P.copy("act", of[:, vc, :], C.psb[bk][:, :], r=["psb%d" % bk], w=["of%d" % vc])
        for m in range(2):
            ds_ = slice(m * 128, (m + 1) * 128)
            bk = 4 + m
            for i in range(4):
                P.mm(C.psb[bk][:, :], kdT[:, i, ds_], vb[:, i, :], i == 0, i == 3, r=["kdT%d" % i, "vb%d" % i], w=["psb%d" % bk])
            P.stt(st[:, m, :], st[:, m, :], cdec[:, 0:1], C.psb[bk][:, :], ALU.mult, ALU.add,
                  r=["st%d" % m, "cdec", "psb%d" % bk], w=["st%d" % m])
            P.copy("pool", stb[:, m, :], st[:, m, :], r=["st%d" % m], w=["stb%d" % m])
        layer_norm_tile(C, of, "of", gng, None, of, "of", None, None, nch=4, gkey="gng")
        yo, ky = (yob, "yob") if out_bf16 else (of, "of")
        for c in range(4):
            P.tt("dve" if c % 2 == 0 else "pool", yo[:, c, :], of[:, c, :], sg[:, c, :], ALU.mult,
                 r=["of%d" % c, "sg%d" % c], w=[ky + str(c)])
        P.dma(ydst(t), yo[:, :, :], r=[ky + str(c) for c in range(4)], is_out=True)
    P.finalize()


def ret_consts(h):
    lg = np.log1p(-np.exp2(-5.0 - h))
    i = np.arange(512, dtype=np.float64)
    j = np.arange(128, dtype=np.float64)
    k = np.arange(4, dtype=np.float64)
    diff = i[None, None, :] - (128.0 * k[None, :, None] + j[:, None, None])
    maskT = np.where(diff >= 0, np.exp(lg * np.maximum(diff, 0.0)), 0.0).reshape(128, 2048)
    qdec = np.broadcast_to(np.exp(lg * (i + 1.0))[None, :], (128, 512))
    kdec = np.exp(lg * (511.0 - (128.0 * k[None, :] + j[:, None])))
    cdec = np.full((128, 1), np.exp(lg * 512.0))
    inv = (10000.0 ** (-np.arange(128, dtype=np.float32) / np.float32(128))).astype(np.float32)[:, None]
    f = lambda a: np.ascontiguousarray(a, dtype=np.float32)
    return dict(maskT=f(maskT), qdec=f(qdec), kdec=f(kdec), cdec=f(cdec), inv=f(inv))


NBLK = 3


def build_lru(nseq=S):
    nc = bass.Bass("TRN2", target_bir_lowering=False)
    W3 = NBLK * 128
    W = {}
    hT = nc.dram_tensor("hT", [D, nseq], F32, kind="ExternalInput").ap()
    for n, shp in (("wgt", [D, W3]), ("wrc", [D, W3]), ("wa", [128, NBLK, 128]), ("wx", [128, NBLK, 128]), ("vec", [128, NBLK, 8])):
        W[n] = nc.dram_tensor(n, shp, F32, kind="ExternalInput").ap()
    yT = nc.dram_tensor("yT", [W3, nseq], F32, kind="ExternalOutput").ap()
    hv = hT.rearrange("(c p) t -> p c t", p=128)
    yv = yT.rearrange("(c p) t -> p c t", p=128)
    emit_lru(nc, W, lambda t: hv[:, :, t * TT:(t + 1) * TT], lambda t: yv[:, :, t * TT:(t + 1) * TT], nseq, False, False)
    return nc


def emit_lru(nc, W, hsrc, ydst, nseq, in_bf16, out_bf16):
    W3 = NBLK * 128
    wgt, wrc, wa_d, wx_d, vec_d = (W[k] for k in ("wgt", "wrc", "wa", "wx", "vec"))
    P = Prog(nc)
    C = Ctx(P, nbanks=8, n_wst=1, n_wbf=0)
    pk = lambda w: w.rearrange("(c p) n -> p c n", p=128)
    wgb = P.sb("wgb", [128, 8, W3], BF16)
    wrb = P.sb("wrb", [128, 8, W3], BF16)
    C.load_resident(pk(wgt), wgb, "wgb", 8, W3)
    C.load_resident(pk(wrc), wrb, "wrb", 8, W3)
    waf = P.sb("waf", [128, NBLK, 128], F32)
    wxf = P.sb("wxf", [128, NBLK, 128], F32)
    wab = P.sb("wab", [128, NBLK, 128], BF16)
    wxb = P.sb("wxb", [128, NBLK, 128], BF16)
    P.dma(waf[:, :, :], wa_d, w=["waf"])
    P.dma(wxf[:, :, :], wx_d, w=["wxf"])
    P.copy("dve", wab[:, :, :], waf[:, :, :], r=["waf"], w=["wab"])
    P.copy("dve", wxb[:, :, :], wxf[:, :, :], r=["wxf"], w=["wxb"])
    vec = P.sb("vec_s", [128, NBLK, 8], F32)
    P.dma(vec[:, :, :], vec_d, w=["vec"])
    der = P.sb("der", [128, NBLK, 4], F32)
    P.act(der[:, :, 0], vec[:, :, 7], AF.Exp, r=["vec"], w=["der"], scale=-1.0)
    P.ts("dve", der[:, :, 0], der[:, :, 0], 1.0, None, ALU.add, r=["der"], w=["der"])
    P.act(der[:, :, 0], der[:, :, 0], AF.Ln, r=["der"], w=["der"])
    P.ts("dve", der[:, :, 1], der[:, :, 0], -8.0, None, ALU.mult, r=["der"], w=["der"])
    P.ts("dve", der[:, :, 2], der[:, :, 0], -16.0, None, ALU.mult, r=["der"], w=["der"])
    P.ts("dve", der[:, :, 3], der[:, :, 0], 8.0, None, ALU.mult, r=["der"], w=["der"])
    xst = C.wst[0]
    xb = P.sb("xb", [128, 8, TT], BF16)
    rec = P.sb("rec", [128, NBLK, TT + 4], F32)
    P.memset("pool", rec[:, :, :], 0.0, w=["rec%d" % b for b in range(NBLK)])
    hprev = P.sb("hprev", [128, NBLK], F32)
    P.memset("pool", hprev[:, :], 0.0, w=["hprev%d" % b for b in range(NBLK)])
    gt = P.sb("gt", [128, TT], F32)
    g2 = P.sb("g2", [128, TT], F32)
    u = P.sb("u", [128, TT], F32)
    ub = P.sb("ub", [128, TT], BF16)
    rr = P.sb("rr", [128, TT], F32)
    ii = P.sb("ii", [128, TT], F32)
    aa = P.sb("aa", [128, TT], F32)
    ee = P.sb("ee", [128, TT], F32)
    th = P.sb("th", [128, TT], F32)
    hs = P.sb("hs", [128, TT], F32)
    yo = P.sb("yo", [128, NBLK, TT], BF16 if out_bf16 else F32)
    for t in range(nseq // TT):
        t0 = t * TT
        if in_bf16:
            for k in range(4):
                P.dma(xb[:, 2 * k:2 * k + 2, :], hsrc(t)[:, k, :, :], w=["xb%d" % (2 * k), "xb%d" % (2 * k + 1)])
        else:
            P.dma(xst[:, :, :], hsrc(t), w=["wst0"])
            for kc in range(8):
                P.copy("dve" if kc % 2 == 0 else "pool", xb[:, kc, :], xst[:, kc, :], r=["wst0"], w=["xb%d" % kc])
        for b in range(NBLK):
            bs = slice(b * 128, (b + 1) * 128)
            kr = "rec%d" % b
            for kc in range(8):
                P.mm(C.psb[0][:, :], wgb[:, kc, bs], xb[:, kc, :], kc == 0, kc == 7, r=["wgb", "xb%d" % kc], w=["psb0"])
            P.act(g2[:, :], C.psb[0][:, :], AF.Square, r=["psb0"], w=["g2"])
            P.ts("dve", g2[:, :], g2[:, :], 0.044715, 1.0, ALU.mult, ALU.add, r=["g2"], w=["g2"])
            P.tt("dve", g2[:, :], g2[:, :], C.psb[0][:, :], ALU.mult, r=["g2", "psb0"], w=["g2"])
            P.act(g2[:, :], g2[:, :], AF.Sigmoid, r=["g2"], w=["g2"], scale=1.5957691216057308)
            P.tt("dve", gt[:, :], g2[:, :], C.psb[0][:, :], ALU.mult, r=["g2", "psb0"], w=["gt"])
            for kc in range(8):
                P.mm(C.psb[1][:, :], wrb[:, kc, bs], xb[:, kc, :], kc == 0, kc == 7, r=["wrb", "xb%d" % kc], w=["psb1"])
            P.copy("act", rec[:, b, 3:3 + TT], C.psb[1][:, :], r=["psb1"], w=[kr])
            P.ts("dve", u[:, :], rec[:, b, 3:3 + TT], vec[:, b, 3:4], vec[:, b, 4:5], ALU.mult, ALU.add, r=[kr, "vec"], w=["u"])
            for k in range(3):
                P.stt(u[:, :], rec[:, b, k:k + TT], vec[:, b, k:k + 1], u[:, :], ALU.mult, ALU.add, r=[kr, "vec", "u"], w=["u"])
            P.copy("pool", rec[:, b, 0:3], rec[:, b, TT:TT + 3], r=[kr], w=[kr])
            P.copy("pool", ub[:, :], u[:, :], r=["u"], w=["ub"])
            P.mm(C.psb[2][:, :], wab[:, b, :], ub[:, :], True, True, r=["wab", "ub"], w=["psb2"])
            P.mm(C.psb[3][:, :], wxb[:, b, :], ub[:, :], True, True, r=["wxb", "ub"], w=["psb3"])
            P.act(rr[:, :], C.psb[2][:, :], AF.Sigmoid, r=["psb2", "vec"], w=["rr"], bias=vec[:, b, 5:6])
            P.act(ii[:, :], C.psb[3][:, :], AF.Sigmoid, r=["psb3", "vec"], w=["ii"], bias=vec[:, b, 6:7])
            P.act(aa[:, :], rr[:, :], AF.Exp, r=["rr", "der"], w=["aa"], scale=der[:, b, 1:2])
            P.act(ee[:, :], rr[:, :], AF.Exp, r=["rr", "der"], w=["ee"], scale=der[:, b, 2:3])
            P.act(th[:, :], rr[:, :], AF.Tanh, r=["rr", "der"], w=["th"], scale=der[:, b, 3:4])
            P.stt(ee[:, :], ee[:, :], 1.0, th[:, :], ALU.add, ALU.mult, r=["ee", "th"], w=["ee"])
            P.act(ee[:, :], ee[:, :], AF.Sqrt, r=["ee"], w=["ee"])
            P.tt("pool", ii[:, :], ii[:, :], u[:, :], ALU.mult, r=["ii", "u"], w=["ii"])
            P.tt("pool", ii[:, :], ii[:, :], ee[:, :], ALU.mult, r=["ii", "ee"], w=["ii"])
            P.emit("dve", lambda e, b=b: e.tensor_tensor_scan(hs[:, :], aa[:, :], ii[:, :], hprev[:, b:b + 1], ALU.mult, ALU.add),
                   r=["aa", "ii", "hprev%d" % b], w=["hs"])
            P.copy("dve", hprev[:, b:b + 1], hs[:, TT - 1:TT], r=["hs"], w=["hprev%d" % b])
            P.tt("pool", yo[:, b, :], hs[:, :], gt[:, :], ALU.mult, r=["hs", "gt"], w=["yo%d" % b])
        P.dma(ydst(t), yo[:, :, :], r=["yo%d" % b for b in range(NBLK)], is_out=True)
    P.finalize()


def rms_norm_tile(C, z, zkey, g_ap, gkey, out_b, okey, nch, eps, bank, T=TT):
    P = C.P
    ps, kp = C.psb[bank], "psb%d" % bank
    for c in range(nch):
        sq = C.sq[C.sqi % 2]
        ksq = "sq%d" % (C.sqi % 2)
        C.sqi += 1
        P.act(sq[:, 0:T], z[:, c, 0:T], AF.Square, r=[zkey + str(c)], w=[ksq])
        P.mm(ps[:, 0:T], C.ones[:, :], sq[:, 0:T], c == 0, c == nch - 1, r=["ones_f", ksq], w=[kp])
    rstd = C.st["rstd"]
    P.ts("dve", rstd[:, 0:T], ps[:, 0:T], 1.0 / (nch * 128.0), eps, ALU.mult, ALU.add, r=[kp], w=["st_rstd"])
    P.act(rstd[:, 0:T], rstd[:, 0:T], AF.Ln, r=["st_rstd"], w=["st_rstd"])
    P.act(rstd[:, 0:T], rstd[:, 0:T], AF.Exp, r=["st_rstd"], w=["st_rstd"], scale=-0.5)
    for c in range(nch):
        t = C.tmp[C.tmpi % 2]
        kt = "tmp%d" % (C.tmpi % 2)
        C.tmpi += 1
        P.tt("dve", t[:, 0:T], z[:, c, 0:T], rstd[:, 0:T], ALU.mult, r=[zkey + str(c), "st_rstd"], w=[kt])
        P.ts("pool", out_b[:, c, 0:T], t[:, 0:T], g_ap[:, c:c + 1], None, ALU.mult, r=[kt, gkey], w=[okey + str(c)])


QSCALE = 192.0 ** -0.5
RMS_EPS = 1e-6


def build_mla(nseq=S):
    nc = bass.Bass("TRN2", target_bir_lowering=False)
    W = {}
    hT = nc.dram_tensor("hT", [D, nseq], F32, kind="ExternalInput").ap()
    for n, shp in (("wcq", [D, 384]), ("wckv", [D, 256]), ("wkpe", [D, 128]), ("wuqn", [384, 256]), ("wuqr", [384, 256]),
                   ("wuk", [256, 256]), ("wuv", [256, 256]), ("qng", [128, 3]), ("kvng", [128, 2]), ("inv", [64, 2]),
                   ("mask", [128, 4 * 512])):
        W[n] = nc.dram_tensor(n, shp, F32, kind="ExternalInput").ap()
    W["pos"] = nc.dram_tensor("pos", [1, nseq], I32, kind="ExternalInput")
    oT = nc.dram_tensor("oT", [256, nseq], F32, kind="ExternalOutput").ap()
    hv = hT.rearrange("(c p) t -> p c t", p=128)
    ov = oT.rearrange("(c p) t -> p c t", p=128)
    emit_mla(nc, W, lambda t: hv[:, :, t * TT:(t + 1) * TT], lambda t: ov[:, :, t * TT:(t + 1) * TT], nseq, False, False)
    return nc


def emit_mla(nc, W, hsrc, ydst, nseq, in_bf16, out_bf16):
    NKB = nseq // 128
    wcq, wckv, wkpe, wuqn, wuqr, wuk, wuv, qng_d, kvng_d, inv_d, mask_d, pos = (
        W[k] for k in ("wcq", "wckv", "wkpe", "wuqn", "wuqr", "wuk", "wuv", "qng", "kvng", "inv", "mask", "pos"))
    P = Prog(nc)
    C = Ctx(P, nbanks=8, n_wst=1, n_wbf=0)
    pk = lambda w: w.rearrange("(c p) n -> p c n", p=128)
    wcqb = P.sb("wcqb", [128, 8, 384], BF16)
    wckvb = P.sb("wckvb", [128, 8, 256], BF16)
    wkpeb = P.sb("wkpeb", [128, 8, 128], BF16)
    wuqnb = P.sb("wuqnb", [128, 3, 256], BF16)
    wuqrb = P.sb("wuqrb", [128, 3, 256], BF16)
    wukb = P.sb("wukb", [128, 2, 256], BF16)
    wuvb = P.sb("wuvb", [128, 2, 256], BF16)
    C.load_resident(pk(wcq), wcqb, "wcqb", 8, 384)
    C.load_resident(pk(wckv), wckvb, "wckvb", 8, 256)
    C.load_resident(pk(wkpe), wkpeb, "wkpeb", 8, 128)
    C.load_resident(pk(wuqn), wuqnb, "wuqnb", 3, 256)
    C.load_resident(pk(wuqr), wuqrb, "wuqrb", 3, 256)
    C.load_resident(pk(wuk), wukb, "wukb", 2, 256)
    C.load_resident(pk(wuv), wuvb, "wuvb", 2, 256)
    qng = P.sb("qng_s", [128, 3], F32)
    kvng = P.sb("kvng_s", [128, 2], F32)
    inv = P.sb("inv_s", [64, 2], F32)
    P.dma(qng[:, :], qng_d, w=["qng"])
    P.dma(kvng[:, :], kvng_d, w=["kvng"])
    P.dma(inv[:, :], inv_d, w=["rt_inv"])
    maskb = P.sb("maskb", [128, 4, 512], BF16)
    C.load_resident(mask_d.rearrange("p (c n) -> p c n", c=4), maskb, "maskb", 4, 512)
    onesb = P.sb("onesb", [128, 128], BF16)
    P.memset("pool", onesb[:, :], 1.0, w=["onesb"])
    rb = {"np": 64, "posi": P.sb("rt_posi", [64, TT], I32), "a": P.sb("rt_a", [64, TT], F32),
          "b": P.sb("rt_b", [64, TT], F32), "ki": P.sb("rt_ki", [64, TT], I32),
          "cs": P.sb("rt_cs", [64, TT], F32), "sn": P.sb("rt_sn", [64, TT], F32)}
    knT = P.sb("knT", [128, 2, nseq], BF16)
    kpe = P.sb("kpe", [64, nseq], BF16)
    vb = P.sb("vb", [128, NKB, 256], BF16)
    xst = C.wst[0]
    xb = P.sb("xb", [128, 8, TT], BF16)
    cqf = P.sb("cqf", [128, 3, TT], F32)
    ckvf = P.sb("ckvf", [128, 2, TT], F32)
    cqn = P.sb("cqn", [128, 3, TT], BF16)
    ckvn = P.sb("ckvn", [128, 2, TT], BF16)
    ra = P.sb("ra", [64, TT], F32)
    rbb = P.sb("rbb", [64, TT], F32)
    qnb = P.sb("qnb", [128, 2, TT], BF16)
    qpb = P.sb("qpb", [64, 2, TT], BF16)
    pTs = [P.sb("pT%d" % i, [128, TT], BF16) for i in range(2)]
    rl = P.sb("rl", [128, TT], F32)
    ob = P.sb("ob", [128, 2, TT], BF16 if out_bf16 else F32)
    pti = 0
    for t in range(nseq // TT):
        t0 = t * TT
        ts_ = slice(t0, t0 + TT)
        if in_bf16:
            for k in range(4):
                P.dma(xb[:, 2 * k:2 * k + 2, :], hsrc(t)[:, k, :, :], w=["xb%d" % (2 * k), "xb%d" % (2 * k + 1)])
        else:
            P.dma(xst[:, :, :], hsrc(t), w=["wst0"])
            for kc in range(8):
                P.copy("dve" if kc % 2 == 0 else "pool", xb[:, kc, :], xst[:, kc, :], r=["wst0"], w=["xb%d" % kc])
        rope_tables(P, pos, t0, TT, inv[:, 0:1], rb, sign_ap=inv[:, 1:2])
        cs, sn = rb["cs"], rb["sn"]
        for m in range(3):
            bk = m % 2
            for kc in range(8):
                P.mm(C.psb[bk][:, :], wcqb[:, kc, m * 128:(m + 1) * 128], xb[:, kc, :], kc == 0, kc == 7,
                     r=["wcqb", "xb%d" % kc], w=["psb%d" % bk])
            P.copy("act", cqf[:, m, :], C.psb[bk][:, :], r=["psb%d" % bk], w=["cqf%d" % m])
        for m in range(2):
            bk = (m + 1) % 2
            for kc in range(8):
                P.mm(C.psb[bk][:, :], wckvb[:, kc, m * 128:(m + 1) * 128], xb[:, kc, :], kc == 0, kc == 7,
                     r=["wckvb", "xb%d" % kc], w=["psb%d" % bk])
            P.copy("act", ckvf[:, m, :], C.psb[bk][:, :], r=["psb%d" % bk], w=["ckvf%d" % m])
        for j in range(2):
            for kc in range(8):
                P.mm(C.psb[j][0:64, :], wkpeb[:, kc, j * 64:(j + 1) * 64], xb[:, kc, :], kc == 0, kc == 7,
                     r=["wkpeb", "xb%d" % kc], w=["psb%d" % j])
        P.tt("dve", ra[:, :], C.psb[0][0:64, :], cs[:, :], ALU.mult, r=["psb0", "rt_cs"], w=["ra"])
        P.tt("dve", rbb[:, :], C.psb[1][0:64, :], sn[:, :], ALU.mult, r=["psb1", "rt_sn"], w=["rbb"])
        P.tt("pool", kpe[:, ts_], ra[:, :], rbb[:, :], ALU.add, r=["ra", "rbb"], w=["kpe"])
        rms_norm_tile(C, cqf, "cqf", qng, "qng", cqn, "cqn", 3, RMS_EPS, 0)
        rms_norm_tile(C, ckvf, "ckvf", kvng, "kvng", ckvn, "ckvn", 2, RMS_EPS, 1)
        for h in range(2):
            for kc in range(2):
                P.mm(C.psb[h][:, :], wukb[:, kc, h * 128:(h + 1) * 128], ckvn[:, kc, :], kc == 0, kc == 1,
                     r=["wukb", "ckvn%d" % kc], w=["psb%d" % h])
            P.copy("act", knT[:, h, ts_], C.psb[h][:, :], r=["psb%d" % h], w=["knT"])
        for i in range(4):
            bk = i % 2
            for kc in range(2):
                P.mm(C.psb[bk][:, 0:256], ckvn[:, kc, i * 128:(i + 1) * 128], wuvb[:, kc, :], kc == 0, kc == 1,
                     r=["wuvb", "ckvn%d" % kc], w=["psb%d" % bk])
            P.copy("act", vb[:, t * 4 + i, :], C.psb[bk][:, 0:256], r=["psb%d" % bk], w=["vb"])
        for h in range(2):
            for kc in range(3):
                P.mm(C.psb[0][:, :], wuqnb[:, kc, h * 128:(h + 1) * 128], cqn[:, kc, :], kc == 0, kc == 2,
                     r=["wuqnb", "cqn%d" % kc], w=["psb0"])
            P.act(qnb[:, h, :], C.psb[0][:, :], AF.Copy, r=["psb0"], w=["qnb%d" % h], scale=QSCALE)
            for j in range(2):
                for kc in range(3):
                    c0 = h * 128 + j * 64
                    P.mm(C.psb[1][0:64, :] if j == 0 else C.psb[0][0:64, :],
                         wuqrb[:, kc, c0:c0 + 64], cqn[:, kc, :], kc == 0, kc == 2,
                         r=["wuqrb", "cqn%d" % kc], w=["psb1" if j == 0 else "psb0"])
            P.tt("dve", ra[:, :], C.psb[1][0:64, :], cs[:, :], ALU.mult, r=["psb1", "rt_cs"], w=["ra"])
            P.tt("dve", rbb[:, :], C.psb[0][0:64, :], sn[:, :], ALU.mult, r=["psb0", "rt_sn"], w=["rbb"])
            P.stt(qpb[:, h, :], ra[:, :], 1.0, rbb[:, :], ALU.mult, ALU.add, r=["ra", "rbb"], w=["qpb%d" % h])
            P.ts("pool", qpb[:, h, :], qpb[:, h, :], QSCALE, None, ALU.mult, r=["qpb%d" % h], w=["qpb%d" % h])
        nkb = 4 * t + 4
        for h in range(2):
            po, kpo = C.psb[4 + 2 * h], "psb%d" % (4 + 2 * h)
            pl, kpl = C.psb[5 + 2 * h], "psb%d" % (5 + 2 * h)
            def scores(kb, slot):
                ks_ = slice(kb * 128, (kb + 1) * 128)
                bs_ = 2 + slot
                P.mm(C.psb[bs_][:, :], knT[:, h, ks_], qnb[:, h, :], True, False, r=["knT", "qnb%d" % h], w=["psb%d" % bs_])
                P.mm(C.psb[bs_][:, :], kpe[:, ks_], qpb[:, h, :], False, True, r=["kpe", "qpb%d" % h], w=["psb%d" % bs_])

            scores(0, pti % 2)
            for kb in range(nkb):
                slot = pti % 2
                bs_ = 2 + slot
                pT, kpt = pTs[slot], "pT%d" % slot
                pti += 1
                P.act(pT[:, :], C.psb[bs_][:, :], AF.Exp, r=["psb%d" % bs_], w=[kpt])
                if kb >= 4 * t:
                    P.tt("pool", pT[:, :], pT[:, :], maskb[:, kb - 4 * t, :], ALU.mult, r=[kpt, "maskb"], w=[kpt])
                if kb + 1 < nkb:
                    scores(kb + 1, pti % 2)
                P.mm(po[:, :], vb[:, kb, h * 128:(h + 1) * 128], pT[:, :], kb == 0, kb == nkb - 1, r=["vb", kpt], w=[kpo])
                P.mm(pl[:, :], onesb[:, :], pT[:, :], kb == 0, kb == nkb - 1, r=["onesb", kpt], w=[kpl])
            P.emit("dve", lambda e, pl=pl: e.reciprocal(rl[:, :], pl[:, :]), r=[kpl], w=["rl"])
            P.tt("dve", ob[:, h, :], po[:, :], rl[:, :], ALU.mult, r=[kpo, "rl"], w=["ob%d" % h])
        P.dma(ydst(t), ob[:, :, :], r=["ob0", "ob1"], is_out=True)
    P.finalize()


def mla_consts():
    inv32 = (10000.0 ** (-np.arange(32, dtype=np.float32) / np.float32(32))).astype(np.float32)
    inv = np.zeros((64, 2), np.float32)
    inv[:, 0] = np.concatenate([inv32, inv32])
    inv[:, 1] = np.concatenate([-np.ones(32), np.ones(32)])
    j = np.arange(128)[:, None, None]
    d = np.arange(4)[None, :, None]
    i = np.arange(512)[None, None, :]
    mask = (d * 128 + j <= i).astype(np.float32).reshape(128, 2048)
    return dict(inv=inv, mask=np.ascontiguousarray(mask))


GROUPS = [[0, 1, 2, 3], [4, 5, 6, 7]]
LRU_BLOCKS_OF = [[0, 1, 2], [3, 4, 5], [6, 7, 0], [8, 9, 0]]
LRU_VALID = [3, 3, 2, 2]
RET_W = (("wq", [D, 256]), ("wk", [D, 256]), ("wv", [D, 512]), ("wg", [D, 512]), ("gng", [128, 4]))
RET_C = (("inv", [128, 1]), ("maskT", [128, 2048]), ("qdec", [128, 512]), ("kdec", [128, 4]), ("cdec", [128, 1]))
LRU_W = (("wgt", [D, 384]), ("wrc", [D, 384]), ("wa", [128, NBLK, 128]), ("wx", [128, NBLK, 128]), ("vec", [128, NBLK, 8]))
MLA_W = (("wcq", [D, 384]), ("wckv", [D, 256]), ("wkpe", [D, 128]), ("wuqn", [384, 256]), ("wuqr", [384, 256]),
         ("wuk", [256, 256]), ("wuv", [256, 256]), ("qng", [128, 3]), ("kvng", [128, 2]), ("inv", [64, 2]),
         ("mask", [128, 4 * 512]))
POST_F = [2048, 1280, 1024, 2048]


def emit_allgather(nc, src, dst, rows):
    n = src.shape[0] // rows
    assert n * rows == src.shape[0] and dst.shape[0] == 4 * src.shape[0]
    with nc.semaphore("ccs%d" % _PHASE_ID[0]) as cc, nc.Block() as block:
        _PHASE_ID[0] += 1

        @block.gpsimd
        def _(g):
            g.sem_clear(cc)
            for k in range(n):
                g.collective_compute("AllGather", ALU.bypass, replica_groups=GROUPS,
                                     ins=[src[k * rows:(k + 1) * rows, :]],
                                     outs=[dst[k * 4 * rows:(k + 1) * 4 * rows, :]]).then_inc(cc, 1)
            g.wait_ge(cc, n)


def emit_copy_phase(nc, pairs):
    P = Prog(nc)
    for o, i in pairs:
        P.dma(o, i, is_out=True)
    P.finalize()


def build_fused(nlayers=DEPTH):
    nc = bass.Bass("TRN2", target_bir_lowering=False)
    ext = lambda n, shp, dt=F32: nc.dram_tensor(n, shp, dt, kind="ExternalInput")
    xT = ext("xT", [D, SEG]).ap()
    pos = ext("pos", [1, S], I32)
    Wm = []
    for i in range(DEPTH):
        kind = i % 3
        W = {}
        if kind == 0:
            for n, shp in RET_W:
                W[n] = ext("L%d_%s" % (i, n), shp).ap()
            for n, shp in RET_C:
                if i == 0:
                    W[n] = ext("retc_" + n, shp).ap()
                else:
                    W[n] = Wm[0][n]
        elif kind == 1:
            for n, shp in LRU_W:
                W[n] = ext("L%d_%s" % (i, n), shp).ap()
        else:
            for n, shp in MLA_W:
                W[n] = ext("L%d_%s" % (i, n), shp).ap()
        W["pos"] = pos
        Wm.append(W)
    Wp = []
    for i in range(DEPTH):
        W = {}
        for n, shp in (("wo", [len(_post_ychunks(i)[0]) * 128, D]), ("w1", [D, 4 * D]), ("w2", [4 * D, D]),
                       ("lng", [128, 2, 8]), ("lnb", [128, 2, 8])):
            W[n] = ext("P%d_%s" % (i, n), shp).ap()
        Wp.append(W)
    oT = nc.dram_tensor("oT", [D, SEG], F32, kind="ExternalOutput").ap()
    hbd = nc.dram_tensor("hbd", [D, SEG], BF16).ap()
    hfull = nc.dram_tensor("hfull", [4 * D, SEG], BF16).ap()
    FCORE = [512, 384, 256, 512]
    yd = [nc.dram_tensor("yd%d" % i, [4 * FCORE[i], SEG], BF16).ap() for i in range(DEPTH)]
    ya = [nc.dram_tensor("ya%d" % i, [16 * FCORE[i], SEG], BF16).ap() for i in range(DEPTH)]
    ym = [nc.dram_tensor("ym%d" % i, [4 * FCORE[i], SEG], BF16).ap() for i in range(DEPTH)]
    rspill = nc.dram_tensor("rspill", [D, SEG], F32).ap()
    hbv = hbd.rearrange("(c p) t -> p c t", p=128)
    ov = oT.rearrange("(c p) t -> p c t", p=128)
    xv = xT.rearrange("(c p) t -> p c t", p=128)
    rsv = rspill.rearrange("(c p) t -> p c t", p=128)

    hfv = hfull.rearrange("(k r j p) t -> r p k j t", k=4, r=4, j=2, p=128)

    def hsrc(t):
        r, cl = t // 4, (t % 4) * TT
        return hfv[r][:, :, :, cl:cl + TT]

    def run_token_phases(stack_layers):
        pass

    def alloc_resid():
        st = ExitStack()
        return st, st.enter_context(nc.sbuf_tensor("resid%d" % _PHASE_ID[0], [128, 8, SEG], F32))

    rstack, resid = alloc_resid()
    P = Prog(nc)
    xb0 = P.sb("xb0", [128, 8, TT], BF16)
    for t in range(SEG // TT):
        sl = slice(t * TT, (t + 1) * TT)
        P.dma(resid[:, :, sl], xv[:, :, sl], w=["r%d" % t])
        P.copy("dve", xb0[:, :, :], resid[:, :, sl], r=["r%d" % t], w=["xb0"])
        P.dma(hbv[:, :, sl], xb0[:, :, :], r=["xb0"], is_out=True)
    P.finalize()
    for i in range(nlayers):
        kind = i % 3
        emit_allgather(nc, hbd, hfull, 256)
        ydv = yd[i].rearrange("(s c p) t -> s p c t", s=4, p=128)
        ydst = lambda t, ydv=ydv: ydv[t // 4][:, :, (t % 4) * TT:(t % 4 + 1) * TT]
        if kind == 2:
            emit_copy_phase(nc, [(rsv[:, :, :], resid[:, :, :])])
            rstack.close()
        if kind == 0:
            emit_ret(nc, Wm[i], hsrc, ydst, S, True, True)
        elif kind == 1:
            emit_lru(nc, Wm[i], hsrc, ydst, S, True, True)
        else:
            emit_mla(nc, Wm[i], hsrc, ydst, S, True, True)
        if kind == 2:
            rstack, resid = alloc_resid()
            emit_copy_phase(nc, [(resid[:, :, :], rsv[:, :, :])])
        emit_allgather(nc, yd[i], ya[i], FCORE[i] // 2)
        emit_select_seg(nc, ya[i], ym[i])
        yav = ym[i].rearrange("(c p) t -> p c t", p=128)
        ychunks, FCY = _post_ychunks(i)
        last = i == nlayers - 1
        emit_post(nc, Wp[i], ychunks, FCY,
                  ysrc=lambda e, cache, t, c0, n, yav=yav: yav[:, c0:c0 + n, t * TT:(t + 1) * TT],
                  y_bf16=True, resid=resid, rsrc=None,
                  odst=(lambda t: ov[:, :, t * TT:(t + 1) * TT]) if last else None,
                  hdst=None if last else (lambda t: hbv[:, :, t * TT:(t + 1) * TT]), ntok=SEG)
    rstack.close()
    return nc


def emit_select_seg(nc, ya2d, ym2d):
    src = ya2d.rearrange("(k r f) t -> k r (f t)", k=8, r=4)
    dst = ym2d.rearrange("(r kk f) t -> kk r (f t)", r=4, kk=2)
    src2 = ya2d.rearrange("(k rf) t -> k (rf t)", k=8)
    with nc.semaphore("sel%d" % _PHASE_ID[0]) as sem, nc.Block() as block:
        _PHASE_ID[0] += 1

        @block.sync
        def _(e):
            e.sem_clear(sem)
            seg2 = e.snap((e.partition_id() % 4) * 2)
            for kk in range(2):
                e.dma_start(out=dst[kk], in_=src2[bass.ds(seg2 + kk, 1), :].rearrange("o (r n) -> (o r) n", r=4)
                            ).then_inc(sem, 16)
            e.wait_ge(sem, 32)


def _post_ychunks(i):
    kind = i % 3
    if kind == 0:
        return list(range(16)), 16
    if kind == 2:
        return list(range(8)), 8
    m = {}
    for g in range(4):
        for k in range(LRU_VALID[g]):
            m[LRU_BLOCKS_OF[g][k]] = g * 3 + k
    return [m[k] for k in range(10)], 12


_NC = {}


def _c(a):
    return np.ascontiguousarray(a, dtype=np.float32)


def _ret_inputs(w_in, gn_g, hd):
    return dict(wq=_c(w_in[:, hd * 256:(hd + 1) * 256]), wk=_c(w_in[:, 1024 + hd * 256:1024 + (hd + 1) * 256]),
                wv=_c(w_in[:, 2048 + hd * 512:2048 + (hd + 1) * 512]), wg=_c(w_in[:, 4096 + hd * 512:4096 + (hd + 1) * 512]),
                gng=_c(gn_g[hd * 512:(hd + 1) * 512].reshape(4, 128).T))


def _lru_inputs(w_in, conv_w, conv_b, w_a, b_a, w_x, b_x, lam, g):
    bl = LRU_BLOCKS_OF[g]
    cols = np.concatenate([np.arange(k * 128, (k + 1) * 128) for k in bl])
    vec = np.stack([conv_w[0][cols], conv_w[1][cols], conv_w[2][cols], conv_w[3][cols], conv_b[cols],
                    b_a[cols], b_x[cols], lam[cols]], -1)
    return dict(wgt=_c(w_in[:, cols]), wrc=_c(w_in[:, 1280 + cols]),
                wa=_c(w_a[bl].transpose(1, 0, 2)), wx=_c(w_x[bl].transpose(1, 0, 2)),
                vec=_c(vec.reshape(3, 128, 8).transpose(1, 0, 2)))


def _mla_inputs(w_in, qn, kvn, w_uq, w_ukv, hp):
    heads = [2 * hp, 2 * hp + 1]
    sw = np.concatenate([np.arange(672, 704), np.arange(640, 672)])
    d = dict(wcq=_c(w_in[:, 0:384]), wckv=_c(w_in[:, 384:640]),
             wkpe=_c(np.concatenate([w_in[:, 640:704], w_in[:, sw]], 1)),
             wuqn=_c(np.concatenate([w_uq[:, k * 192:k * 192 + 128] for k in heads], 1)),
             wuqr=_c(np.concatenate([np.concatenate([w_uq[:, k * 192 + 128:k * 192 + 192], w_uq[:, k * 192 + 160:k * 192 + 192],
                                                     w_uq[:, k * 192 + 128:k * 192 + 160]], 1) for k in heads], 1)),
             wuk=_c(np.concatenate([w_ukv[:, k * 256:k * 256 + 128] for k in heads], 1)),
             wuv=_c(np.concatenate([w_ukv[:, k * 256 + 128:k * 256 + 256] for k in heads], 1)),
             qng=_c(qn.reshape(3, 128).T), kvng=_c(kvn.reshape(2, 128).T))
    d.update(mla_consts())
    return d


def _ln_arr(v):
    return _c(v.reshape(2, 8, 128).transpose(2, 0, 1))


def kernel(x, positions, ret_w_in, ret_gn_g, ret_w_o,
           lru_w_in, lru_conv_w, lru_conv_b, lru_w_a, lru_b_a, lru_w_x, lru_b_x, lru_lam, lru_w_o,
           mla_w_in, mla_q_norm, mla_kv_norm, mla_w_uq, mla_w_ukv, mla_w_o,
           ln_g, ln_b, mlp_w1, mlp_w2):
    A = np.asarray
    x = A(x, dtype=np.float32)
    positions = A(positions)
    if "nc" not in _NC:
        _NC["nc"] = build_fused()
    nc = _NC["nc"]
    xf = x.reshape(B * S, D)
    wos = [A(ret_w_o)[0], A(lru_w_o)[0], A(mla_w_o)[0], A(ret_w_o)[1]]
    shared = {}
    for i in range(DEPTH):
        shared["P%d_wo" % i] = _c(wos[i])
        shared["P%d_w1" % i] = _c(A(mlp_w1)[i])
        shared["P%d_w2" % i] = _c(A(mlp_w2)[i])
        shared["P%d_lng" % i] = _ln_arr(A(ln_g)[i])
        shared["P%d_lnb" % i] = _ln_arr(A(ln_b)[i])
    ins = []
    for c in range(NCORE):
        b, g = c // 4, c % 4
        d = dict(shared)
        d["xT"] = _c(xf[c * SEG:(c + 1) * SEG].T)
        d["pos"] = np.ascontiguousarray(positions[b:b + 1], dtype=np.int32)
        for i in range(DEPTH):
            kind, j = i % 3, i // 3
            if kind == 0:
                w = _ret_inputs(A(ret_w_in)[j], A(ret_gn_g)[j], g)
                if i == 0:
                    for k, v in ret_consts(g).items():
                        d["retc_" + k] = v
            elif kind == 1:
                w = _lru_inputs(A(lru_w_in)[j], A(lru_conv_w)[j], A(lru_conv_b)[j], A(lru_w_a)[j], A(lru_b_a)[j],
                                A(lru_w_x)[j], A(lru_b_x)[j], A(lru_lam)[j], g)
            else:
                w = _mla_inputs(A(mla_w_in)[j], A(mla_q_norm)[j], A(mla_kv_norm)[j], A(mla_w_uq)[j], A(mla_w_ukv)[j], g)
            for k, v in w.items():
                d["L%d_%s" % (i, k)] = v
        ins.append(d)
    res = run_bass_kernel_spmd(nc, ins, core_ids=list(range(NCORE))).results
    out = np.empty((B * S, D), np.float32)
    for c in range(NCORE):
        out[c * SEG:(c + 1) * SEG] = res[c]["oT"].T
    return out.reshape(B, S, D)
```

```python
import numpy as np
from contextlib import ExitStack
import concourse.bass as bass
import concourse.mybir as mybir
from concourse.bass_utils import run_bass_kernel_spmd

F32 = mybir.dt.float32
BF16 = mybir.dt.bfloat16
I32 = mybir.dt.int32
AF = mybir.ActivationFunctionType
ALU = mybir.AluOpType
AX = mybir.AxisListType

D = 1024
S = 8192
B = 2
DEPTH = 4
ALPHA = (2.0 * DEPTH) ** 0.25
LN_EPS = 1e-5
NCORE = 8
TT = 512
SEG = 2048

ENGS = ("pe", "act", "dve", "pool", "sp")
SAME_ENG_SYNC = True


class Ins:
    __slots__ = ("eng", "fn", "deps", "sig", "sem", "val", "dma", "gidx")

    def __init__(self, eng, fn, dma, gidx):
        self.eng = eng
        self.fn = fn
        self.deps = set()
        self.sig = False
        self.sem = None
        self.val = 0
        self.dma = dma
        self.gidx = gidx


_PHASE_ID = [0]


class Prog:
    def __init__(self, nc, n_dma_sems=24):
        self.nc = nc
        _PHASE_ID[0] += 1
        self.pid = _PHASE_ID[0]
        self.all = []
        self.last_w = {}
        self.readers = {}
        self.stack = ExitStack()
        self.n_dma_sems = n_dma_sems
        self.out_dmas = []

    def sb(self, name, shape, dt):
        return self.stack.enter_context(self.nc.sbuf_tensor("%s_p%d" % (name, self.pid), list(shape), dt))

    def ps(self, name, shape, dt=F32):
        return self.stack.enter_context(self.nc.psum_tensor("%s_p%d" % (name, self.pid), list(shape), dt))

    def emit(self, eng, fn, r=(), w=(), dma=False, out=False):
        ins = Ins(eng, fn, dma, len(self.all))
        deps = set()
        for k in r:
            lw = self.last_w.get(k)
            if lw is not None:
                deps.add(lw)
        for k in w:
            lw = self.last_w.get(k)
            if lw is not None:
                deps.add(lw)
            for rd in self.readers.get(k, ()):
                deps.add(rd)
        for k in r:
            self.readers.setdefault(k, []).append(ins)
        for k in w:
            self.last_w[k] = ins
            self.readers[k] = []
        deps.discard(ins)
        for d in deps:
            if d.eng == eng and not d.dma:
                if eng == "pe" or eng == "sp" or not SAME_ENG_SYNC:
                    continue
            ins.deps.add(d)
        self.all.append(ins)
        if out:
            self.out_dmas.append(ins)
        return ins

    def dma(self, out, in_, r=(), w=(), eng="sp", is_out=False, slow=False):
        if slow:
            return self.emit(eng, lambda e: e.dma_start(out=out, in_=in_, allow_slow_non_contiguous=True),
                             r, w, dma=True, out=is_out)
        return self.emit(eng, lambda e: e.dma_start(out=out, in_=in_), r, w, dma=True, out=is_out)

    def mm(self, out, lhsT, rhs, start, stop, r=(), w=()):
        return self.emit("pe", lambda e: e.matmul(out, lhsT, rhs, start=start, stop=stop), r, w)

    def act(self, out, in_, func, r=(), w=(), bias=None, scale=None, eng="act"):
        kw = {}
        if bias is not None:
            kw["bias"] = bias
        if scale is not None:
            kw["scale"] = scale
        return self.emit("act", lambda e: e.activation(out, in_, func, **kw), r, w)

    def tt(self, eng, out, in0, in1, op, r=(), w=()):
        return self.emit(eng, lambda e: e.tensor_tensor(out, in0, in1, op), r, w)

    def ts(self, eng, out, in0, s1, s2, op0, op1=None, r=(), w=()):
        if op1 is None:
            return self.emit(eng, lambda e: e.tensor_scalar(out, in0, s1, None, op0), r, w)
        return self.emit(eng, lambda e: e.tensor_scalar(out, in0, s1, s2, op0, op1), r, w)

    def stt(self, out, in0, scalar, in1, op0, op1, r=(), w=()):
        return self.emit("dve", lambda e: e.scalar_tensor_tensor(out, in0, scalar, in1, op0, op1), r, w)

    def copy(self, eng, out, in_, r=(), w=()):
        if eng == "act":
            return self.emit("act", lambda e: e.activation(out, in_, AF.Copy), r, w)
        return self.emit(eng, lambda e: e.tensor_copy(out, in_), r, w)

    def memset(self, eng, ap, val, w=()):
        return self.emit(eng, lambda e: e.memset(ap, val), (), w)

    def finalize(self):
        nc = self.nc
        dma_prev = [None] * self.n_dma_sems
        di = 0
        for ins in self.all:
            if ins.dma:
                s = di % self.n_dma_sems
                di += 1
                ins.sem = ("dma", s)
                ins.sig = True
                if dma_prev[s] is not None:
                    ins.deps.add(dma_prev[s])
                dma_prev[s] = ins
        for ins in self.all:
            for d in ins.deps:
                d.sig = True
        cnt = {}
        for ins in self.all:
            if not ins.sig:
                continue
            if not ins.dma:
                ins.sem = ("eng", ins.eng)
            c = cnt.get(ins.sem, 0) + (16 if ins.dma else 1)
            cnt[ins.sem] = c
            ins.val = c
        sems = {}
        for key in sorted(cnt, key=str):
            sems[key] = self.stack.enter_context(nc.semaphore("s%d_%s_%s" % ((self.pid,) + key)))
        per = {e: [i for i in self.all if i.eng == e] for e in ENGS}
        final = list(self.out_dmas)

        def run(engobj, lst, extra_final=()):
            have = {}
            for ins in lst:
                need = {}
                for d in ins.deps:
                    if need.get(d.sem, 0) < d.val:
                        need[d.sem] = d.val
                for sk, v in need.items():
                    if have.get(sk, 0) < v:
                        engobj.wait_ge(sems[sk], v)
                        have[sk] = v
                bi = ins.fn(engobj)
                if ins.sig:
                    bi.then_inc(sems[ins.sem], 16 if ins.dma else 1)
            for d in extra_final:
                if have.get(d.sem, 0) < d.val:
                    engobj.wait_ge(sems[d.sem], d.val)
                    have[d.sem] = d.val

        with nc.Block() as block:
            @block.gpsimd
            def _(e):
                for s in sems.values():
                    e.sem_clear(s)

        with nc.Block() as block:
            @block.sync
            def _(e):
                run(e, per["sp"], final)

            if per["pe"]:
                @block.tensor
                def _(e):
                    run(e, per["pe"])
            if per["act"]:
                @block.scalar
                def _(e):
                    run(e, per["act"])
            if per["dve"]:
                @block.vector
                def _(e):
                    run(e, per["dve"])
            if per["pool"]:
                @block.gpsimd
                def _(e):
                    run(e, per["pool"])
        self.stack.close()


class Ctx:
    def __init__(self, P, nbanks=8, ln_banks=(6, 7), bf_bank=False, n_wst=2, n_wbf=2):
        self.P = P
        self.psb = [P.ps("psb%d" % i, [128, 512]) for i in range(nbanks)]
        self.pst = P.ps("pst", [128, 1024], BF16) if bf_bank else None
        self.ln_banks = ln_banks
        self.wst = [P.sb("wst%d" % i, [128, 8, 512], F32) for i in range(n_wst)]
        self.wbf = [P.sb("wbf%d" % i, [128, 8, 512], BF16) for i in range(n_wbf)]
        self.wi = 0
        self.ones = P.sb("ones_f", [128, 128], F32)
        P.memset("pool", self.ones[:, :], 1.0, w=["ones_f"])
        self.sq = [P.sb("sq%d" % i, [128, 512], F32) for i in range(2)]
        self.sqi = 0
        self.st = {n: P.sb("st_" + n, [128, 512], F32) for n in ("mean", "msq", "var", "rstd", "nmr")}
        self.tmp = [P.sb("tmp%d" % i, [128, 512], F32) for i in range(2)]
        self.tmpi = 0

    def load_wblock(self, wdram_pkn, kc0, nkc, c0, ncol=512):
        P = self.P
        i = self.wi % 2
        self.wi += 1
        st, bf = self.wst[i], self.wbf[i]
        ks, kb = "wst%d" % i, "wbf%d" % i
        P.dma(st[:, 0:nkc, 0:ncol], wdram_pkn[:, kc0:kc0 + nkc, c0:c0 + ncol], w=[ks])
        a = (nkc * 3 + 7) // 8
        b = (nkc * 6 + 7) // 8
        for eng, lo, hi, sfx in (("pool", 0, a, "a"), ("act", a, b, "b"), ("dve", b, nkc, "c")):
            if hi > lo:
                P.copy(eng, bf[:, lo:hi, 0:ncol], st[:, lo:hi, 0:ncol], r=[ks], w=[kb + sfx])
        return bf, (lambda kc: kb + ("a" if kc < a else ("b" if kc < b else "c")))

    def load_resident(self, wdram_pkn, dst, dkey, nkc, ncol):
        P = self.P
        i = self.wi % len(self.wst)
        self.wi += 1
        st = self.wst[i]
        ks = "wst%d" % i
        P.dma(st[:, 0:nkc, 0:ncol], wdram_pkn[:, 0:nkc, 0:ncol], w=[ks])
        P.copy("pool", dst[:, 0:nkc, 0:ncol], st[:, 0:nkc, 0:ncol], r=[ks], w=[dkey])


def layer_norm_tile(C, z, zkey, g_ap, b_ap, out_f, okey_f, out_b, okey_b, nch=8, T=TT, eps=LN_EPS, gkey="lnp"):
    P = C.P
    nfeat = float(nch * 128)
    ps1, k1 = C.psb[C.ln_banks[0]], "psb%d" % C.ln_banks[0]
    ps2, k2 = C.psb[C.ln_banks[1]], "psb%d" % C.ln_banks[1]
    for c in range(nch):
        sq = C.sq[C.sqi % 2]
        ksq = "sq%d" % (C.sqi % 2)
        C.sqi += 1
        P.act(sq[:, 0:T], z[:, c, 0:T], AF.Square, r=[zkey + str(c)], w=[ksq])
        P.mm(ps1[:, 0:T], C.ones[:, :], z[:, c, 0:T], c == 0, c == nch - 1, r=["ones_f", zkey + str(c)], w=[k1])
        P.mm(ps2[:, 0:T], C.ones[:, :], sq[:, 0:T], c == 0, c == nch - 1, r=["ones_f", ksq], w=[k2])
    mean, msq, var, rstd, nmr = (C.st[n] for n in ("mean", "msq", "var", "rstd", "nmr"))
    P.act(mean[:, 0:T], ps1[:, 0:T], AF.Copy, r=[k1], w=["st_mean"], scale=1.0 / nfeat)
    P.tt("dve", msq[:, 0:T], mean[:, 0:T], mean[:, 0:T], ALU.mult, r=["st_mean"], w=["st_msq"])
    P.stt(var[:, 0:T], ps2[:, 0:T], 1.0 / nfeat, msq[:, 0:T], ALU.mult, ALU.subtract, r=[k2, "st_msq"], w=["st_var"])
    P.ts("dve", var[:, 0:T], var[:, 0:T], eps, None, ALU.add, r=["st_var"], w=["st_var"])
    P.act(rstd[:, 0:T], var[:, 0:T], AF.Ln, r=["st_var"], w=["st_rstd"])
    P.act(rstd[:, 0:T], rstd[:, 0:T], AF.Exp, r=["st_rstd"], w=["st_rstd"], scale=-0.5)
    P.stt(nmr[:, 0:T], mean[:, 0:T], -1.0, rstd[:, 0:T], ALU.mult, ALU.mult, r=["st_mean", "st_rstd"], w=["st_nmr"])
    for c in range(nch):
        t = C.tmp[C.tmpi % 2]
        kt = "tmp%d" % (C.tmpi % 2)
        C.tmpi += 1
        P.tt("dve", t[:, 0:T], z[:, c, 0:T], rstd[:, 0:T], ALU.mult, r=[zkey + str(c), "st_rstd"], w=[kt])
        P.tt("pool", t[:, 0:T], t[:, 0:T], nmr[:, 0:T], ALU.add, r=[kt, "st_nmr"], w=[kt])
        if b_ap is not None:
            P.act(out_f[:, c, 0:T], t[:, 0:T], AF.Identity, r=[kt, gkey], w=[okey_f + str(c)],
                  bias=b_ap[:, c:c + 1], scale=g_ap[:, c:c + 1])
        else:
            P.act(out_f[:, c, 0:T], t[:, 0:T], AF.Copy, r=[kt, gkey], w=[okey_f + str(c)],
                  scale=g_ap[:, c:c + 1])
        if out_b is not None:
            P.copy("pool", out_b[:, c, 0:T], out_f[:, c, 0:T], r=[okey_f + str(c)], w=[okey_b + str(c)])


def build_post(F, ntok=SEG):
    nc = bass.Bass("TRN2", target_bir_lowering=False)
    FC = F // 128
    yT = nc.dram_tensor("yT", [F, ntok], F32, kind="ExternalInput").ap()
    rT = nc.dram_tensor("rT", [D, ntok], F32, kind="ExternalInput").ap()
    W = {}
    for n, shp in (("wo", [F, D]), ("w1", [D, 4 * D]), ("w2", [4 * D, D]), ("lng", [128, 2, 8]), ("lnb", [128, 2, 8])):
        W[n] = nc.dram_tensor(n, shp, F32, kind="ExternalInput").ap()
    oT = nc.dram_tensor("oT", [D, ntok], F32, kind="ExternalOutput").ap()
    yv = yT.rearrange("(c p) t -> p c t", p=128)
    rv = rT.rearrange("(c p) t -> p c t", p=128)
    ov = oT.rearrange("(c p) t -> p c t", p=128)
    emit_post(nc, W, list(range(FC)), FC,
              ysrc=lambda e, cache, t, c0, n: yv[:, c0:c0 + n, t * TT:(t + 1) * TT], y_bf16=False,
              resid=None, rsrc=lambda t: rv[:, :, t * TT:(t + 1) * TT],
              odst=lambda t: ov[:, :, t * TT:(t + 1) * TT], hdst=None, ntok=ntok)
    return nc


def emit_post(nc, W, ychunks, FCY, ysrc, y_bf16, resid, rsrc, odst, hdst, ntok=SEG):
    wo, w1, w2, lng, lnb = (W[k] for k in ("wo", "w1", "w2", "lng", "lnb"))
    FW = len(ychunks)
    P = Prog(nc)
    C = Ctx(P)
    wov = wo.rearrange("(c p) n -> p c n", p=128)
    w1v = w1.rearrange("(c p) n -> p c n", p=128)
    w2v = w2.rearrange("(c p) n -> p c n", p=128)
    lnp_g = P.sb("lnp_g", [128, 2, 8], F32)
    lnp_b = P.sb("lnp_b", [128, 2, 8], F32)
    P.dma(lnp_g[:, :, :], lng, w=["lnp"])
    P.dma(lnp_b[:, :, :], lnb, w=["lnp"])
    yst = None if y_bf16 else P.sb("yst", [128, 8, TT], F32)
    yb = P.sb("yb", [128, FCY, TT], BF16)
    zloc = None if resid is not None else P.sb("z", [128, 8, TT], F32)
    hb = P.sb("hb", [128, 8, TT], BF16)
    hid = P.sb("hid", [128, 32, TT], BF16)
    rl = [P.sb("rl%d" % i, [128, TT], F32) for i in range(2)]
    ycache = {}
    NT = ntok // TT
    blocks = [(k0, min(8, FW - k0)) for k0 in range(0, FW, 8)]

    def zk(t):
        return "z%d_" % t

    def ztile(t):
        return zloc if resid is None else resid[:, :, t * TT:(t + 1) * TT]

    def wo_phase(t):
        z = ztile(t)
        for c0 in range(0, FCY, 8):
            n = min(8, FCY - c0)
            if y_bf16:
                P.emit("sp", lambda e, t=t, c0=c0, n=n: e.dma_start(out=yb[:, c0:c0 + n, :], in_=ysrc(e, ycache, t, c0, n)),
                       (), ["yb"], dma=True)
            else:
                P.emit("sp", lambda e, t=t, c0=c0, n=n: e.dma_start(out=yst[:, 0:n, :], in_=ysrc(e, ycache, t, c0, n)),
                       (), ["yst"], dma=True)
                P.copy("dve", yb[:, c0:c0 + n, :], yst[:, 0:n, :], r=["yst"], w=["yb"])
        if resid is None:
            P.dma(z[:, :, :], rsrc(t), w=[zk(t) + str(i) for i in range(8)], eng="pool")
        for half in range(2):
            for bi, (k0, n) in enumerate(blocks):
                wb, kb = C.load_wblock(wov, k0, n, half * 512)
                for m in range(4):
                    for kc in range(n):
                        P.mm(C.psb[m][:, :], wb[:, kc, m * 128:(m + 1) * 128], yb[:, ychunks[k0 + kc], :],
                             bi == 0 and kc == 0, bi == len(blocks) - 1 and kc == n - 1,
                             r=[kb(kc), "yb"], w=["psb%d" % m])
            for m in range(4):
                c = half * 4 + m
                P.stt(z[:, c, :], z[:, c, :], ALPHA, C.psb[m][:, :], ALU.mult, ALU.add,
                      r=[zk(t) + str(c), "psb%d" % m], w=[zk(t) + str(c)])

    overlap = resid is not None
    if overlap:
        wo_phase(0)
    for t in range(NT):
        if not overlap:
            wo_phase(t)
        z = ztile(t)
        Z8 = [zk(t) + str(i) for i in range(8)]
        layer_norm_tile(C, z, zk(t), lnp_g[:, 0, :], lnp_b[:, 0, :], z, zk(t), hb, "hb")
        for nb in range(8):
            wb, kb = C.load_wblock(w1v, 0, 8, nb * 512)
            for m in range(4):
                pb = C.psb[m]
                kp = "psb%d" % m
                for kc in range(8):
                    P.mm(pb[:, :], wb[:, kc, m * 128:(m + 1) * 128], hb[:, kc, :], kc == 0, kc == 7,
                         r=[kb(kc), "hb%d" % kc], w=[kp])
                rr = rl[m % 2]
                kr = "rl%d" % (m % 2)
                P.act(rr[:, :], pb[:, :], AF.Relu, r=[kp], w=[kr])
                P.tt("dve", hid[:, nb * 4 + m, :], rr[:, :], rr[:, :], ALU.mult, r=[kr], w=["hid"])
        for half in range(2):
            for kb4 in range(4):
                wb, kb = C.load_wblock(w2v, kb4 * 8, 8, half * 512)
                for m in range(4):
                    for kc in range(8):
                        P.mm(C.psb[m][:, :], wb[:, kc, m * 128:(m + 1) * 128], hid[:, kb4 * 8 + kc, :],
                             kb4 == 0 and kc == 0, kb4 == 3 and kc == 7, r=[kb(kc), "hid"], w=["psb%d" % m])
            for m in range(4):
                c = half * 4 + m
                P.stt(z[:, c, :], z[:, c, :], ALPHA, C.psb[m][:, :], ALU.mult, ALU.add,
                      r=[zk(t) + str(c), "psb%d" % m], w=[zk(t) + str(c)])
        if overlap and t + 1 < NT:
            wo_phase(t + 1)
        layer_norm_tile(C, z, zk(t), lnp_g[:, 1, :], lnp_b[:, 1, :], z, zk(t), hb if hdst is not None else None, "hb")
        if odst is not None:
            P.dma(odst(t), z[:, :, :], r=Z8, is_out=True)
        if hdst is not None:
            P.dma(hdst(t), hb[:, :, :], r=["hb%d" % i for i in range(8)], is_out=True)
    P.finalize()


def rope_tables(P, pos_dram, t0, T, inv_ap, bufs, sign_ap=None):
    import math
    npart = bufs["np"]
    posi, a, b, ki, cs, sn = (bufs[k] for k in ("posi", "a", "b", "ki", "cs", "sn"))
    sl = slice(0, npart)
    P.dma(posi[sl, 0:T], bass.AP(pos_dram, t0, [[0, npart], [1, T]]), w=["rt_posi"])
    P.copy("dve", a[sl, 0:T], posi[sl, 0:T], r=["rt_posi"], w=["rt_a"])
    P.ts("dve", a[sl, 0:T], a[sl, 0:T], inv_ap, None, ALU.mult, r=["rt_a", "rt_inv"], w=["rt_a"])
    P.ts("dve", b[sl, 0:T], a[sl, 0:T], 1.0 / (2 * math.pi), None, ALU.mult, r=["rt_a"], w=["rt_b"])
    P.copy("dve", ki[sl, 0:T], b[sl, 0:T], r=["rt_b"], w=["rt_ki"])
    P.copy("dve", b[sl, 0:T], ki[sl, 0:T], r=["rt_ki"], w=["rt_b"])
    c1 = float(np.float32(6.28125))
    c2 = float(2 * math.pi - 6.28125)
    P.stt(a[sl, 0:T], b[sl, 0:T], -c1, a[sl, 0:T], ALU.mult, ALU.add, r=["rt_a", "rt_b"], w=["rt_a"])
    P.stt(a[sl, 0:T], b[sl, 0:T], -c2, a[sl, 0:T], ALU.mult, ALU.add, r=["rt_a", "rt_b"], w=["rt_a"])
    P.ts("dve", b[sl, 0:T], a[sl, 0:T], math.pi / 2, None, ALU.add, r=["rt_a"], w=["rt_b"])
    P.ts("dve", cs[sl, 0:T], b[sl, 0:T], math.pi, None, ALU.is_gt, r=["rt_b"], w=["rt_cs"])
    P.stt(b[sl, 0:T], cs[sl, 0:T], -2 * math.pi, b[sl, 0:T], ALU.mult, ALU.add, r=["rt_b", "rt_cs"], w=["rt_b"])
    lim = 3.14159
    P.ts("dve", b[sl, 0:T], b[sl, 0:T], lim, -lim, ALU.min, ALU.max, r=["rt_b"], w=["rt_b"])
    P.ts("dve", a[sl, 0:T], a[sl, 0:T], lim, -lim, ALU.min, ALU.max, r=["rt_a"], w=["rt_a"])
    P.act(cs[sl, 0:T], b[sl, 0:T], AF.Sin, r=["rt_b"], w=["rt_cs"])
    if sign_ap is None:
        P.act(sn[sl, 0:T], a[sl, 0:T], AF.Sin, r=["rt_a"], w=["rt_sn"])
    else:
        P.act(sn[sl, 0:T], a[sl, 0:T], AF.Sin, r=["rt_a", "rt_inv"], w=["rt_sn"], scale=sign_ap)


def make_identity_bf(P, idb, key):
    P.memset("pool", idb[:, :], 0.0, w=[key])
    P.emit("pool", lambda e: e.affine_select(idb[:, :], idb[:, :], [[-1, 128]], ALU.not_equal, 1.0,
                                             base=0, channel_multiplier=1), r=[key], w=[key])


def build_ret(nseq=S):
    nc = bass.Bass("TRN2", target_bir_lowering=False)
    W = {}
    hT = nc.dram_tensor("hT", [D, nseq], F32, kind="ExternalInput").ap()
    for n, shp in (("wq", [D, 256]), ("wk", [D, 256]), ("wv", [D, 512]), ("wg", [D, 512]), ("gng", [128, 4]),
                   ("inv", [128, 1]), ("maskT", [128, 2048]), ("qdec", [128, 512]), ("kdec", [128, 4]), ("cdec", [128, 1])):
        W[n] = nc.dram_tensor(n, shp, F32, kind="ExternalInput").ap()
    W["pos"] = nc.dram_tensor("pos", [1, nseq], I32, kind="ExternalInput")
    yT = nc.dram_tensor("yT", [512, nseq], F32, kind="ExternalOutput").ap()
    hv = hT.rearrange("(c p) t -> p c t", p=128)
    yv = yT.rearrange("(c p) t -> p c t", p=128)
    emit_ret(nc, W, lambda t: hv[:, :, t * TT:(t + 1) * TT], lambda t: yv[:, :, t * TT:(t + 1) * TT], nseq, False, False)
    return nc


def emit_ret(nc, W, hsrc, ydst, nseq, in_bf16, out_bf16):
    wq, wk, wv, wg, gng_d, inv_d, maskT_d, qdec_d, kdec_d, cdec_d, pos = (
        W[k] for k in ("wq", "wk", "wv", "wg", "gng", "inv", "maskT", "qdec", "kdec", "cdec", "pos"))
    P = Prog(nc)
    C = Ctx(P, nbanks=7, ln_banks=(4, 5), bf_bank=True, n_wst=1, n_wbf=0)
    pk = lambda w: w.rearrange("(c p) n -> p c n", p=128)
    wqb = P.sb("wqb", [128, 8, 256], BF16)
    wkb = P.sb("wkb", [128, 8, 256], BF16)
    wvb = P.sb("wvb", [128, 8, 512], BF16)
    wgb = P.sb("wgb", [128, 8, 512], BF16)
    C.load_resident(pk(wq), wqb, "wqb", 8, 256)
    C.load_resident(pk(wk), wkb, "wkb", 8, 256)
    C.load_resident(pk(wv), wvb, "wvb", 8, 512)
    C.load_resident(pk(wg), wgb, "wgb", 8, 512)
    gng = P.sb("gng_s", [128, 4], F32)
    inv = P.sb("inv_s", [128, 1], F32)
    maskT = P.sb("maskT_s", [128, 4, 512], F32)
    qdec = P.sb("qdec_s", [128, 512], F32)
    kdec = P.sb("kdec_s", [128, 4], F32)
    cdec = P.sb("cdec_s", [128, 1], F32)
    P.dma(gng[:, :], gng_d, w=["gng"])
    P.dma(inv[:, :], inv_d, w=["rt_inv"])
    P.dma(maskT[:, :, :], maskT_d.rearrange("p (k t) -> p k t", k=4), w=["maskT"])
    P.dma(qdec[:, :], qdec_d, w=["qdec"])
    P.dma(kdec[:, :], kdec_d, w=["kdec"])
    P.dma(cdec[:, :], cdec_d, w=["cdec"])
    idb = P.sb("idb", [128, 128], BF16)
    make_identity_bf(P, idb, "idb")
    rb = {"np": 128, "posi": P.sb("rt_posi", [128, TT], I32), "a": P.sb("rt_a", [128, TT], F32),
          "b": P.sb("rt_b", [128, TT], F32), "ki": P.sb("rt_ki", [128, TT], I32),
          "cs": P.sb("rt_cs", [128, TT], F32), "sn": P.sb("rt_sn", [128, TT], F32)}
    csk = P.sb("csk", [128, TT], F32)
    snk = P.sb("snk", [128, TT], F32)
    xst = C.wst[0]
    xb = P.sb("xb", [128, 8, TT], BF16)
    yob = P.sb("yob", [128, 4, TT], BF16) if out_bf16 else None
    qf = P.sb("qf", [128, 2, TT], F32)
    qb = P.sb("qb", [128, 2, TT], BF16)
    qdb = P.sb("qdb", [128, 2, TT], BF16)
    kb = P.sb("kb", [128, 2, TT], BF16)
    kdT = P.sb("kdT", [128, 4, 256], BF16)
    vb = P.sb("vb", [128, 4, 512], BF16)
    sg = P.sb("sg", [128, 4, TT], F32)
    of = P.sb("of", [128, 4, TT], F32)
    ra = P.sb("ra", [128, TT], F32)
    rbb = P.sb("rbb", [128, TT], F32)
    pT = P.sb("pT", [128, 4, TT], BF16)
    st = P.sb("st", [128, 2, 512], F32)
    stb = P.sb("stb", [128, 2, 512], BF16)
    P.memset("pool", st[:, :, :], 0.0, w=["st0", "st1"])
    P.memset("pool", stb[:, :, :], 0.0, w=["stb0", "stb1"])
    X8 = ["xb%d" % i for i in range(8)]
    for t in range(nseq // TT):
        t0 = t * TT
        if in_bf16:
            for k in range(4):
                P.dma(xb[:, 2 * k:2 * k + 2, :], hsrc(t)[:, k, :, :], w=["xb%d" % (2 * k), "xb%d" % (2 * k + 1)])
        else:
            P.dma(xst[:, :, :], hsrc(t), w=["wst0"])
            for kc in range(8):
                P.copy("dve" if kc % 2 == 0 else "pool", xb[:, kc, :], xst[:, kc, :], r=["wst0"], w=["xb%d" % kc])
        rope_tables(P, pos, t0, TT, inv[:, 0:1], rb)
        cs, sn = rb["cs"], rb["sn"]
        P.ts("pool", csk[:, :], cs[:, :], 1.0 / 16.0, None, ALU.mult, r=["rt_cs"], w=["csk"])
        P.ts("pool", snk[:, :], sn[:, :], 1.0 / 16.0, None, ALU.mult, r=["rt_sn"], w=["snk"])
        for m in range(2):
            for kc in range(8):
                P.mm(C.psb[m][:, :], wqb[:, kc, m * 128:(m + 1) * 128], xb[:, kc, :], kc == 0, kc == 7,
                     r=["wqb", "xb%d" % kc], w=["psb%d" % m])
        P.tt("dve", ra[:, :], C.psb[0][:, :], cs[:, :], ALU.mult, r=["psb0", "rt_cs"], w=["ra"])
        P.tt("dve", rbb[:, :], C.psb[1][:, :], sn[:, :], ALU.mult, r=["psb1", "rt_sn"], w=["rbb"])
        P.tt("pool", qf[:, 0, :], ra[:, :], rbb[:, :], ALU.subtract, r=["ra", "rbb"], w=["qf0"])
        P.tt("dve", ra[:, :], C.psb[0][:, :], sn[:, :], ALU.mult, r=["psb0", "rt_sn"], w=["ra"])
        P.tt("dve", rbb[:, :], C.psb[1][:, :], cs[:, :], ALU.mult, r=["psb1", "rt_cs"], w=["rbb"])
        P.tt("pool", qf[:, 1, :], ra[:, :], rbb[:, :], ALU.add, r=["ra", "rbb"], w=["qf1"])
        for m in range(2):
            P.copy("act", qb[:, m, :], qf[:, m, :], r=["qf%d" % m], w=["qb%d" % m])
            P.tt("pool", qdb[:, m, :], qf[:, m, :], qdec[:, :], ALU.mult, r=["qf%d" % m, "qdec"], w=["qdb%d" % m])
        for m in range(2):
            for kc in range(8):
                P.mm(C.psb[2 + m][:, :], wkb[:, kc, m * 128:(m + 1) * 128], xb[:, kc, :], kc == 0, kc == 7,
                     r=["wkb", "xb%d" % kc], w=["psb%d" % (2 + m)])
        P.tt("dve", ra[:, :], C.psb[2][:, :], csk[:, :], ALU.mult, r=["psb2", "csk"], w=["ra"])
        P.tt("dve", rbb[:, :], C.psb[3][:, :], snk[:, :], ALU.mult, r=["psb3", "snk"], w=["rbb"])
        P.tt("pool", kb[:, 0, :], ra[:, :], rbb[:, :], ALU.subtract, r=["ra", "rbb"], w=["kb0"])
        P.tt("dve", ra[:, :], C.psb[2][:, :], snk[:, :], ALU.mult, r=["psb2", "snk"], w=["ra"])
        P.tt("dve", rbb[:, :], C.psb[3][:, :], csk[:, :], ALU.mult, r=["psb3", "csk"], w=["rbb"])
        P.tt("pool", kb[:, 1, :], ra[:, :], rbb[:, :], ALU.add, r=["ra", "rbb"], w=["kb1"])
        for m in range(4):
            bk = m % 4
            for kc in range(8):
                P.mm(C.psb[bk][:, :], wgb[:, kc, m * 128:(m + 1) * 128], xb[:, kc, :], kc == 0, kc == 7,
                     r=["wgb", "xb%d" % kc], w=["psb%d" % bk])
            P.act(sg[:, m, :], C.psb[bk][:, :], AF.Silu, r=["psb%d" % bk], w=["sg%d" % m])
        for i in range(4):
            bk = i % 4
            for kc in range(8):
                P.mm(C.psb[bk][:, :], xb[:, kc, i * 128:(i + 1) * 128], wvb[:, kc, :], kc == 0, kc == 7,
                     r=["wvb", "xb%d" % kc], w=["psb%d" % bk])
            P.copy("act", vb[:, i, :], C.psb[bk][:, :], r=["psb%d" % bk], w=["vb%d" % i])
        for i in range(4):
            for m in range(2):
                col = (i * 2 + m) * 128
                P.emit("pe", lambda e, col=col, m=m, i=i: e.transpose(C.pst[:, col:col + 128], kb[:, m, i * 128:(i + 1) * 128], idb[:, :]),
                       r=["kb%d" % m, "idb"], w=["pst"])
        for i in range(4):
            P.ts("dve", kdT[:, i, :], C.pst[:, i * 256:(i + 1) * 256], kdec[:, i:i + 1], None, ALU.mult,
                 r=["pst", "kdec"], w=["kdT%d" % i])
        for kblk in range(4):
            cl = slice(kblk * 128, (kblk + 1) * 128)
            qs = slice(kblk * 128, TT)
            bk = 4 + (kblk % 2)
            for m in range(2):
                P.mm(C.psb[bk][:, qs], kb[:, m, cl], qb[:, m, qs], m == 0, m == 1,
                     r=["kb%d" % m, "qb%d" % m], w=["psb%d" % bk])
            P.tt("dve", pT[:, kblk, qs], C.psb[bk][:, qs], maskT[:, kblk, qs], ALU.mult,
                 r=["psb%d" % bk, "maskT"], w=["pT%d" % kblk])
        for vc in range(4):
            vs = slice(vc * 128, (vc + 1) * 128)
            bk = 6 if vc % 2 == 0 else 3
            for m in range(2):
                P.mm(C.psb[bk][:, :], stb[:, m, vs], qdb[:, m, :], m == 0, False,
                     r=["stb%d" % m, "qdb%d" % m], w=["psb%d" % bk])
            for kblk in range(4):
                qs = slice(kblk * 128, TT)
                P.mm(C.psb[bk][:, qs], vb[:, kblk, vs], pT[:, kblk, qs], False, kblk == 3,
                     r=["vb%d" % kblk, "pT%d" % kblk], w=["psb%d" % bk])
            P.copy("act", of[:, vc, :], C.psb[bk][:, :], r=["psb%d" % bk], w=["of%d" % vc])
        for m in range(2):
            ds_ = slice(m * 128, (m + 1) * 128)
            bk = 4 + m
            for i in range(4):
                P.mm(C.psb[bk][:, :], kdT[:, i, ds_], vb[:, i, :], i == 0, i == 3, r=["kdT%d" % i, "vb%d" % i], w=["psb%d" % bk])
            P.stt(st[:, m, :], st[:, m, :], cdec[:, 0:1], C.psb[bk][:, :], ALU.mult, ALU.add,
                  r=["st%d" % m, "cdec", "psb%d" % bk], w=["st%d" % m])
            P.copy("pool", stb[:, m, :], st[:, m, :], r=["st%d" % m], w=["stb%d" % m])
        layer_norm_tile(C, of, "of", gng, None, of, "of", None, None, nch=4, gkey="gng")
        yo, ky = (yob, "yob") if out_bf16 else (of, "of")
        for c in range(4):
            P.tt("dve" if c % 2 == 0 else "pool", yo[:, c, :], of[:, c, :], sg[:, c, :], ALU.mult,
                 r=["of%d" % c, "sg%d" % c], w=[ky + str(c)])
        P.dma(ydst(t), yo[:, :, :], r=[ky + str(c) for c in range(4)], is_out=True)
    P.finalize()


def ret_consts(h):
    lg = np.log1p(-np.exp2(-5.0 - h))
    i = np.arange(512, dtype=np.float64)
    j = np.arange(128, dtype=np.float64)
    k = np.arange(4, dtype=np.float64)
    diff = i[None, None, :] - (128.0 * k[None, :, None] + j[:, None, None])
    maskT = np.where(diff >= 0, np.exp(lg * np.maximum(diff, 0.0)), 0.0).reshape(128, 2048)
    qdec = np.broadcast_to(np.exp(lg * (i + 1.0))[None, :], (128, 512))
    kdec = np.exp(lg * (511.0 - (128.0 * k[None, :] + j[:, None])))
    cdec = np.full((128, 1), np.exp(lg * 512.0))
    inv = (10000.0 ** (-np.arange(128, dtype=np.float32) / np.float32(128))).astype(np.float32)[:, None]
    f = lambda a: np.ascontiguousarray(a, dtype=np.float32)
    return dict(maskT=f(maskT), qdec=f(qdec), kdec=f(kdec), cdec=f(cdec), inv=f(inv))


NBLK = 3


def build_lru(nseq=S):
    nc = bass.Bass("TRN2", target_bir_lowering=False)
    W3 = NBLK * 128
    W = {}
    hT = nc.dram_tensor("hT", [D, nseq], F32, kind="ExternalInput").ap()
    for n, shp in (("wgt", [D, W3]), ("wrc", [D, W3]), ("wa", [128, NBLK, 128]), ("wx", [128, NBLK, 128]), ("vec", [128, NBLK, 8])):
        W[n] = nc.dram_tensor(n, shp, F32, kind="ExternalInput").ap()
    yT = nc.dram_tensor("yT", [W3, nseq], F32, kind="ExternalOutput").ap()
    hv = hT.rearrange("(c p) t -> p c t", p=128)
    yv = yT.rearrange("(c p) t -> p c t", p=128)
    emit_lru(nc, W, lambda t: hv[:, :, t * TT:(t + 1) * TT], lambda t: yv[:, :, t * TT:(t + 1) * TT], nseq, False, False)
    return nc


def emit_lru(nc, W, hsrc, ydst, nseq, in_bf16, out_bf16):
    W3 = NBLK * 128
    wgt, wrc, wa_d, wx_d, vec_d = (W[k] for k in ("wgt", "wrc", "wa", "wx", "vec"))
    P = Prog(nc)
    C = Ctx(P, nbanks=8, n_wst=1, n_wbf=0)
    pk = lambda w: w.rearrange("(c p) n -> p c n", p=128)
    wgb = P.sb("wgb", [128, 8, W3], BF16)
    wrb = P.sb("wrb", [128, 8, W3], BF16)
    C.load_resident(pk(wgt), wgb, "wgb", 8, W3)
    C.load_resident(pk(wrc), wrb, "wrb", 8, W3)
    waf = P.sb("waf", [128, NBLK, 128], F32)
    wxf = P.sb("wxf", [128, NBLK, 128], F32)
    wab = P.sb("wab", [128, NBLK, 128], BF16)
    wxb = P.sb("wxb", [128, NBLK, 128], BF16)
    P.dma(waf[:, :, :], wa_d, w=["waf"])
    P.dma(wxf[:, :, :], wx_d, w=["wxf"])
    P.copy("dve", wab[:, :, :], waf[:, :, :], r=["waf"], w=["wab"])
    P.copy("dve", wxb[:, :, :], wxf[:, :, :], r=["wxf"], w=["wxb"])
    vec = P.sb("vec_s", [128, NBLK, 8], F32)
    P.dma(vec[:, :, :], vec_d, w=["vec"])
    der = P.sb("der", [128, NBLK, 4], F32)
    P.act(der[:, :, 0], vec[:, :, 7], AF.Exp, r=["vec"], w=["der"], scale=-1.0)
    P.ts("dve", der[:, :, 0], der[:, :, 0], 1.0, None, ALU.add, r=["der"], w=["der"])
    P.act(der[:, :, 0], der[:, :, 0], AF.Ln, r=["der"], w=["der"])
    P.ts("dve", der[:, :, 1], der[:, :, 0], -8.0, None, ALU.mult, r=["der"], w=["der"])
    P.ts("dve", der[:, :, 2], der[:, :, 0], -16.0, None, ALU.mult, r=["der"], w=["der"])
    P.ts("dve", der[:, :, 3], der[:, :, 0], 8.0, None, ALU.mult, r=["der"], w=["der"])
    xst = C.wst[0]
    xb = P.sb("xb", [128, 8, TT], BF16)
    rec = P.sb("rec", [128, NBLK, TT + 4], F32)
    P.memset("pool", rec[:, :, :], 0.0, w=["rec%d" % b for b in range(NBLK)])
    hprev = P.sb("hprev", [128, NBLK], F32)
    P.memset("pool", hprev[:, :], 0.0, w=["hprev%d" % b for b in range(NBLK)])
    gt = P.sb("gt", [128, TT], F32)
    g2 = P.sb("g2", [128, TT], F32)
    u = P.sb("u", [128, TT], F32)
    ub = P.sb("ub", [128, TT], BF16)
    rr = P.sb("rr", [128, TT], F32)
    ii = P.sb("ii", [128, TT], F32)
    aa = P.sb("aa", [128, TT], F32)
    ee = P.sb("ee", [128, TT], F32)
    th = P.sb("th", [128, TT], F32)
    hs = P.sb("hs", [128, TT], F32)
    yo = P.sb("yo", [128, NBLK, TT], BF16 if out_bf16 else F32)
    for t in range(nseq // TT):
        t0 = t * TT
        if in_bf16:
            for k in range(4):
                P.dma(xb[:, 2 * k:2 * k + 2, :], hsrc(t)[:, k, :, :], w=["xb%d" % (2 * k), "xb%d" % (2 * k + 1)])
        else:
            P.dma(xst[:, :, :], hsrc(t), w=["wst0"])
            for kc in range(8):
                P.copy("dve" if kc % 2 == 0 else "pool", xb[:, kc, :], xst[:, kc, :], r=["wst0"], w=["xb%d" % kc])
        for b in range(NBLK):
            bs = slice(b * 128, (b + 1) * 128)
            kr = "rec%d" % b
            for kc in range(8):
                P.mm(C.psb[0][:, :], wgb[:, kc, bs], xb[:, kc, :], kc == 0, kc == 7, r=["wgb", "xb%d" % kc], w=["psb0"])
            P.act(g2[:, :], C.psb[0][:, :], AF.Square, r=["psb0"], w=["g2"])
            P.ts("dve", g2[:, :], g2[:, :], 0.044715, 1.0, ALU.mult, ALU.add, r=["g2"], w=["g2"])
            P.tt("dve", g2[:, :], g2[:, :], C.psb[0][:, :], ALU.mult, r=["g2", "psb0"], w=["g2"])
            P.act(g2[:, :], g2[:, :], AF.Sigmoid, r=["g2"], w=["g2"], scale=1.5957691216057308)
            P.tt("dve", gt[:, :], g2[:, :], C.psb[0][:, :], ALU.mult, r=["g2", "psb0"], w=["gt"])
            for kc in range(8):
                P.mm(C.psb[1][:, :], wrb[:, kc, bs], xb[:, kc, :], kc == 0, kc == 7, r=["wrb", "xb%d" % kc], w=["psb1"])
            P.copy("act", rec[:, b, 3:3 + TT], C.psb[1][:, :], r=["psb1"], w=[kr])
            P.ts("dve", u[:, :], rec[:, b, 3:3 + TT], vec[:, b, 3:4], vec[:, b, 4:5], ALU.mult, ALU.add, r=[kr, "vec"], w=["u"])
            for k in range(3):
                P.stt(u[:, :], rec[:, b, k:k + TT], vec[:, b, k:k + 1], u[:, :], ALU.mult, ALU.add, r=[kr, "vec", "u"], w=["u"])
            P.copy("pool", rec[:, b, 0:3], rec[:, b, TT:TT + 3], r=[kr], w=[kr])
            P.copy("pool", ub[:, :], u[:, :], r=["u"], w=["ub"])
            P.mm(C.psb[2][:, :], wab[:, b, :], ub[:, :], True, True, r=["wab", "ub"], w=["psb2"])
            P.mm(C.psb[3][:, :], wxb[:, b, :], ub[:, :], True, True, r=["wxb", "ub"], w=["psb3"])
            P.act(rr[:, :], C.psb[2][:, :], AF.Sigmoid, r=["psb2", "vec"], w=["rr"], bias=vec[:, b, 5:6])
            P.act(ii[:, :], C.psb[3][:, :], AF.Sigmoid, r=["psb3", "vec"], w=["ii"], bias=vec[:, b, 6:7])
            P.act(aa[:, :], rr[:, :], AF.Exp, r=["rr", "der"], w=["aa"], scale=der[:, b, 1:2])
            P.act(ee[:, :], rr[:, :], AF.Exp, r=["rr", "der"], w=["ee"], scale=der[:, b, 2:3])
            P.act(th[:, :], rr[:, :], AF.Tanh, r=["rr", "der"], w=["th"], scale=der[:, b, 3:4])
            P.stt(ee[:, :], ee[:, :], 1.0, th[:, :], ALU.add, ALU.mult, r=["ee", "th"], w=["ee"])
            P.act(ee[:, :], ee[:, :], AF.Sqrt, r=["ee"], w=["ee"])
            P.tt("pool", ii[:, :], ii[:, :], u[:, :], ALU.mult, r=["ii", "u"], w=["ii"])
            P.tt("pool", ii[:, :], ii[:, :], ee[:, :], ALU.mult, r=["ii", "ee"], w=["ii"])
            P.emit("dve", lambda e, b=b: e.tensor_tensor_scan(hs[:, :], aa[:, :], ii[:, :], hprev[:, b:b + 1], ALU.mult, ALU.add),
                   r=["aa", "ii", "hprev%d" % b], w=["hs"])
            P.copy("dve", hprev[:, b:b + 1], hs[:, TT - 1:TT], r=["hs"], w=["hprev%d" % b])
            P.tt("pool", yo[:, b, :], hs[:, :], gt[:, :], ALU.mult, r=["hs", "gt"], w=["yo%d" % b])
        P.dma(ydst(t), yo[:, :, :], r=["yo%d" % b for b in range(NBLK)], is_out=True)
    P.finalize()


def rms_norm_tile(C, z, zkey, g_ap, gkey, out_b, okey, nch, eps, bank, T=TT):
    P = C.P
    ps, kp = C.psb[bank], "psb%d" % bank
    for c in range(nch):
        sq = C.sq[C.sqi % 2]
        ksq = "sq%d" % (C.sqi % 2)
        C.sqi += 1
        P.act(sq[:, 0:T], z[:, c, 0:T], AF.Square, r=[zkey + str(c)], w=[ksq])
        P.mm(ps[:, 0:T], C.ones[:, :], sq[:, 0:T], c == 0, c == nch - 1, r=["ones_f", ksq], w=[kp])
    rstd = C.st["rstd"]
    P.ts("dve", rstd[:, 0:T], ps[:, 0:T], 1.0 / (nch * 128.0), eps, ALU.mult, ALU.add, r=[kp], w=["st_rstd"])
    P.act(rstd[:, 0:T], rstd[:, 0:T], AF.Ln, r=["st_rstd"], w=["st_rstd"])
    P.act(rstd[:, 0:T], rstd[:, 0:T], AF.Exp, r=["st_rstd"], w=["st_rstd"], scale=-0.5)
    for c in range(nch):
        t = C.tmp[C.tmpi % 2]
        kt = "tmp%d" % (C.tmpi % 2)
        C.tmpi += 1
        P.tt("dve", t[:, 0:T], z[:, c, 0:T], rstd[:, 0:T], ALU.mult, r=[zkey + str(c), "st_rstd"], w=[kt])
        P.ts("pool", out_b[:, c, 0:T], t[:, 0:T], g_ap[:, c:c + 1], None, ALU.mult, r=[kt, gkey], w=[okey + str(c)])


QSCALE = 192.0 ** -0.5
RMS_EPS = 1e-6


def build_mla(nseq=S):
    nc = bass.Bass("TRN2", target_bir_lowering=False)
    W = {}
    hT = nc.dram_tensor("hT", [D, nseq], F32, kind="ExternalInput").ap()
    for n, shp in (("wcq", [D, 384]), ("wckv", [D, 256]), ("wkpe", [D, 128]), ("wuqn", [384, 256]), ("wuqr", [384, 256]),
                   ("wuk", [256, 256]), ("wuv", [256, 256]), ("qng", [128, 3]), ("kvng", [128, 2]), ("inv", [64, 2]),
                   ("mask", [128, 4 * 512])):
        W[n] = nc.dram_tensor(n, shp, F32, kind="ExternalInput").ap()
    W["pos"] = nc.dram_tensor("pos", [1, nseq], I32, kind="ExternalInput")
    oT = nc.dram_tensor("oT", [256, nseq], F32, kind="ExternalOutput").ap()
    hv = hT.rearrange("(c p) t -> p c t", p=128)
    ov = oT.rearrange("(c p) t -> p c t", p=128)
    emit_mla(nc, W, lambda t: hv[:, :, t * TT:(t + 1) * TT], lambda t: ov[:, :, t * TT:(t + 1) * TT], nseq, False, False)
    return nc


def emit_mla(nc, W, hsrc, ydst, nseq, in_bf16, out_bf16):
    NKB = nseq // 128
    wcq, wckv, wkpe, wuqn, wuqr, wuk, wuv, qng_d, kvng_d, inv_d, mask_d, pos = (
        W[k] for k in ("wcq", "wckv", "wkpe", "wuqn", "wuqr", "wuk", "wuv", "qng", "kvng", "inv", "mask", "pos"))
    P = Prog(nc)
    C = Ctx(P, nbanks=8, n_wst=1, n_wbf=0)
    pk = lambda w: w.rearrange("(c p) n -> p c n", p=128)
    wcqb = P.sb("wcqb", [128, 8, 384], BF16)
    wckvb = P.sb("wckvb", [128, 8, 256], BF16)
    wkpeb = P.sb("wkpeb", [128, 8, 128], BF16)
    wuqnb = P.sb("wuqnb", [128, 3, 256], BF16)
    wuqrb = P.sb("wuqrb", [128, 3, 256], BF16)
    wukb = P.sb("wukb", [128, 2, 256], BF16)
    wuvb = P.sb("wuvb", [128, 2, 256], BF16)
    C.load_resident(pk(wcq), wcqb, "wcqb", 8, 384)
    C.load_resident(pk(wckv), wckvb, "wckvb", 8, 256)
    C.load_resident(pk(wkpe), wkpeb, "wkpeb", 8, 128)
    C.load_resident(pk(wuqn), wuqnb, "wuqnb", 3, 256)
    C.load_resident(pk(wuqr), wuqrb, "wuqrb", 3, 256)
    C.load_resident(pk(wuk), wukb, "wukb", 2, 256)
    C.load_resident(pk(wuv), wuvb, "wuvb", 2, 256)
    qng = P.sb("qng_s", [128, 3], F32)
    kvng = P.sb("kvng_s", [128, 2], F32)
    inv = P.sb("inv_s", [64, 2], F32)
    P.dma(qng[:, :], qng_d, w=["qng"])
    P.dma(kvng[:, :], kvng_d, w=["kvng"])
    P.dma(inv[:, :], inv_d, w=["rt_inv"])
    maskb = P.sb("maskb", [128, 4, 512], BF16)
    C.load_resident(mask_d.rearrange("p (c n) -> p c n", c=4), maskb, "maskb", 4, 512)
    onesb = P.sb("onesb", [128, 128], BF16)
    P.memset("pool", onesb[:, :], 1.0, w=["onesb"])
    rb = {"np": 64, "posi": P.sb("rt_posi", [64, TT], I32), "a": P.sb("rt_a", [64, TT], F32),
          "b": P.sb("rt_b", [64, TT], F32), "ki": P.sb("rt_ki", [64, TT], I32),
          "cs": P.sb("rt_cs", [64, TT], F32), "sn": P.sb("rt_sn", [64, TT], F32)}
    knT = P.sb("knT", [128, 2, nseq], BF16)
    kpe = P.sb("kpe", [64, nseq], BF16)
    vb = P.sb("vb", [128, NKB, 256], BF16)
    xst = C.wst[0]
    xb = P.sb("xb", [128, 8, TT], BF16)
    cqf = P.sb("cqf", [128, 3, TT], F32)
    ckvf = P.sb("ckvf", [128, 2, TT], F32)
    cqn = P.sb("cqn", [128, 3, TT], BF16)
    ckvn = P.sb("ckvn", [128, 2, TT], BF16)
    ra = P.sb("ra", [64, TT], F32)
    rbb = P.sb("rbb", [64, TT], F32)
    qnb = P.sb("qnb", [128, 2, TT], BF16)
    qpb = P.sb("qpb", [64, 2, TT], BF16)
    pTs = [P.sb("pT%d" % i, [128, TT], BF16) for i in range(2)]
    rl = P.sb("rl", [128, TT], F32)
    ob = P.sb("ob", [128, 2, TT], BF16 if out_bf16 else F32)
    pti = 0
    for t in range(nseq // TT):
        t0 = t * TT
        ts_ = slice(t0, t0 + TT)
        if in_bf16:
            for k in range(4):
                P.dma(xb[:, 2 * k:2 * k + 2, :], hsrc(t)[:, k, :, :], w=["xb%d" % (2 * k), "xb%d" % (2 * k + 1)])
        else:
            P.dma(xst[:, :, :], hsrc(t), w=["wst0"])
            for kc in range(8):
                P.copy("dve" if kc % 2 == 0 else "pool", xb[:, kc, :], xst[:, kc, :], r=["wst0"], w=["xb%d" % kc])
        rope_tables(P, pos, t0, TT, inv[:, 0:1], rb, sign_ap=inv[:, 1:2])
        cs, sn = rb["cs"], rb["sn"]
        for m in range(3):
            bk = m % 2
            for kc in range(8):
                P.mm(C.psb[bk][:, :], wcqb[:, kc, m * 128:(m + 1) * 128], xb[:, kc, :], kc == 0, kc == 7,
                     r=["wcqb", "xb%d" % kc], w=["psb%d" % bk])
            P.copy("act", cqf[:, m, :], C.psb[bk][:, :], r=["psb%d" % bk], w=["cqf%d" % m])
        for m in range(2):
            bk = (m + 1) % 2
            for kc in range(8):
                P.mm(C.psb[bk][:, :], wckvb[:, kc, m * 128:(m + 1) * 128], xb[:, kc, :], kc == 0, kc == 7,
                     r=["wckvb", "xb%d" % kc], w=["psb%d" % bk])
            P.copy("act", ckvf[:, m, :], C.psb[bk][:, :], r=["psb%d" % bk], w=["ckvf%d" % m])
        for j in range(2):
            for kc in range(8):
                P.mm(C.psb[j][0:64, :], wkpeb[:, kc, j * 64:(j + 1) * 64], xb[:, kc, :], kc == 0, kc == 7,
                     r=["wkpeb", "xb%d" % kc], w=["psb%d" % j])
        P.tt("dve", ra[:, :], C.psb[0][0:64, :], cs[:, :], ALU.mult, r=["psb0", "rt_cs"], w=["ra"])
        P.tt("dve", rbb[:, :], C.psb[1][0:64, :], sn[:, :], ALU.mult, r=["psb1", "rt_sn"], w=["rbb"])
        P.tt("pool", kpe[:, ts_], ra[:, :], rbb[:, :], ALU.add, r=["ra", "rbb"], w=["kpe"])
        rms_norm_tile(C, cqf, "cqf", qng, "qng", cqn, "cqn", 3, RMS_EPS, 0)
        rms_norm_tile(C, ckvf, "ckvf", kvng, "kvng", ckvn, "ckvn", 2, RMS_EPS, 1)
        for h in range(2):
            for kc in range(2):
                P.mm(C.psb[h][:, :], wukb[:, kc, h * 128:(h + 1) * 128], ckvn[:, kc, :], kc == 0, kc == 1,
                     r=["wukb", "ckvn%d" % kc], w=["psb%d" % h])
            P.copy("act", knT[:, h, ts_], C.psb[h][:, :], r=["psb%d" % h], w=["knT"])
        for i in range(4):
            bk = i % 2
            for kc in range(2):
                P.mm(C.psb[bk][:, 0:256], ckvn[:, kc, i * 128:(i + 1) * 128], wuvb[:, kc, :], kc == 0, kc == 1,
                     r=["wuvb", "ckvn%d" % kc], w=["psb%d" % bk])
            P.copy("act", vb[:, t * 4 + i, :], C.psb[bk][:, 0:256], r=["psb%d" % bk], w=["vb"])
        for h in range(2):
            for kc in range(3):
                P.mm(C.psb[0][:, :], wuqnb[:, kc, h * 128:(h + 1) * 128], cqn[:, kc, :], kc == 0, kc == 2,
                     r=["wuqnb", "cqn%d" % kc], w=["psb0"])
            P.act(qnb[:, h, :], C.psb[0][:, :], AF.Copy, r=["psb0"], w=["qnb%d" % h], scale=QSCALE)
            for j in range(2):
                for kc in range(3):
                    c0 = h * 128 + j * 64
                    P.mm(C.psb[1][0:64, :] if j == 0 else C.psb[0][0:64, :],
                         wuqrb[:, kc, c0:c0 + 64], cqn[:, kc, :], kc == 0, kc == 2,
                         r=["wuqrb", "cqn%d" % kc], w=["psb1" if j == 0 else "psb0"])
            P.tt("dve", ra[:, :], C.psb[1][0:64, :], cs[:, :], ALU.mult, r=["psb1", "rt_cs"], w=["ra"])
            P.tt("dve", rbb[:, :], C.psb[0][0:64, :], sn[:, :], ALU.mult, r=["psb0", "rt_sn"], w=["rbb"])
            P.stt(qpb[:, h, :], ra[:, :], 1.0, rbb[:, :], ALU.mult, ALU.add, r=["ra", "rbb"], w=["qpb%d" % h])
            P.ts("pool", qpb[:, h, :], qpb[:, h, :], QSCALE, None, ALU.mult, r=["qpb%d" % h], w=["qpb%d" % h])
        nkb = 4 * t + 4
        for h in range(2):
            po, kpo = C.psb[4 + 2 * h], "psb%d" % (4 + 2 * h)
            pl, kpl = C.psb[5 + 2 * h], "psb%d" % (5 + 2 * h)
            def scores(kb, slot):
                ks_ = slice(kb * 128, (kb + 1) * 128)
                bs_ = 2 + slot
                P.mm(C.psb[bs_][:, :], knT[:, h, ks_], qnb[:, h, :], True, False, r=["knT", "qnb%d" % h], w=["psb%d" % bs_])
                P.mm(C.psb[bs_][:, :], kpe[:, ks_], qpb[:, h, :], False, True, r=["kpe", "qpb%d" % h], w=["psb%d" % bs_])

            scores(0, pti % 2)
            for kb in range(nkb):
                slot = pti % 2
                bs_ = 2 + slot
                pT, kpt = pTs[slot], "pT%d" % slot
                pti += 1
                P.act(pT[:, :], C.psb[bs_][:, :], AF.Exp, r=["psb%d" % bs_], w=[kpt])
                if kb >= 4 * t:
                    P.tt("pool", pT[:, :], pT[:, :], maskb[:, kb - 4 * t, :], ALU.mult, r=[kpt, "maskb"], w=[kpt])
                if kb + 1 < nkb:
                    scores(kb + 1, pti % 2)
                P.mm(po[:, :], vb[:, kb, h * 128:(h + 1) * 128], pT[:, :], kb == 0, kb == nkb - 1, r=["vb", kpt], w=[kpo])
                P.mm(pl[:, :], onesb[:, :], pT[:, :], kb == 0, kb == nkb - 1, r=["onesb", kpt], w=[kpl])
            P.emit("dve", lambda e, pl=pl: e.reciprocal(rl[:, :], pl[:, :]), r=[kpl], w=["rl"])
            P.tt("dve", ob[:, h, :], po[:, :], rl[:, :], ALU.mult, r=[kpo, "rl"], w=["ob%d" % h])
        P.dma(ydst(t), ob[:, :, :], r=["ob0", "ob1"], is_out=True)
    P.finalize()


def mla_consts():
    inv32 = (10000.0 ** (-np.arange(32, dtype=np.float32) / np.float32(32))).astype(np.float32)
    inv = np.zeros((64, 2), np.float32)
    inv[:, 0] = np.concatenate([inv32, inv32])
    inv[:, 1] = np.concatenate([-np.ones(32), np.ones(32)])
    j = np.arange(128)[:, None, None]
    d = np.arange(4)[None, :, None]
    i = np.arange(512)[None, None, :]
    mask = (d * 128 + j <= i).astype(np.float32).reshape(128, 2048)
    return dict(inv=inv, mask=np.ascontiguousarray(mask))


GROUPS = [[0, 1, 2, 3], [4, 5, 6, 7]]
LRU_BLOCKS_OF = [[0, 1, 2], [3, 4, 5], [6, 7, 0], [8, 9, 0]]
LRU_VALID = [3, 3, 2, 2]
RET_W = (("wq", [D, 256]), ("wk", [D, 256]), ("wv", [D, 512]), ("wg", [D, 512]), ("gng", [128, 4]))
RET_C = (("inv", [128, 1]), ("maskT", [128, 2048]), ("qdec", [128, 512]), ("kdec", [128, 4]), ("cdec", [128, 1]))
LRU_W = (("wgt", [D, 384]), ("wrc", [D, 384]), ("wa", [128, NBLK, 128]), ("wx", [128, NBLK, 128]), ("vec", [128, NBLK, 8]))
MLA_W = (("wcq", [D, 384]), ("wckv", [D, 256]), ("wkpe", [D, 128]), ("wuqn", [384, 256]), ("wuqr", [384, 256]),
         ("wuk", [256, 256]), ("wuv", [256, 256]), ("qng", [128, 3]), ("kvng", [128, 2]), ("inv", [64, 2]),
         ("mask", [128, 4 * 512]))
POST_F = [2048, 1280, 1024, 2048]


def emit_allgather(nc, src, dst, rows):
    n = src.shape[0] // rows
    assert n * rows == src.shape[0] and dst.shape[0] == 4 * src.shape[0]
    with nc.semaphore("ccs%d" % _PHASE_ID[0]) as cc, nc.Block() as block:
        _PHASE_ID[0] += 1

        @block.gpsimd
        def _(g):
            g.sem_clear(cc)
            for k in range(n):
                g.collective_compute("AllGather", ALU.bypass, replica_groups=GROUPS,
                                     ins=[src[k * rows:(k + 1) * rows, :]],
                                     outs=[dst[k * 4 * rows:(k + 1) * 4 * rows, :]]).then_inc(cc, 1)
            g.wait_ge(cc, n)


def emit_copy_phase(nc, pairs):
    P = Prog(nc)
    for o, i in pairs:
        P.dma(o, i, is_out=True)
    P.finalize()


def build_fused(nlayers=DEPTH):
    nc = bass.Bass("TRN2", target_bir_lowering=False)
    ext = lambda n, shp, dt=F32: nc.dram_tensor(n, shp, dt, kind="ExternalInput")
    xT = ext("xT", [D, SEG]).ap()
    pos = ext("pos", [1, S], I32)
    Wm = []
    for i in range(DEPTH):
        kind = i % 3
        W = {}
        if kind == 0:
            for n, shp in RET_W:
                W[n] = ext("L%d_%s" % (i, n), shp).ap()
            for n, shp in RET_C:
                if i == 0:
                    W[n] = ext("retc_" + n, shp).ap()
                else:
                    W[n] = Wm[0][n]
        elif kind == 1:
            for n, shp in LRU_W:
                W[n] = ext("L%d_%s" % (i, n), shp).ap()
        else:
            for n, shp in MLA_W:
                W[n] = ext("L%d_%s" % (i, n), shp).ap()
        W["pos"] = pos
        Wm.append(W)
    Wp = []
    for i in range(DEPTH):
        W = {}
        for n, shp in (("wo", [len(_post_ychunks(i)[0]) * 128, D]), ("w1", [D, 4 * D]), ("w2", [4 * D, D]),
                       ("lng", [128, 2, 8]), ("lnb", [128, 2, 8])):
            W[n] = ext("P%d_%s" % (i, n), shp).ap()
        Wp.append(W)
    oT = nc.dram_tensor("oT", [D, SEG], F32, kind="ExternalOutput").ap()
    hbd = nc.dram_tensor("hbd", [D, SEG], BF16).ap()
    hfull = nc.dram_tensor("hfull", [4 * D, SEG], BF16).ap()
    FCORE = [512, 384, 256, 512]
    yd = [nc.dram_tensor("yd%d" % i, [4 * FCORE[i], SEG], BF16).ap() for i in range(DEPTH)]
    ya = [nc.dram_tensor("ya%d" % i, [16 * FCORE[i], SEG], BF16).ap() for i in range(DEPTH)]
    ym = [nc.dram_tensor("ym%d" % i, [4 * FCORE[i], SEG], BF16).ap() for i in range(DEPTH)]
    rspill = nc.dram_tensor("rspill", [D, SEG], F32).ap()
    hbv = hbd.rearrange("(c p) t -> p c t", p=128)
    ov = oT.rearrange("(c p) t -> p c t", p=128)
    xv = xT.rearrange("(c p) t -> p c t", p=128)
    rsv = rspill.rearrange("(c p) t -> p c t", p=128)

    hfv = hfull.rearrange("(k r j p) t -> r p k j t", k=4, r=4, j=2, p=128)

    def hsrc(t):
        r, cl = t // 4, (t % 4) * TT
        return hfv[r][:, :, :, cl:cl + TT]

    def run_token_phases(stack_layers):
        pass

    def alloc_resid():
        st = ExitStack()
        return st, st.enter_context(nc.sbuf_tensor("resid%d" % _PHASE_ID[0], [128, 8, SEG], F32))

    rstack, resid = alloc_resid()
    P = Prog(nc)
    xb0 = P.sb("xb0", [128, 8, TT], BF16)
    for t in range(SEG // TT):
        sl = slice(t * TT, (t + 1) * TT)
        P.dma(resid[:, :, sl], xv[:, :, sl], w=["r%d" % t])
        P.copy("dve", xb0[:, :, :], resid[:, :, sl], r=["r%d" % t], w=["xb0"])
        P.dma(hbv[:, :, sl], xb0[:, :, :], r=["xb0"], is_out=True)
    P.finalize()
    for i in range(nlayers):
        kind = i % 3
        emit_allgather(nc, hbd, hfull, 256)
        ydv = yd[i].rearrange("(s c p) t -> s p c t", s=4, p=128)
        ydst = lambda t, ydv=ydv: ydv[t // 4][:, :, (t % 4) * TT:(t % 4 + 1) * TT]
        if kind == 2:
            emit_copy_phase(nc, [(rsv[:, :, :], resid[:, :, :])])
            rstack.close()
        if kind == 0:
            emit_ret(nc, Wm[i], hsrc, ydst, S, True, True)
        elif kind == 1:
            emit_lru(nc, Wm[i], hsrc, ydst, S, True, True)
        else:
            emit_mla(nc, Wm[i], hsrc, ydst, S, True, True)
        if kind == 2:
            rstack, resid = alloc_resid()
            emit_copy_phase(nc, [(resid[:, :, :], rsv[:, :, :])])
        emit_allgather(nc, yd[i], ya[i], FCORE[i] // 2)
        emit_select_seg(nc, ya[i], ym[i])
        yav = ym[i].rearrange("(c p) t -> p c t", p=128)
        ychunks, FCY = _post_ychunks(i)
        last = i == nlayers - 1
        emit_post(nc, Wp[i], ychunks, FCY,
                  ysrc=lambda e, cache, t, c0, n, yav=yav: yav[:, c0:c0 + n, t * TT:(t + 1) * TT],
                  y_bf16=True, resid=resid, rsrc=None,
                  odst=(lambda t: ov[:, :, t * TT:(t + 1) * TT]) if last else None,
                  hdst=None if last else (lambda t: hbv[:, :, t * TT:(t + 1) * TT]), ntok=SEG)
    rstack.close()
    return nc


def emit_select_seg(nc, ya2d, ym2d):
    src = ya2d.rearrange("(k r f) t -> k r (f t)", k=8, r=4)
    dst = ym2d.rearrange("(r kk f) t -> kk r (f t)", r=4, kk=2)
    src2 = ya2d.rearrange("(k rf) t -> k (rf t)", k=8)
    with nc.semaphore("sel%d" % _PHASE_ID[0]) as sem, nc.Block() as block:
        _PHASE_ID[0] += 1

        @block.sync
        def _(e):
            e.sem_clear(sem)
            seg2 = e.snap((e.partition_id() % 4) * 2)
            for kk in range(2):
                e.dma_start(out=dst[kk], in_=src2[bass.ds(seg2 + kk, 1), :].rearrange("o (r n) -> (o r) n", r=4)
                            ).then_inc(sem, 16)
            e.wait_ge(sem, 32)


def _post_ychunks(i):
    kind = i % 3
    if kind == 0:
        return list(range(16)), 16
    if kind == 2:
        return list(range(8)), 8
    m = {}
    for g in range(4):
        for k in range(LRU_VALID[g]):
            m[LRU_BLOCKS_OF[g][k]] = g * 3 + k
    return [m[k] for k in range(10)], 12


_NC = {}


def _c(a):
    return np.ascontiguousarray(a, dtype=np.float32)


def _ret_inputs(w_in, gn_g, hd):
    return dict(wq=_c(w_in[:, hd * 256:(hd + 1) * 256]), wk=_c(w_in[:, 1024 + hd * 256:1024 + (hd + 1) * 256]),
                wv=_c(w_in[:, 2048 + hd * 512:2048 + (hd + 1) * 512]), wg=_c(w_in[:, 4096 + hd * 512:4096 + (hd + 1) * 512]),
                gng=_c(gn_g[hd * 512:(hd + 1) * 512].reshape(4, 128).T))


def _lru_inputs(w_in, conv_w, conv_b, w_a, b_a, w_x, b_x, lam, g):
    bl = LRU_BLOCKS_OF[g]
    cols = np.concatenate([np.arange(k * 128, (k + 1) * 128) for k in bl])
    vec = np.stack([conv_w[0][cols], conv_w[1][cols], conv_w[2][cols], conv_w[3][cols], conv_b[cols],
                    b_a[cols], b_x[cols], lam[cols]], -1)
    return dict(wgt=_c(w_in[:, cols]), wrc=_c(w_in[:, 1280 + cols]),
                wa=_c(w_a[bl].transpose(1, 0, 2)), wx=_c(w_x[bl].transpose(1, 0, 2)),
                vec=_c(vec.reshape(3, 128, 8).transpose(1, 0, 2)))


def _mla_inputs(w_in, qn, kvn, w_uq, w_ukv, hp):
    heads = [2 * hp, 2 * hp + 1]
    sw = np.concatenate([np.arange(672, 704), np.arange(640, 672)])
    d = dict(wcq=_c(w_in[:, 0:384]), wckv=_c(w_in[:, 384:640]),
             wkpe=_c(np.concatenate([w_in[:, 640:704], w_in[:, sw]], 1)),
             wuqn=_c(np.concatenate([w_uq[:, k * 192:k * 192 + 128] for k in heads], 1)),
             wuqr=_c(np.concatenate([np.concatenate([w_uq[:, k * 192 + 128:k * 192 + 192], w_uq[:, k * 192 + 160:k * 192 + 192],
                                                     w_uq[:, k * 192 + 128:k * 192 + 160]], 1) for k in heads], 1)),
             wuk=_c(np.concatenate([w_ukv[:, k * 256:k * 256 + 128] for k in heads], 1)),
             wuv=_c(np.concatenate([w_ukv[:, k * 256 + 128:k * 256 + 256] for k in heads], 1)),
             qng=_c(qn.reshape(3, 128).T), kvng=_c(kvn.reshape(2, 128).T))
    d.update(mla_consts())
    return d


def _ln_arr(v):
    return _c(v.reshape(2, 8, 128).transpose(2, 0, 1))


def kernel(x, positions, ret_w_in, ret_gn_g, ret_w_o,
           lru_w_in, lru_conv_w, lru_conv_b, lru_w_a, lru_b_a, lru_w_x, lru_b_x, lru_lam, lru_w_o,
           mla_w_in, mla_q_norm, mla_kv_norm, mla_w_uq, mla_w_ukv, mla_w_o,
           ln_g, ln_b, mlp_w1, mlp_w2):
    A = np.asarray
    x = A(x, dtype=np.float32)
    positions = A(positions)
    if "nc" not in _NC:
        _NC["nc"] = build_fused()
    nc = _NC["nc"]
    xf = x.reshape(B * S, D)
    wos = [A(ret_w_o)[0], A(lru_w_o)[0], A(mla_w_o)[0], A(ret_w_o)[1]]
    shared = {}
    for i in range(DEPTH):
        shared["P%d_wo" % i] = _c(wos[i])
        shared["P%d_w1" % i] = _c(A(mlp_w1)[i])
        shared["P%d_w2" % i] = _c(A(mlp_w2)[i])
        shared["P%d_lng" % i] = _ln_arr(A(ln_g)[i])
        shared["P%d_lnb" % i] = _ln_arr(A(ln_b)[i])
    ins = []
    for c in range(NCORE):
        b, g = c // 4, c % 4
        d = dict(shared)
        d["xT"] = _c(xf[c * SEG:(c + 1) * SEG].T)
        d["pos"] = np.ascontiguousarray(positions[b:b + 1], dtype=np.int32)
        for i in range(DEPTH):
            kind, j = i % 3, i // 3
            if kind == 0:
                w = _ret_inputs(A(ret_w_in)[j], A(ret_gn_g)[j], g)
                if i == 0:
                    for k, v in ret_consts(g).items():
                        d["retc_" + k] = v
            elif kind == 1:
                w = _lru_inputs(A(lru_w_in)[j], A(lru_conv_w)[j], A(lru_conv_b)[j], A(lru_w_a)[j], A(lru_b_a)[j],
                                A(lru_w_x)[j], A(lru_b_x)[j], A(lru_lam)[j], g)
            else:
                w = _mla_inputs(A(mla_w_in)[j], A(mla_q_norm)[j], A(mla_kv_norm)[j], A(mla_w_uq)[j], A(mla_w_ukv)[j], g)
            for k, v in w.items():
                d["L%d_%s" % (i, k)] = v
        ins.append(d)
    res = run_bass_kernel_spmd(nc, ins, core_ids=list(range(NCORE))).results
    out = np.empty((B * S, D), np.float32)
    for c in range(NCORE):
        out[c * SEG:(c + 1) * SEG] = res[c]["oT"].T
    return out.reshape(B, S, D)
```
